# Optimizing a Trainium2 kernel written in Bass

```python
import jax
import jax.numpy as jnp
from jax import lax
import numpy as np

D_MODEL = 1024
BATCH = 2
SEQ = 8192
DEPTH = 2

GRID_W = 64
CTX_LEN = 256
N_MOD = 9
D_FF = 2816
N_BRANCH = 4
BRANCH_WIDTH = D_MODEL // 2
GLA_HEADS = 4
GLA_DV = BRANCH_WIDTH // GLA_HEADS
GLA_DK = GLA_DV // 2
GLA_RANK = 16
GLA_NORMALIZER = 16.0
GLA_CHUNK = 64
SGU_GROUPS = 4
SGU_GC = BRANCH_WIDTH // SGU_GROUPS
SGU_CHUNK = 128
FNET_GROUPS = 4
FNET_GC = BRANCH_WIDTH // FNET_GROUPS
CONV_TAPS = 3
EPS = 1e-6
IN_SIZES = (GLA_HEADS * GLA_DK, GLA_HEADS * GLA_DK, BRANCH_WIDTH, BRANCH_WIDTH, 2 * GLA_RANK,
            BRANCH_WIDTH, BRANCH_WIDTH, BRANCH_WIDTH, BRANCH_WIDTH, BRANCH_WIDTH, BRANCH_WIDTH)
COL_Q = GLA_HEADS * GLA_DK
COL_K = 2 * GLA_HEADS * GLA_DK
COL_V = COL_K + BRANCH_WIDTH
COL_R = COL_V + BRANCH_WIDTH
COL_A = COL_R + 2 * GLA_RANK
IN_WIDTH = COL_A + 6 * BRANCH_WIDTH

kernel_name = 'hybrid_gla_sgu_fnet_conv_dit_block'


def rmsnorm(x, g):
    x32 = x.astype(jnp.float32)
    y = x32 * lax.rsqrt(jnp.mean(x32 * x32, axis=-1, keepdims=True) + EPS)
    return (y * g.astype(jnp.float32)).astype(x.dtype)


def layernorm(x):
    x32 = x.astype(jnp.float32)
    xc = x32 - jnp.mean(x32, axis=-1, keepdims=True)
    var = jnp.mean(xc * xc, axis=-1, keepdims=True)
    return (xc * lax.rsqrt(var + EPS)).astype(x.dtype)


def adaln(cvec, w, b):
    m = jax.nn.silu(cvec) @ w + b
    return jnp.transpose(m.reshape(cvec.shape[0], N_MOD, D_MODEL), (1, 0, 2))[:, :, None, :]


def modulate(h, shift, scale):
    return h * (1.0 + scale) + shift


def swiglu(h, w1, w3, w2):
    return (jax.nn.silu(h @ w1) * (h @ w3)) @ w2


def ffn_half_step(s, g, mods, j, w1, w3, w2):
    h = modulate(rmsnorm(s, g), mods[j], mods[j + 1])
    return s + 0.5 * mods[j + 2] * swiglu(h, w1, w3, w2)


def split_columns(p, sizes):
    return jnp.split(p, np.cumsum(sizes)[:-1].tolist(), axis=-1)


def gla_scan(q, k, v, g, s0):
    bsz, L, H, _ = k.shape
    dv = v.shape[-1]
    n = L // GLA_CHUNK

    def chunks(t):
        return jnp.transpose(t.reshape(bsz, n, GLA_CHUNK, H, t.shape[-1]), (1, 0, 3, 2, 4))

    mask = jnp.tril(jnp.ones((GLA_CHUNK, GLA_CHUNK), dtype=bool))[:, :, None]
    xs = (chunks(k), chunks(v), chunks(g)) if q is None else (chunks(k), chunks(v), chunks(g), chunks(q))

    def step(s, inp):
        kc, vc, gc = (t.astype(jnp.float32) for t in inp[:3])
        b = jnp.cumsum(gc, axis=2)
        b_end = b[:, :, -1:, :]
        s_new = (jnp.exp(b_end)[:, :, 0, :, None] * s
                 + jnp.einsum('bhjd,bhje->bhde', kc * jnp.exp(b_end - b), vc))
        if q is None:
            return s_new, None
        qc = inp[3].astype(jnp.float32)
        o = jnp.einsum('bhid,bhde->bhie', qc * jnp.exp(b), s)
        decay = jnp.exp(jnp.where(mask, b[:, :, :, None, :] - b[:, :, None, :, :], -jnp.inf))
        a = jnp.einsum('bhid,bhjd,bhijd->bhij', qc, kc, decay)
        return s_new, o + jnp.einsum('bhij,bhje->bhie', a, vc)

    s_fin, o = lax.scan(step, s0, xs)
    if q is None:
        return None, s_fin
    o = jnp.transpose(o, (1, 0, 3, 2, 4)).reshape(bsz, L, H, dv).astype(v.dtype)
    return o, s_fin


def gla_mixer(q, k, v, r, a, w_a2, b_a2, g_o, s_f0, s_b0):
    bsz, L = k.shape[:2]
    kh = k.reshape(bsz, L, GLA_HEADS, GLA_DK)
    vh = v.reshape(bsz, L, GLA_HEADS, GLA_DV)

    def log_decay(a_dir, w, bias):
        logits = (a_dir @ w + bias).astype(jnp.float32)
        return (jax.nn.log_sigmoid(logits) / GLA_NORMALIZER).reshape(bsz, L, GLA_HEADS, GLA_DK)

    def flip(t):
        return jnp.flip(t, axis=1)

    g_f = log_decay(a[..., :GLA_RANK], w_a2[0], b_a2[0])
    g_b = log_decay(a[..., GLA_RANK:], w_a2[1], b_a2[1])
    if q is None:
        q_f, q_b = None, None
    else:
        qh = q.reshape(bsz, L, GLA_HEADS, GLA_DK) * (GLA_DK ** -0.5)
        q_f, q_b = qh, flip(qh)
    o_f, s_f = gla_scan(q_f, kh, vh, g_f, s_f0)
    o_b, s_b = gla_scan(q_b, flip(kh), flip(vh), flip(g_b), s_b0)
    if q is None:
        return None, s_f, s_b
    o = rmsnorm(o_f + flip(o_b), g_o) * jax.nn.silu(r.reshape(bsz, L, GLA_HEADS, GLA_DV))
    return o.reshape(bsz, L, BRANCH_WIDTH), s_f, s_b


def spatial_gating(u, v, w_s, b_s):
    bsz, L, _ = u.shape
    shape = (bsz, L // SGU_CHUNK, SGU_CHUNK, SGU_GROUPS, SGU_GC)
    z = layernorm(v.reshape(shape))
    s = jnp.einsum('gij,bnjgc->bnigc', w_s, z) + jnp.transpose(b_s)[None, None, :, :, None]
    return (u.reshape(shape) * s).reshape(bsz, L, BRANCH_WIDTH)


def fourier_mix(f):
    bsz, L, _ = f.shape
    f32 = f.astype(jnp.float32).reshape(bsz, L, FNET_GROUPS, FNET_GC)
    y = jnp.fft.fftn(f32, axes=(1, 3), norm='ortho').real
    return y.reshape(bsz, L, BRANCH_WIDTH).astype(f.dtype)


def conv3(z, w, axis):
    pad = [(0, 0)] * z.ndim
    pad[axis] = (1, 1)
    zp = jnp.pad(z, pad)
    n = z.shape[axis]
    tap = lambda s: lax.slice_in_dim(zp, s, s + n, axis=axis)
    return tap(0) * w[0] + tap(1) * w[1] + tap(2) * w[2]


def short_conv(cb, cc, cx, w, rows):
    z = cc * cx
    if rows is None:
        y = conv3(z, w, axis=1)
    else:
        bsz, L, ch = z.shape
        y = conv3(z.reshape(bsz, rows, GRID_W, ch), w, axis=2).reshape(bsz, L, ch)
    return cb * y


def mixer_branches(p, o_gla, w_sgu, b_sgu, w_conv, rows):
    su, sv, fn, cb, cc, cx = p[5:]
    return (o_gla, spatial_gating(su, sv, w_sgu, b_sgu), fourier_mix(fn),
            short_conv(cb, cc, cx, w_conv, rows))


def merge(h, branches, w_branch, w_gate, b_gate, w_out):
    m = None
    for k in range(N_BRANCH):
        term = jax.nn.sigmoid(h @ w_gate[k] + b_gate[k]) * (branches[k] @ w_branch[k])
        m = term if m is None else m + term
    return m @ w_out


def setup_inputs(seed: int = 0) -> dict:
    key = jax.random.key(seed)
    ks = jax.random.split(key, 22)

    def nrm(k, shape, scale):
        return scale * jax.random.normal(k, shape, jnp.float32)

    return {
        'x': nrm(ks[0], (BATCH, SEQ, D_MODEL), 1.0),
        'c': nrm(ks[1], (BATCH, D_MODEL), 1.0),
        'ctx': nrm(ks[2], (BATCH, CTX_LEN, D_MODEL), 1.0),
        'c_ctx': nrm(ks[3], (D_MODEL,), 1.0),
        'w_ada': nrm(ks[4], (DEPTH, D_MODEL, N_MOD * D_MODEL), 0.5 * D_MODEL ** -0.5),
        'b_ada': nrm(ks[5], (DEPTH, N_MOD * D_MODEL), 0.02),
        'g_norm': 1.0 + nrm(ks[6], (DEPTH, 3, D_MODEL), 0.05),
        'w_ff1': nrm(ks[7], (DEPTH, 2, D_MODEL, D_FF), D_MODEL ** -0.5),
        'w_ff3': nrm(ks[8], (DEPTH, 2, D_MODEL, D_FF), D_MODEL ** -0.5),
        'w_ff2': nrm(ks[9], (DEPTH, 2, D_FF, D_MODEL), D_FF ** -0.5),
        'w_in': nrm(ks[10], (DEPTH, D_MODEL, IN_WIDTH), D_MODEL ** -0.5),
        'w_gla_a2': nrm(ks[11], (DEPTH, 2, GLA_RANK, GLA_HEADS * GLA_DK), GLA_RANK ** -0.5),
        'b_gla_a2': 2.0 + nrm(ks[12], (DEPTH, 2, GLA_HEADS * GLA_DK), 0.1),
        'g_gla_norm': 1.0 + nrm(ks[13], (DEPTH, GLA_HEADS, GLA_DV), 0.05),
        'w_sgu': nrm(ks[14], (DEPTH, SGU_GROUPS, SGU_CHUNK, SGU_CHUNK), 0.5 * SGU_CHUNK ** -0.5),
        'b_sgu': 1.0 + nrm(ks[15], (DEPTH, SGU_GROUPS, SGU_CHUNK), 0.1),
        'w_conv': nrm(ks[16], (DEPTH, CONV_TAPS, BRANCH_WIDTH), CONV_TAPS ** -0.5),
        'w_branch': nrm(ks[17], (DEPTH, N_BRANCH, BRANCH_WIDTH, D_MODEL), BRANCH_WIDTH ** -0.5),
        'w_gate': nrm(ks[18], (DEPTH, N_BRANCH, D_MODEL, D_MODEL), D_MODEL ** -0.5),
        'b_gate': nrm(ks[19], (DEPTH, N_BRANCH, D_MODEL), 0.02),
        'w_out': nrm(ks[20], (DEPTH, D_MODEL, D_MODEL), D_MODEL ** -0.5),
        'g_final': 1.0 + nrm(ks[21], (D_MODEL,), 0.05),
    }


def reference(x, c, ctx, c_ctx, w_ada, b_ada, g_norm, w_ff1, w_ff3, w_ff2, w_in, w_gla_a2,
              b_gla_a2, g_gla_norm, w_sgu, b_sgu, w_conv, w_branch, w_gate, b_gate, w_out, g_final):
    rows = x.shape[1] // GRID_W
    s_zero = jnp.zeros((ctx.shape[0], GLA_HEADS, GLA_DK, GLA_DV), jnp.float32)
    for i in range(DEPTH):
        last = i == DEPTH - 1
        m_lat = adaln(c, w_ada[i], b_ada[i])
        m_ctx = adaln(c_ctx[None, :], w_ada[i], b_ada[i])
        gla_w = (w_gla_a2[i], b_gla_a2[i], g_gla_norm[i])

        x = ffn_half_step(x, g_norm[i, 0], m_lat, 0, w_ff1[i, 0], w_ff3[i, 0], w_ff2[i, 0])
        ctx = ffn_half_step(ctx, g_norm[i, 0], m_ctx, 0, w_ff1[i, 0], w_ff3[i, 0], w_ff2[i, 0])

        h_c = modulate(rmsnorm(ctx, g_norm[i, 1]), m_ctx[3], m_ctx[4])
        if last:
            wi = w_in[i]
            _, s_f, s_b = gla_mixer(None, h_c @ wi[:, COL_Q:COL_K], h_c @ wi[:, COL_K:COL_V], None,
                                    h_c @ wi[:, COL_R:COL_A], *gla_w, s_zero, s_zero)
        else:
            p_c = split_columns(h_c @ w_in[i], IN_SIZES)
            o_c, s_f, s_b = gla_mixer(*p_c[:5], *gla_w, s_zero, s_zero)

        h = modulate(rmsnorm(x, g_norm[i, 1]), m_lat[3], m_lat[4])
        p_l = split_columns(h @ w_in[i], IN_SIZES)
        o_l, _, _ = gla_mixer(*p_l[:5], *gla_w, s_f, s_b)
        y = merge(h, mixer_branches(p_l, o_l, w_sgu[i], b_sgu[i], w_conv[i], rows),
                  w_branch[i], w_gate[i], b_gate[i], w_out[i])
        x = x + m_lat[5] * y

        x = ffn_half_step(x, g_norm[i, 2], m_lat, 6, w_ff1[i, 1], w_ff3[i, 1], w_ff2[i, 1])

        if not last:
            y_c = merge(h_c, mixer_branches(p_c, o_c, w_sgu[i], b_sgu[i], w_conv[i], None),
                        w_branch[i], w_gate[i], b_gate[i], w_out[i])
            ctx = ctx + m_ctx[5] * y_c
            ctx = ffn_half_step(ctx, g_norm[i, 2], m_ctx, 6, w_ff1[i, 1], w_ff3[i, 1], w_ff2[i, 1])
    return rmsnorm(x, g_final)
```

```python
import os
import numpy as np
import ml_dtypes
import concourse.bass as bass
import concourse.mybir as mybir
from concourse.bass_utils import run_bass_kernel_spmd

F32 = mybir.dt.float32
BF16 = mybir.dt.bfloat16
AF = mybir.ActivationFunctionType
ALU = mybir.AluOpType
AX = mybir.AxisListType

D = 1024
KC = 8
DEPTH = 2
SEQ = 8192
CTX = 256
TL = 2048
T = CTX + TL
DFF = 2816
NFC = 22
INW = 4640
EPS = 1e-6
TILES = [(0, 256), (256, 512), (768, 512), (1280, 512), (1792, 512)]
GROUPS4 = [[0, 1, 2, 3], [4, 5, 6, 7]]


class Op:
    __slots__ = ("eng", "fn", "kind", "stream", "deps", "count", "sig", "idx", "pe_acc", "semi", "epoch")


class Sched:
    ENGS = ("pe", "act", "dve", "pool", "sp")
    KDMA = 8

    def __init__(self, nc):
        self.nc = nc
        self.ops = {e: [] for e in self.ENGS}
        self.allops = []
        self.last_w = {}
        self.readers = {}
        self.nstreams = {}
        self.epoch = {e: 0 for e in self.ENGS}
        self.epoch_ops = {e: 0 for e in self.ENGS}

    def _add(self, eng, fn, reads, writes, kind="c", stream=None, pe_acc=False):
        op = Op()
        op.eng, op.fn, op.kind, op.stream, op.pe_acc = eng, fn, kind, stream, pe_acc
        op.deps = []
        op.sig = False
        op.count = None
        op.idx = len(self.allops)
        op.epoch = self.epoch[eng]
        self.epoch_ops[eng] += 1
        deps = {}
        for k in reads:
            w = self.last_w.get(k)
            if w is not None:
                deps[w.idx] = (w, "raw")
        for k in writes:
            w = self.last_w.get(k)
            if w is not None and w.idx not in deps:
                deps[w.idx] = (w, "waw")
            for r in self.readers.get(k, ()):
                if r.idx not in deps:
                    deps[r.idx] = (r, "war")
        for k in reads:
            lst = self.readers.setdefault(k, [])
            if kind == "c":
                lst[:] = [r for r in lst if not (r.kind == "c" and r.eng == eng)]
            lst.append(op)
        for k in writes:
            self.last_w[k] = op
            self.readers[k] = []
        for w, typ in deps.values():
            if w is op:
                continue
            if w.kind == "c" and op.kind == "c" and w.eng == eng:
                if eng == "pe":
                    continue
                if typ != "raw":
                    continue
            op.deps.append(w)
            w.sig = True
        self.ops[eng].append(op)
        self.allops.append(op)
        return op

    def c(self, eng, fn, reads=(), writes=(), pe_acc=False):
        return self._add(eng, fn, reads, writes, "c", pe_acc=pe_acc)

    def dma(self, eng, fn, reads=(), writes=(), stream="d"):
        return self._add(eng, fn, reads, writes, "d", stream=(eng, stream))

    def cc(self, fn, reads=(), writes=()):
        return self._add("pool", fn, reads, writes, "cc", stream=("pool", "cc"))

    def barrier(self):
        lasts = []
        for e in self.ENGS:
            for op in reversed(self.ops[e]):
                if op.kind == "c":
                    lasts.append(op)
                    break
        seen = {}
        for op in reversed(self.allops):
            if op.kind in ("d", "cc"):
                k = seen.get(op.stream, 0)
                if k < (self.KDMA if op.kind == "d" else 1):
                    seen[op.stream] = k + 1
                    lasts.append(op)
        for e in self.ENGS:
            op = self._add(e, None, [], [], "c")
            for w in lasts:
                if w.kind == "c" and w.eng == e:
                    continue
                op.deps.append(w)
                w.sig = True
        for e in self.ENGS:
            if self.epoch_ops[e] > 1500:
                self.epoch[e] += 1
                self.epoch_ops[e] = 0

    def emit(self, block, sems, dma_sems, cc_sem):
        nc = self.nc
        cnt = {}
        scnt = {}
        for e in self.ENGS:
            pending = []
            for op in self.ops[e]:
                if op.kind == "c":
                    if op.sig:
                        cnt[(e, op.epoch)] = cnt.get((e, op.epoch), 0) + 1
                        op.count = cnt[(e, op.epoch)]
                    else:
                        op.count = None
                elif op.kind == "d":
                    st = op.stream
                    n = scnt.get(st, 0)
                    scnt[st] = n + 1
                    op.semi = n % self.KDMA
                    op.count = 16 * (n // self.KDMA + 1)
                else:
                    st = op.stream
                    scnt[st] = scnt.get(st, 0) + 1
                    op.count = scnt[st]
        engobj = {"pe": "tensor", "act": "scalar", "dve": "vector", "pool": "gpsimd", "sp": "sync"}

        def semof(op):
            if op.kind == "c":
                return sems[op.eng][op.epoch]
            if op.kind == "d":
                return dma_sems[op.stream][op.semi]
            return cc_sem

        def body(e):
            def f(eng):
                waited = {}
                for op in self.ops[e]:
                    need = {}
                    for d in op.deps:
                        s = semof(d)
                        assert d.count is not None, "dependency on non-signalling op"
                        key = id(s)
                        if waited.get(key, 0) >= d.count:
                            continue
                        if key not in need or need[key][1] < d.count:
                            need[key] = (s, d.count)
                    for key, (s, v) in need.items():
                        eng.wait_ge(s, v)
                        waited[key] = v
                    if op.fn is None:
                        if op.sig:
                            ins = eng.nop()
                            ins.then_inc(sems[e][op.epoch], 1)
                        continue
                    ins = op.fn(eng)
                    if op.kind == "c":
                        if op.sig:
                            ins.then_inc(sems[e][op.epoch], 1)
                    elif op.kind == "d":
                        ins.then_inc(dma_sems[op.stream][op.semi], 16)
                    else:
                        ins.then_inc(cc_sem)
            return f

        block.tensor(body("pe"))
        block.scalar(body("act"))
        block.vector(body("dve"))
        block.gpsimd(body("pool"))
        block.sync(body("sp"))
        self.max_counts = cnt
        return scnt


def build_program(stop_after=None, debug=False, dbg_opts=()):
    nc = bass.Bass("TRN2", target_bir_lowering=False)
    S = Sched(nc)

    def din(name, shape, dt=F32):
        return nc.dram_tensor(name, list(shape), dt, kind="ExternalInput").ap()

    x_in = din("x_fm", [D, T])
    cvec = din("cvec", [16, 128])
    ident_in = din("ident", [128, 128])
    w_ada = din("w_ada", [DEPTH, D, 9 * D])
    b_ada = din("b_ada", [DEPTH, 72, 128])
    g_norm = din("g_norm", [DEPTH, 24, 128])
    w_ff1 = din("w_ff1", [DEPTH, 2, D, DFF])
    w_ff3 = din("w_ff3", [DEPTH, 2, D, DFF])
    w_ff2 = din("w_ff2", [DEPTH, 2, DFF, D])
    g_final = din("g_final", [8, 128])
    w_in = din("w_in", [DEPTH, D, INW])
    w_sguT = din("w_sguT", [DEPTH, 4, 128, 128])
    b_sgu = din("b_sgu", [DEPTH, 512])
    w_conv = din("w_conv", [DEPTH, 12, 128])
    w_branch = din("w_branch", [DEPTH, 4, 512, D])
    w_gate = din("w_gate", [DEPTH, 4, D, D])
    b_gate = din("b_gate", [DEPTH, 32, 128])
    w_out = din("w_out", [DEPTH, D, D])
    selm_in = din("selm", [4, 128, 128], BF16)
    flags_in = din("flags", [128, 8])
    pfm = nc.dram_tensor("pfm", [3616, T], BF16).ap()
    ptm = nc.dram_tensor("ptm", [T, 1024], BF16).ap()
    br = nc.dram_tensor("br", [4, 512, T], BF16).ap()
    fx_in = nc.dram_tensor("fx_in", [2, 256, TL], BF16).ap()
    fx_out = nc.dram_tensor("fx_out", [2, 1024, TL], BF16).ap()
    fy_in = nc.dram_tensor("fy_in", [2, 128, 4096], BF16).ap()
    fy_out = nc.dram_tensor("fy_out", [2, 512, 4096], BF16).ap()
    wa2_in = din("wa2", [DEPTH, 4, 32, 128])
    ba2_in = din("ba2", [DEPTH, 4, 128])
    ggo_in = din("ggo", [DEPTH, 4, 128])
    gmask_in = din("gmask", [128, 256], BF16)
    sca_in = din("sca", [128, 2])
    gx_d = nc.dram_tensor("gx_d", [4, 128, 256], F32).ap()
    gxo_d = nc.dram_tensor("gxo_d", [4, 512, 256], F32).ap()
    fence_in = nc.dram_tensor("fence_in", [128, 64], F32).ap()
    fence_out = nc.dram_tensor("fence_out", [512, 64], F32).ap()
    fc_in = din("fc_tab", [128, 256], BF16)
    fhi_in = din("fhi_tab", [64, 256], BF16)
    m_tab = din("m_tab", [64, 128, 256], BF16)
    f256_in = din("f256_tab", [2, 128, 512], BF16)
    out = nc.dram_tensor("out_fm", [D, TL], F32, kind="ExternalOutput").ap()
    dbg = nc.dram_tensor("dbg", [128, 4096], F32, kind="ExternalOutput").ap() if debug else None
    out_ctx = nc.dram_tensor("out_ctx", [D, CTX], F32, kind="ExternalOutput").ap() if debug else None

    def dump(c0, ap, n, keys, np_=128):
        if debug:
            S.dma("pool", lambda e: e.dma_start(out=dbg[0:np_, c0:c0 + n], in_=ap), keys, ["dbg%d" % c0], stream="d")

    import contextlib
    es = contextlib.ExitStack()

    def sb(name, shape, dt):
        return es.enter_context(nc.sbuf_tensor(name, list(shape), dt))

    def ps(name, shape, dt=F32):
        return es.enter_context(nc.psum_tensor(name, list(shape), dt))

    with es:
        xs = sb("xs", [128, KC, T], F32)
        arena = sb("arena", [128, 32768], F32)
        hall = arena[:, 0:9216].bitcast(BF16).rearrange("p (k t) -> p k t", k=KC)
        bbuf = arena[:, 9216:25600].bitcast(BF16)
        fbuf = arena[:, 25600:31744]
        xbuf = arena[:, 31744:32768]
        hflat = arena[:, 0:9216]
        ident = sb("ident_sb", [128, 128], F32)
        ones_bf = sb("ones_bf", [128, 128], BF16)
        epsc = sb("epsc", [128, 1], F32)
        mods = sb("mods", [128, DEPTH, 72, 2], F32)
        gsh = sb("gsh", [128, DEPTH, 3, 3, KC, 2], F32)
        gn = sb("gn", [128, DEPTH, 24], F32)
        gfin = sb("gfin", [128, 8], F32)
        csil = sb("csil", [128, 16], BF16)
        small = sb("small", [128, 256], F32)
        wcv = sb("wcv", [128, DEPTH, 12], F32)
        bgt = sb("bgt", [128, DEPTH, 32], F32)
        selm = sb("selm_sb", [128, 4, 128], BF16)
        flg = sb("flg", [128, 8], F32)
        nba = sb("nba", [128, DEPTH, 4], F32)
        ggo = sb("ggo_sb", [128, DEPTH, 4], F32)
        sca = sb("sca_sb", [128, 2], F32)
        onec = sb("onec", [128, 1], F32)
        ident_bf = sb("ident_bf", [128, 128], BF16)
        gmask = sb("gmask_sb", [128, 256], BF16)
        banks = [ps("bank%d" % i, [128, 512]) for i in range(8)]

        dma_streams = [("sp", "w"), ("sp", "d"), ("pool", "d"), ("pool", "w"), ("act", "d")]
        dma_sems = {st: [es.enter_context(nc.semaphore("dsem_%s_%s_%d" % (st[0], st[1], i_))) for i_ in range(Sched.KDMA)]
                    for st in dma_streams}
        cc_sem = es.enter_context(nc.semaphore("cc_sem"))

        S.dma("sp", lambda e: e.dma_start(out=ident[:], in_=ident_in[:, :]), [], ["ident"])
        S.c("dve", lambda e: e.memset(ones_bf[:], 1.0), [], ["ones"])
        S.c("dve", lambda e: e.memset(epsc[:], EPS), [], ["epsc"])
        S.dma("sp", lambda e: e.dma_start(
            out=xs[:], in_=x_in.rearrange("(kc p) t -> p kc t", p=128)), [], [("xs", t0_, k_) for (t0_, _w) in TILES for k_ in range(KC)])

        def load_cols(src_ap, rows, dst_ap, key):
            stg = fbuf[0:rows, 0:128]
            S.dma("sp", lambda e: e.dma_start(out=stg, in_=src_ap), ["stg_free"], ["stg"])
            S.c("pe", lambda e: e.transpose(out=banks[7][:, 0:rows], in_=stg, identity=ident[0:rows, 0:rows]),
                ["stg", "ident"], ["bank7"])
            S.c("dve", lambda e: e.tensor_copy(out=dst_ap, in_=banks[7][:, 0:rows]), ["bank7"], [key, "stg_free"])

        S.dma("sp", lambda e: e.dma_start(out=selm[:], in_=selm_in.rearrange("g p m -> p g m")), [], ["selm"])
        S.dma("sp", lambda e: e.dma_start(out=flg[:], in_=flags_in[:, :]), [], ["flg"])
        S.dma("sp", lambda e: e.dma_start(out=sca[:], in_=sca_in[:, :]), [], ["sca"])
        S.dma("sp", lambda e: e.dma_start(out=gmask[:], in_=gmask_in[:, :]), [], ["gmask"])
        S.c("dve", lambda e: e.memset(onec[:], 1.0), [], ["onec"])
        S.c("dve", lambda e: e.tensor_copy(out=ident_bf[:], in_=ident[:]), ["ident"], ["ident_bf"])
        load_cols(cvec[:, :], 16, small[:, 0:16], "cv")
        S.c("act", lambda e: e.activation(out=csil[:].rearrange("p (k j) -> p j k", j=2), in_=small[:, 0:16].rearrange("p (j k) -> p j k", j=2), func=AF.Silu), ["cv"], ["csil"])
        load_cols(g_final[:, :], 8, gfin[:], "gfin")
        dump(3072, small[:, 0:16], 16, ["cv"])
        dump(3088, csil[:, 0:16], 16, ["csil"])
        dump(3104, gfin[:, 0:8], 8, ["gfin"])
        def setup_layer(l):
            load_cols(g_norm[l], 24, gn[:, l, :], "gn%d" % l)
            load_cols(w_conv[l], 12, wcv[:, l, :], "wcv%d" % l)
            load_cols(ba2_in[l], 4, nba[:, l, :], "nba%d" % l)
            S.c("dve", lambda e: e.tensor_scalar(out=nba[:, l, :], in0=nba[:, l, :], scalar1=-1.0, scalar2=None, op0=ALU.mult),
                ["nba%d" % l], ["nba%d" % l])
            load_cols(ggo_in[l], 4, ggo[:, l, :], "ggo%d" % l)
            load_cols(b_gate[l], 32, bgt[:, l, :], "bgt%d" % l)
            load_cols(b_ada[l], 72, small[:, 16:88], "bada")
            NB = 1152
            for blk in range(8):
                wv = bbuf[:, 0:KC * NB].rearrange("p (k n) -> p k n", k=KC)
                for kc in range(KC):
                    S.dma("pool", lambda e, blk=blk, wv=wv, kc=kc: e.dma_start(
                        out=wv[:, kc, :], in_=w_ada[l, kc * 128:(kc + 1) * 128, blk * NB:(blk + 1) * NB]),
                        [], ["wada%d" % kc], stream="w")
                for nch in range(9):
                    ncol = blk * 9 + nch
                    for kc in range(KC):
                        S.c("pe", lambda e, nch=nch, kc=kc, ncol=ncol, wv=wv: e.matmul(
                            banks[6][:, 2 * ncol:2 * ncol + 2], lhsT=wv[:, kc, nch * 128:(nch + 1) * 128],
                            rhs=csil[:, 2 * kc:2 * kc + 2], start=(kc == 0), stop=(kc == KC - 1)),
                            ["wada%d" % kc, "csil"], ["bank6"], pe_acc=True)
            if l == 0:
                dump(3500, wv[:, 0, 0:512], 512, ["wada0"])
                S.c("dve", lambda e: e.tensor_copy(out=fbuf[:, 4096:4240], in_=banks[6][:, 0:144]), ["bank6"], ["b6copy"])
                dump(3200, fbuf[:, 4096:4240], 144, ["b6copy"])
                dump(3400, small[:, 16:88], 72, ["bada"])
            for j in range(2):
                S.c("dve", lambda e, l=l, j=j: e.tensor_tensor(
                    out=mods[:, l, :, j], in0=banks[6][:, j:144:2], in1=small[:, 16:88], op=ALU.add),
                    ["bank6", "bada"], ["mods%d" % l])
            for sub in range(3):
                for j in range(2):
                    sc = mods[:, l, (3 * sub + 1) * 8:(3 * sub + 2) * 8, j]
                    S.c("dve", lambda e, l=l, sub=sub, j=j, sc=sc: e.scalar_tensor_tensor(
                        out=gsh[:, l, sub, 0, :, j], in0=sc, scalar=1.0, in1=gn[:, l, sub * 8:(sub + 1) * 8],
                        op0=ALU.add, op1=ALU.mult), ["mods%d" % l, "gn%d" % l], ["gsh%d" % l])
                    S.c("dve", lambda e, l=l, sub=sub, j=j: e.tensor_copy(
                        out=gsh[:, l, sub, 1, :, j], in_=mods[:, l, (3 * sub) * 8:(3 * sub + 1) * 8, j]),
                        ["mods%d" % l], ["gsh%d" % l])
                    S.c("dve", lambda e, l=l, sub=sub, j=j: e.tensor_scalar(
                        out=gsh[:, l, sub, 2, :, j], in0=mods[:, l, (3 * sub + 2) * 8:(3 * sub + 3) * 8, j],
                        scalar1=(1.0 if sub == 1 else 0.5), scalar2=None, op0=ALU.mult),
                        ["mods%d" % l], ["gsh%d" % l])
        for l_ in range(DEPTH):
            setup_layer(l_)
        S.barrier()

        def norm_to_hall(l, sub, tiles, tag):
            for ti, (t0, w) in enumerate(tiles):
                j = 1 if t0 < CTX else 0
                sq = bbuf[:, 24576:24576 + KC * 512].rearrange("p (k t) -> p k t", k=KC)[:, :, 0:w]
                S.c("act", lambda e, t0=t0, w=w, sq=sq: e.activation(out=sq, in_=xs[:, :, t0:t0 + w], func=AF.Square),
                    [("xs", t0, k_) for k_ in range(KC)], ["sq"])
                for kc in range(KC):
                    S.c("pe", lambda e, kc=kc, w=w, sq=sq: e.matmul(
                        banks[7][:, 0:w], lhsT=ones_bf[:], rhs=sq[:, kc, :], start=(kc == 0), stop=(kc == KC - 1)),
                        ["sq", "ones"], ["bank7"], pe_acc=True)
                rs = fbuf[:, 0:w]
                S.c("act", lambda e, w=w, rs=rs: e.activation(out=rs, in_=banks[7][:, 0:w], func=AF.Sqrt,
                                                             scale=1.0 / D, bias=epsc[:]), ["bank7", "epsc"], ["rs"])
                S.c("dve", lambda e, rs=rs: e.reciprocal(out=rs, in_=rs), ["rs"], ["rs"])
                for kc in range(KC):
                    tmp = fbuf[:, 512 + (kc % 2) * 512:512 + (kc % 2) * 512 + w]
                    S.c("dve", lambda e, kc=kc, t0=t0, w=w, tmp=tmp, rs=rs: e.tensor_tensor(
                        out=tmp, in0=xs[:, kc, t0:t0 + w], in1=rs, op=ALU.mult), [("xs", t0, kc), "rs"], ["ntmp%d" % (kc % 2)])
                    S.c("act", lambda e, kc=kc, t0=t0, w=w, tmp=tmp, j=j: e.activation(
                        out=hall[:, kc, t0:t0 + w], in_=tmp, func=AF.Identity,
                        scale=gsh[:, l, sub, 0, kc, j:j + 1], bias=gsh[:, l, sub, 1, kc, j:j + 1]),
                        ["ntmp%d" % (kc % 2), "gsh%d" % l], ["hall%d" % ti])

        def ffn(l, i, tiles):
            sub = 0 if i == 0 else 2
            norm_to_hall(l, sub, tiles, "ffn")
            if False:
                dump(0, hall[:, 0, 256:768], 512, ["hall1"])
                dump(1024, gsh[:, 0, 2, :, :, :], 48, ["gsh0"])
                S.c("dve", lambda e: e.tensor_copy(out=fbuf[:, 4096:4608], in_=xs[:, 0, 256:768]), [("xs", 256, 0)], ["xdump"])
                dump(3072, fbuf[:, 4096:4608], 512, ["xdump"])
            fgroups = [(0, 4), (4, 4), (8, 4), (12, 4), (16, 4), (20, 2)]
            W1o, W3o, W2o = 0, 4096, 8192
            for gi, (f0, nf) in enumerate(fgroups):
                par = gi % 2
                base = par * 12288
                w1 = bbuf[:, base + W1o:base + W1o + KC * 512].rearrange("p (k f) -> p k f", k=KC)
                w3 = bbuf[:, base + W3o:base + W3o + KC * 512].rearrange("p (k f) -> p k f", k=KC)
                w2 = bbuf[:, base + W2o:base + W2o + 4 * 1024].rearrange("p (f d) -> p f d", f=4)
                wk = "ffw%d" % par
                nfc = nf * 128
                S.dma("pool", lambda e, w1=w1, f0=f0, nfc=nfc: e.dma_start(
                    out=w1[:, :, 0:nfc], in_=w_ff1[l, i].rearrange("(kc p) f -> p kc f", p=128)[:, :, f0 * 128:f0 * 128 + nfc]),
                    [], [wk + "a"], stream="w")
                S.dma("pool", lambda e, w3=w3, f0=f0, nfc=nfc: e.dma_start(
                    out=w3[:, :, 0:nfc], in_=w_ff3[l, i].rearrange("(kc p) f -> p kc f", p=128)[:, :, f0 * 128:f0 * 128 + nfc]),
                    [], [wk + "b"], stream="w")
                S.dma("pool", lambda e, w2=w2, f0=f0, nf=nf: e.dma_start(
                    out=w2[:, 0:nf, :], in_=w_ff2[l, i][f0 * 128:(f0 + nf) * 128, :].rearrange("(fc p) d -> p fc d", p=128)),
                    [], [wk + "c"], stream="w")
                for ti, (t0, w) in enumerate(tiles):
                    j = 1 if t0 < CTX else 0
                    gk = "gt%d" % (ti % 2)
                    gt = bbuf[:, 28672 + (ti % 2) * 2048:28672 + (ti % 2) * 2048 + 2048].rearrange(
                        "p (f t) -> p f t", f=4)
                    for fc in range(nf):
                        pp = (fc % 2)
                        b1, b3 = banks[pp * 2], banks[pp * 2 + 1]
                        for kc in range(KC):
                            S.c("pe", lambda e, kc=kc, fc=fc, b1=b1, w1=w1, t0=t0, w=w: e.matmul(
                                b1[:, 0:w], lhsT=w1[:, kc, fc * 128:(fc + 1) * 128], rhs=hall[:, kc, t0:t0 + w],
                                start=(kc == 0), stop=(kc == KC - 1)), [wk + "a", "hall%d" % ti], ["bank%d" % (pp * 2)], pe_acc=True)
                        for kc in range(KC):
                            S.c("pe", lambda e, kc=kc, fc=fc, b3=b3, w3=w3, t0=t0, w=w: e.matmul(
                                b3[:, 0:w], lhsT=w3[:, kc, fc * 128:(fc + 1) * 128], rhs=hall[:, kc, t0:t0 + w],
                                start=(kc == 0), stop=(kc == KC - 1)), [wk + "b", "hall%d" % ti], ["bank%d" % (pp * 2 + 1)], pe_acc=True)
                        st = fbuf[:, 2048 + pp * 512:2048 + pp * 512 + w]
                        S.c("act", lambda e, b1=b1, st=st, w=w: e.activation(out=st, in_=b1[:, 0:w], func=AF.Silu),
                            ["bank%d" % (pp * 2)], ["silu%d" % pp])
                        S.c("dve", lambda e, b3=b3, st=st, w=w, gt=gt, fc=fc: e.tensor_tensor(
                            out=gt[:, fc, 0:w], in0=b3[:, 0:w], in1=st, op=ALU.mult),
                            ["bank%d" % (pp * 2 + 1), "silu%d" % pp], [gk])
                    if False:
                        dump(512, w1[:, 0, :], 512, [wk + "a"])
                        dump(1536, w2[:, 0, 0:512], 512, [wk + "c"])
                        dump(2048, gt[:, 0, :], 512, [gk])
                        dump(2560, w3[:, 7, :], 512, [wk + "b"])
                    for dc in range(KC):
                        pb = banks[4 + dc % 2]
                        for fc in range(nf):
                            S.c("pe", lambda e, dc=dc, fc=fc, pb=pb, w2=w2, gt=gt, w=w: e.matmul(
                                pb[:, 0:w], lhsT=w2[:, fc, dc * 128:(dc + 1) * 128], rhs=gt[:, fc, 0:w],
                                start=(fc == 0), stop=(fc == nf - 1)), [wk + "c", gk], ["bank%d" % (4 + dc % 2)], pe_acc=True)
                        S.c("dve", lambda e, dc=dc, pb=pb, t0=t0, w=w, j=j: e.scalar_tensor_tensor(
                            out=xs[:, dc, t0:t0 + w], in0=pb[:, 0:w], scalar=gsh[:, l, sub, 2, dc, j:j + 1],
                            in1=xs[:, dc, t0:t0 + w], op0=ALU.mult, op1=ALU.add),
                            ["bank%d" % (4 + dc % 2), ("xs", t0, dc), "gsh%d" % l], [("xs", t0, dc)])
            S.barrier()


        PF = {"q": 0, "k": 256, "r": 512, "a": 1024, "su": 1056, "fn": 1568, "cb": 2080, "cc": 2592, "cx": 3104}

        def evac(idx, out_ap, in_ap, reads, writes):
            if idx % 2 == 0:
                S.c("act", lambda e: e.activation(out=out_ap, in_=in_ap, func=AF.Copy), reads, writes)
            else:
                S.c("dve", lambda e: e.tensor_copy(out=out_ap, in_=in_ap), reads, writes)

        def win_proj(l, tiles):
            blocks = [(0, 512, "fm", 0), (512, 512, "tm", 0), (1024, 512, "fm", 512), (1536, 32, "fm", 1024),
                      (1568, 512, "fm", 1056), (2080, 512, "tm", 512), (2592, 512, "fm", 1568),
                      (3104, 512, "fm", 2080), (3616, 512, "fm", 2592), (4128, 512, "fm", 3104)]
            cnt = [0]
            for bi, (c0, ncol, kind, dst) in enumerate(blocks):
                par = bi % 2
                wv = bbuf[:, par * 12288: par * 12288 + 4096].rearrange("p (k n) -> p k n", k=KC)
                wk = "winw%d" % par
                S.dma("pool", lambda e, wv=wv, c0=c0, ncol=ncol: e.dma_start(
                    out=wv[:, :, 0:ncol], in_=w_in[l].rearrange("(kc p) n -> p kc n", p=128)[:, :, c0:c0 + ncol]),
                    [], [wk], stream="w")
                for ti, (t0, w) in enumerate(tiles):
                    if kind == "fm":
                        for ch in range((ncol + 127) // 128):
                            m = min(128, ncol - ch * 128)
                            i4 = cnt[0] % 4
                            cnt[0] += 1
                            bk = banks[i4]
                            for kc in range(KC):
                                S.c("pe", lambda e, bk=bk, m=m, w=w, wv=wv, kc=kc, ch=ch, t0=t0: e.matmul(
                                    bk[0:m, 0:w], lhsT=wv[:, kc, ch * 128:ch * 128 + m], rhs=hall[:, kc, t0:t0 + w],
                                    start=(kc == 0), stop=(kc == KC - 1)), [wk, "hall%d" % ti], ["bank%d" % i4], pe_acc=True)
                            stg = bbuf[:, 24576 + i4 * 512:24576 + i4 * 512 + w]
                            evac(cnt[0], stg[0:m, :], bk[0:m, 0:w], ["bank%d" % i4], ["wstg%d" % i4])
                            if c0 == 2592 and t0 >= CTX:
                                dst_ap = fx_in[ch // 2, (ch % 2) * 128:(ch % 2) * 128 + m, t0 - CTX:t0 - CTX + w]
                                dkey = ("fx_in", ch // 2)
                            else:
                                dst_ap = pfm[dst + ch * 128:dst + ch * 128 + m, t0:t0 + w]
                                dkey = ("pfm", dst + ch * 128, t0)
                            S.dma("sp", lambda e, dst_ap=dst_ap, stg=stg, m=m: e.dma_start(out=dst_ap, in_=stg[0:m, :]),
                                  ["wstg%d" % i4], [dkey])
                    else:
                        for sidx in range(w // 128):
                            i4 = cnt[0] % 4
                            cnt[0] += 1
                            bk = banks[i4]
                            ts = t0 + sidx * 128
                            for kc in range(KC):
                                S.c("pe", lambda e, bk=bk, wv=wv, kc=kc, ts=ts: e.matmul(
                                    bk[:, 0:512], lhsT=hall[:, kc, ts:ts + 128], rhs=wv[:, kc, 0:512],
                                    start=(kc == 0), stop=(kc == KC - 1)), [wk, "hall%d" % ti], ["bank%d" % i4], pe_acc=True)
                            stg = bbuf[:, 24576 + i4 * 512:24576 + i4 * 512 + 512]
                            evac(cnt[0], stg, bk[:, 0:512], ["bank%d" % i4], ["wstg%d" % i4])
                            S.dma("sp", lambda e, stg=stg, ts=ts, dst=dst: e.dma_start(
                                out=ptm[ts:ts + 128, dst:dst + 512], in_=stg), ["wstg%d" % i4], [("ptm", dst, ts)])
            S.barrier()

        def conv_branch(l, tiles):
            for ti, (t0, w) in enumerate(tiles):
                rl = 256 if t0 < CTX else 64
                for ch in range(4):
                    it = ti * 4 + ch
                    p2 = it % 2
                    inb = bbuf[:, p2 * 2048:p2 * 2048 + 3 * 512].rearrange("p (s t) -> p s t", s=3)
                    S.dma("sp", lambda e, inb=inb, ch=ch, t0=t0, w=w: e.dma_start(
                        out=inb[:, :, 0:w],
                        in_=pfm[2080:3616, :].rearrange("(s g p) t -> p s g t", s=3, g=4)[:, :, ch, t0:t0 + w]),
                        [("pfm", 2080 + s_ * 512 + ch * 128, t0) for s_ in range(3)], ["cvin%d" % p2])
                    z = fbuf[:, p2 * 1024:p2 * 1024 + w]
                    y = fbuf[:, p2 * 1024 + 512:p2 * 1024 + 512 + w]
                    z3 = z.rearrange("p (r c) -> p r c", c=rl)
                    y3 = y.rearrange("p (r c) -> p r c", c=rl)
                    S.c("dve", lambda e, z=z, inb=inb, w=w: e.tensor_tensor(out=z, in0=inb[:, 1, 0:w], in1=inb[:, 2, 0:w], op=ALU.mult),
                        ["cvin%d" % p2], ["cvz%d" % p2])
                    S.c("act", lambda e, y=y, z=z, ch=ch: e.activation(out=y, in_=z, func=AF.Identity, scale=wcv[:, l, 4 + ch:5 + ch]),
                        ["cvz%d" % p2, "wcv%d" % l], ["cvy%d" % p2])
                    S.c("dve", lambda e, y3=y3, z3=z3, ch=ch, rl=rl: e.scalar_tensor_tensor(
                        out=y3[:, :, 1:rl], in0=z3[:, :, 0:rl - 1], scalar=wcv[:, l, ch:ch + 1], in1=y3[:, :, 1:rl],
                        op0=ALU.mult, op1=ALU.add), ["cvz%d" % p2, "cvy%d" % p2, "wcv%d" % l], ["cvy%d" % p2])
                    S.c("dve", lambda e, y3=y3, z3=z3, ch=ch, rl=rl: e.scalar_tensor_tensor(
                        out=y3[:, :, 0:rl - 1], in0=z3[:, :, 1:rl], scalar=wcv[:, l, 8 + ch:9 + ch], in1=y3[:, :, 0:rl - 1],
                        op0=ALU.mult, op1=ALU.add), ["cvz%d" % p2, "cvy%d" % p2, "wcv%d" % l], ["cvy%d" % p2])
                    ob = bbuf[:, 4096 + p2 * 512:4096 + p2 * 512 + w]
                    S.c("dve", lambda e, ob=ob, inb=inb, y=y, w=w: e.tensor_tensor(out=ob, in0=inb[:, 0, 0:w], in1=y, op=ALU.mult),
                        ["cvin%d" % p2, "cvy%d" % p2], ["cvo%d" % p2])
                    S.dma("sp", lambda e, ob=ob, ch=ch, t0=t0, w=w: e.dma_start(
                        out=br[3, ch * 128:(ch + 1) * 128, t0:t0 + w], in_=ob), ["cvo%d" % p2], [("br", 3, ch, t0)])
            S.barrier()

        def sgu_branch(l, tiles):
            wst = bbuf[:, 8192:8192 + 512].rearrange("p (g i) -> p g i", g=4)
            S.dma("pool", lambda e: e.dma_start(out=wst, in_=w_sguT[l].rearrange("g j i -> j g i")), [], ["wst"], stream="w")
            bbc = fbuf[:, 4096:4608]
            S.dma("sp", lambda e: e.dma_start(out=bbc, in_=b_sgu[l].partition_broadcast(128)), [], ["bbc"])
            ci = 0
            for ti, (t0, w) in enumerate(tiles):
                for sidx in range(w // 128):
                    ts = t0 + sidx * 128
                    p2 = ci % 2
                    ci += 1
                    svt = bbuf[:, 9216 + p2 * 512:9216 + p2 * 512 + 512]
                    sut = bbuf[:, 10240 + p2 * 512:10240 + p2 * 512 + 512].rearrange("p (g t) -> p g t", g=4)
                    S.dma("sp", lambda e, svt=svt, ts=ts: e.dma_start(out=svt, in_=ptm[ts:ts + 128, 512:1024]),
                          [("ptm", 512, ts)], ["svt%d" % p2])
                    S.dma("sp", lambda e, sut=sut, ts=ts: e.dma_start(
                        out=sut, in_=pfm[1056:1568, ts:ts + 128].rearrange("(g c) t -> c g t", c=128)),
                        [("pfm", 1056 + g_ * 128, t0) for g_ in range(4)], ["sut%d" % p2])
                    st = xbuf[:, p2 * 32:p2 * 32 + 32]
                    sq = fbuf[:, p2 * 512:p2 * 512 + 512]
                    sv3 = svt.rearrange("p (g c) -> p g c", g=4)
                    S.c("dve", lambda e, st=st, sv3=sv3: e.tensor_reduce(out=st[:, 0:4], in_=sv3, axis=AX.X, op=ALU.add),
                        ["svt%d" % p2], ["sgst%d" % p2])
                    S.c("act", lambda e, sq=sq, svt=svt: e.activation(out=sq, in_=svt, func=AF.Square), ["svt%d" % p2], ["sgsq%d" % p2])
                    S.c("dve", lambda e, st=st, sq=sq: e.tensor_reduce(out=st[:, 4:8], in_=sq.rearrange("p (g c) -> p g c", g=4),
                                                                     axis=AX.X, op=ALU.add), ["sgsq%d" % p2, "sgst%d" % p2], ["sgst%d" % p2])
                    S.c("dve", lambda e, st=st: e.tensor_scalar(out=st[:, 8:12], in0=st[:, 0:4], scalar1=1.0 / 128, scalar2=None, op0=ALU.mult),
                        ["sgst%d" % p2], ["sgst%d" % p2])
                    S.c("dve", lambda e, st=st: e.tensor_tensor(out=st[:, 16:20], in0=st[:, 8:12], in1=st[:, 8:12], op=ALU.mult),
                        ["sgst%d" % p2], ["sgst%d" % p2])
                    S.c("dve", lambda e, st=st: e.scalar_tensor_tensor(out=st[:, 12:16], in0=st[:, 4:8], scalar=1.0 / 128, in1=st[:, 16:20],
                                                                       op0=ALU.mult, op1=ALU.subtract), ["sgst%d" % p2], ["sgst%d" % p2])
                    S.c("act", lambda e, st=st: e.activation(out=st[:, 12:16], in_=st[:, 12:16], func=AF.Sqrt, scale=1.0, bias=epsc[:]),
                        ["sgst%d" % p2, "epsc"], ["sgst%d" % p2])
                    S.c("dve", lambda e, st=st: e.reciprocal(out=st[:, 12:16], in_=st[:, 12:16]), ["sgst%d" % p2], ["sgst%d" % p2])
                    zt = bbuf[:, 11264 + p2 * 512:11264 + p2 * 512 + 512]
                    for g in range(4):
                        S.c("dve", lambda e, zt=zt, svt=svt, st=st, g=g: e.tensor_scalar(
                            out=zt[:, g * 128:(g + 1) * 128], in0=svt[:, g * 128:(g + 1) * 128], scalar1=st[:, 8 + g:9 + g],
                            scalar2=st[:, 12 + g:13 + g], op0=ALU.subtract, op1=ALU.mult), ["svt%d" % p2, "sgst%d" % p2], ["sgz%d" % p2])
                    bk = banks[p2]
                    for g in range(4):
                        S.c("pe", lambda e, bk=bk, zt=zt, g=g: e.matmul(
                            bk[:, g * 128:(g + 1) * 128], lhsT=zt[:, g * 128:(g + 1) * 128], rhs=wst[:, g, :], start=True, stop=True),
                            ["sgz%d" % p2, "wst"], ["bank%d" % p2])
                    tmp = fbuf[:, 1024 + p2 * 512:1024 + p2 * 512 + 512]
                    S.c("dve", lambda e, tmp=tmp, bk=bk: e.tensor_tensor(out=tmp, in0=bk[:, 0:512], in1=bbc, op=ALU.add),
                        ["bank%d" % p2, "bbc"], ["sgt%d" % p2])
                    ob = bbuf[:, 12288 + p2 * 512:12288 + p2 * 512 + 512].rearrange("p (g t) -> p g t", g=4)
                    S.c("pool", lambda e, ob=ob, tmp=tmp, sut=sut: e.tensor_tensor(
                        out=ob, in0=tmp.rearrange("p (g t) -> p g t", g=4), in1=sut, op=ALU.mult), ["sgt%d" % p2, "sut%d" % p2], ["sgo%d" % p2])
                    S.dma("sp", lambda e, ob=ob, ts=ts: e.dma_start(
                        out=br[1, :, ts:ts + 128].rearrange("(g c) t -> c g t", c=128), in_=ob), ["sgo%d" % p2],
                        [("br", 1, g_, t0) for g_ in range(4)] if sidx == w // 128 - 1 else [("brpart", ts)])
            S.barrier()

        def zero_branch(k, tiles):
            zt = bbuf[:, 0:512]
            S.c("dve", lambda e: e.memset(zt, 0.0), [], ["zt"])
            for ti, (t0, w) in enumerate(tiles):
                for ch in range(4):
                    S.dma("sp", lambda e, ch=ch, t0=t0, w=w: e.dma_start(out=br[k, ch * 128:(ch + 1) * 128, t0:t0 + w], in_=zt[:, 0:w]),
                          ["zt"], [("br", k, ch, t0)])
            S.barrier()

        def merge(l, tiles):
            norm_to_hall(l, 1, tiles, "mix")
            S.barrier()
            mall = bbuf[:, 0:18432].rearrange("p (k t) -> p k t", k=KC)
            for dc in range(KC):
                par = dc % 2
                wg = bbuf[:, 18432 + par * 6144:18432 + par * 6144 + 4096].rearrange("p (k c n) -> p k c n", k=4, c=KC)
                wb = bbuf[:, 18432 + par * 6144 + 4096:18432 + par * 6144 + 6144].rearrange("p (k c n) -> p k c n", k=4, c=4)
                for k in range(4):
                    S.dma("pool", lambda e, wg=wg, k=k, dc=dc: e.dma_start(
                        out=wg[:, k, :, :], in_=w_gate[l, k].rearrange("(kc p) n -> p kc n", p=128)[:, :, dc * 128:(dc + 1) * 128]),
                        [], ["wg%d_%d" % (par, k)], stream="w")
                    S.dma("pool", lambda e, wb=wb, k=k, dc=dc: e.dma_start(
                        out=wb[:, k, :, :], in_=w_branch[l, k].rearrange("(cc p) n -> p cc n", p=128)[:, :, dc * 128:(dc + 1) * 128]),
                        [], ["wb%d_%d" % (par, k)], stream="w")
                for ti, (t0, w) in enumerate(tiles):
                    it = dc * len(tiles) + ti
                    p2 = it % 2
                    brt = fbuf[:, 0:4096].bitcast(BF16).rearrange("p (k c t) -> p k c t", k=4, c=4)
                    for k in range(4):
                        S.dma("sp", lambda e, brt=brt, k=k, t0=t0, w=w: e.dma_start(
                            out=brt[:, k, :, 0:w], in_=br[k, :, t0:t0 + w].rearrange("(c p) t -> p c t", p=128)),
                            [("br", k, c_, t0) for c_ in range(4)], ["brt_%d" % k])
                    macc = fbuf[:, 4096 + p2 * 512:4096 + p2 * 512 + w]
                    for k in range(4):
                        pg, pb = banks[(k % 2) * 2], banks[(k % 2) * 2 + 1]
                        for kc in range(KC):
                            S.c("pe", lambda e, pg=pg, wg=wg, k=k, kc=kc, t0=t0, w=w: e.matmul(
                                pg[:, 0:w], lhsT=wg[:, k, kc, :], rhs=hall[:, kc, t0:t0 + w], start=(kc == 0), stop=(kc == KC - 1)),
                                ["wg%d_%d" % (par, k), "hall%d" % ti], ["bank%d" % ((k % 2) * 2)], pe_acc=True)
                        for cc in range(4):
                            S.c("pe", lambda e, pb=pb, wb=wb, k=k, cc=cc, brt=brt, w=w: e.matmul(
                                pb[:, 0:w], lhsT=wb[:, k, cc, :], rhs=brt[:, k, cc, 0:w], start=(cc == 0), stop=(cc == 3)),
                                ["wb%d_%d" % (par, k), "brt_%d" % k], ["bank%d" % ((k % 2) * 2 + 1)], pe_acc=True)
                        sg = fbuf[:, 5120 + (k % 2) * 512:5120 + (k % 2) * 512 + w]
                        S.c("act", lambda e, sg=sg, pg=pg, k=k, dc=dc, w=w: e.activation(
                            out=sg, in_=pg[:, 0:w], func=AF.Sigmoid, bias=bgt[:, l, k * 8 + dc:k * 8 + dc + 1], scale=1.0),
                            ["bank%d" % ((k % 2) * 2), "bgt%d" % l], ["msg%d" % (k % 2)])
                        if k == 0:
                            S.c("dve", lambda e, macc=macc, pb=pb, sg=sg, w=w: e.tensor_tensor(out=macc, in0=pb[:, 0:w], in1=sg, op=ALU.mult),
                                ["bank%d" % ((k % 2) * 2 + 1), "msg%d" % (k % 2)], ["macc%d" % p2])
                        else:
                            S.c("dve", lambda e, sg=sg, pb=pb, w=w: e.tensor_tensor(out=sg, in0=pb[:, 0:w], in1=sg, op=ALU.mult),
                                ["bank%d" % ((k % 2) * 2 + 1), "msg%d" % (k % 2)], ["msg%d" % (k % 2)])
                            if k < 3:
                                S.c("pool", lambda e, macc=macc, sg=sg: e.tensor_tensor(out=macc, in0=macc, in1=sg, op=ALU.add),
                                    ["macc%d" % p2, "msg%d" % (k % 2)], ["macc%d" % p2])
                            else:
                                S.c("pool", lambda e, macc=macc, sg=sg, dc=dc, t0=t0, w=w: e.tensor_tensor(
                                    out=mall[:, dc, t0:t0 + w], in0=macc, in1=sg, op=ALU.add),
                                    ["macc%d" % p2, "msg%d" % (k % 2)], [("mall", ti)])
            S.barrier()
            wo = bbuf[:, 18432:18432 + 8192].rearrange("p (k n) -> p k n", k=KC)
            S.dma("pool", lambda e: e.dma_start(out=wo, in_=w_out[l].rearrange("(kc p) n -> p kc n", p=128)), [], ["wo"], stream="w")
            for ti, (t0, w) in enumerate(tiles):
                j = 1 if t0 < CTX else 0
                for dc in range(KC):
                    pb = banks[4 + dc % 2]
                    for kc in range(KC):
                        S.c("pe", lambda e, pb=pb, kc=kc, dc=dc, t0=t0, w=w: e.matmul(
                            pb[:, 0:w], lhsT=wo[:, kc, dc * 128:(dc + 1) * 128], rhs=mall[:, kc, t0:t0 + w],
                            start=(kc == 0), stop=(kc == KC - 1)), ["wo", ("mall", ti)], ["bank%d" % (4 + dc % 2)], pe_acc=True)
                    S.c("dve", lambda e, dc=dc, pb=pb, t0=t0, w=w, j=j: e.scalar_tensor_tensor(
                        out=xs[:, dc, t0:t0 + w], in0=pb[:, 0:w], scalar=gsh[:, l, 1, 2, dc, j:j + 1],
                        in1=xs[:, dc, t0:t0 + w], op0=ALU.mult, op1=ALU.add),
                        ["bank%d" % (4 + dc % 2), ("xs", t0, dc), "gsh%d" % l], [("xs", t0, dc)])
            S.barrier()


        def ag(src2d, dst2d, rkeys, wkeys):
            S.cc(lambda e: e.collective_compute("AllGather", ALU.bypass, replica_groups=GROUPS4,
                                                ins=[src2d], outs=[dst2d]), rkeys, wkeys)

        def ag_fence(keys):
            S.cc(lambda e: e.collective_compute("AllGather", ALU.bypass, replica_groups=GROUPS4,
                                                ins=[fence_in[:, :]], outs=[fence_out[:, :]]), keys, keys)

        def fnet_exchange_in():
            for q in range(2):
                ag(fx_in[q], fx_out[q], [("fx_in", q)], [("fx_out", q)])

        def fnet_branch(l, with_ctx):
            xsel = arena[:, 0:4096].bitcast(BF16)
            zsb = arena[:, 4096:20480].bitcast(BF16)
            z3 = zsb.rearrange("p (n k) -> p n k", k=256)
            bsb = arena[:, 20480:28672].bitcast(BF16)
            b3 = bsb.rearrange("p (k m) -> p k m", m=128)
            ysb = xsel
            misc = arena[:, 28672:32768].bitcast(BF16)
            fc = misc[:, 0:256]
            fhi = misc[0:64, 256:512]
            f256 = misc[:, 512:1536].rearrange("p (b m) -> p b m", b=2)
            S.dma("sp", lambda e: e.dma_start(out=fc, in_=fc_in[:, :]), [], ["fc"])
            S.dma("sp", lambda e: e.dma_start(out=fhi, in_=fhi_in[:, :]), [], ["fhi"])
            S.dma("sp", lambda e: e.dma_start(out=f256, in_=f256_in.rearrange("b p m -> p b m")), [], ["f256"])
            ev = [0]
            for r in range(4):
                for sx in range(4):
                    it = r * 4 + sx
                    p2 = it % 2
                    xin = misc[:, 1536 + p2 * 2048:1536 + p2 * 2048 + 2048].rearrange("p (g t) -> p g t", g=4)
                    for q in range(2):
                        S.dma("pool", lambda e, xin=xin, q=q, r=r, sx=sx: e.dma_start(
                            out=xin[:, 2 * q:2 * q + 2, :],
                            in_=fx_out[q, r * 256:(r + 1) * 256, sx * 512:(sx + 1) * 512].rearrange("(h c) t -> c h t", c=128)),
                            [("fx_out", 0), ("fx_out", 1)], ["xin%d_%d" % (p2, q)])
                    bk = banks[p2]
                    for g in range(4):
                        S.c("pe", lambda e, bk=bk, xin=xin, g=g: e.matmul(
                            bk[:, 0:512], lhsT=selm[:, g, :], rhs=xin[:, g, :], start=(g == 0), stop=(g == 3)),
                            ["selm", "xin%d_%d" % (p2, g // 2)], ["bank%d" % p2], pe_acc=True)
                    n0 = 2048 * r + 512 * sx
                    ev[0] += 1
                    evac(ev[0], xsel[:, n0:n0 + 512], bk[:, 0:512], ["bank%d" % p2], ["xsel"])
            S.barrier()
            if False:
                dump(0, xsel[:, 0:512], 512, ["xsel"])
            for pr in range(64):
                p2 = pr % 2
                bk = banks[2 + p2]
                for hh in range(2):
                    nlo = 2 * pr + hh
                    S.c("pe", lambda e, bk=bk, hh=hh, nlo=nlo: e.matmul(
                        bk[0:64, hh * 256:(hh + 1) * 256], lhsT=xsel[:, nlo:8192:128], rhs=fc, start=True, stop=True),
                        ["xsel", "fc"], ["bank%d" % (2 + p2)])
                ev[0] += 1
                evac(ev[0], zsb[0:64, pr * 512:(pr + 1) * 512], bk[0:64, 0:512], ["bank%d" % (2 + p2)], ["zsb"])
            S.barrier()
            if False:
                dump(512, zsb[0:64, 0:512], 512, ["zsb"], 64)
            for grp in range(32):
                p2 = grp % 2
                bk = banks[4 + p2]
                for i4 in range(4):
                    k2c = grp * 4 + i4
                    S.c("pe", lambda e, bk=bk, i4=i4, k2c=k2c: e.matmul(
                        bk[:, i4 * 128:(i4 + 1) * 128], lhsT=z3[0:64, :, k2c], rhs=fhi[:, 0:128], start=True, stop=False),
                        ["zsb", "fhi"], ["bank%d" % (4 + p2)], pe_acc=True)
                    S.c("pe", lambda e, bk=bk, i4=i4, k2c=k2c: e.matmul(
                        bk[:, i4 * 128:(i4 + 1) * 128], lhsT=z3[0:64, :, 128 + k2c], rhs=fhi[:, 128:256], start=False, stop=True),
                        ["zsb", "fhi"], ["bank%d" % (4 + p2)], pe_acc=True)
                ev[0] += 1
                evac(ev[0], bsb[:, grp * 512:(grp + 1) * 512], bk[:, 0:512], ["bank%d" % (4 + p2)], ["bsb"])
            S.barrier()
            if False:
                dump(1024, bsb[:, 0:512], 512, ["bsb"])
            yv = ysb.rearrange("p (kl kh) -> p kl kh", kh=64)
            for grp in range(16):
                p2 = grp % 2
                mt = misc[:, 5632 + p2 * 1024:5632 + p2 * 1024 + 1024].rearrange("p (k m) -> p k m", k=4)
                S.dma("sp", lambda e, mt=mt, grp=grp: e.dma_start(
                    out=mt, in_=m_tab[grp * 4:(grp + 1) * 4].rearrange("k p m -> p k m")), [], ["mt%d" % p2])
                bk = banks[6 + p2]
                for i4 in range(4):
                    khi = grp * 4 + i4
                    S.c("pe", lambda e, bk=bk, i4=i4, khi=khi, mt=mt: e.matmul(
                        bk[:, i4 * 128:(i4 + 1) * 128], lhsT=b3[:, :, khi], rhs=mt[:, i4, 0:128], start=True, stop=False),
                        ["bsb", "mt%d" % p2], ["bank%d" % (6 + p2)], pe_acc=True)
                    S.c("pe", lambda e, bk=bk, i4=i4, khi=khi, mt=mt: e.matmul(
                        bk[:, i4 * 128:(i4 + 1) * 128], lhsT=b3[:, :, 64 + khi], rhs=mt[:, i4, 128:256], start=False, stop=True),
                        ["bsb", "mt%d" % p2], ["bank%d" % (6 + p2)], pe_acc=True)
                ev[0] += 1
                evac(ev[0], yv[:, :, grp * 4:(grp + 1) * 4], bk[:, 0:512].rearrange("p (i kl) -> p kl i", i=4),
                     ["bank%d" % (6 + p2)], ["ysb"])
            S.barrier()
            if False:
                dump(1536, ysb[:, 0:512], 512, ["ysb"])
            for q in range(2):
                S.dma("sp", lambda e, q=q: e.dma_start(out=fy_in[q], in_=ysb[:, q * 4096:(q + 1) * 4096]), ["ysb"], [("fy_in", q)])
                ag(fy_in[q], fy_out[q], [("fy_in", q)], [("fy_out", q)])
            if with_ctx:
                for g in range(4):
                    p2 = g % 2
                    xc = misc[:, 1536 + p2 * 256:1536 + p2 * 256 + 256]
                    S.dma("sp", lambda e, xc=xc, g=g: e.dma_start(out=xc, in_=pfm[1568 + g * 128:1568 + (g + 1) * 128, 0:256]),
                          [("pfm", 1568 + g * 128, 0)], ["xc%d" % p2])
                    bk = banks[p2]
                    zc = misc[:, 2560 + p2 * 512:2560 + p2 * 512 + 512].rearrange("p (b m) -> p b m", b=2)
                    for tb in range(2):
                        S.c("pe", lambda e, bk=bk, xc=xc, tb=tb: e.matmul(
                            bk[:, tb * 256:(tb + 1) * 256], lhsT=xc[:, tb * 128:(tb + 1) * 128], rhs=fc, start=True, stop=True),
                            ["xc%d" % p2, "fc"], ["bank%d" % p2])
                    ev[0] += 1
                    evac(ev[0], zc, bk[:, 0:512].rearrange("p (b m) -> p b m", b=2), ["bank%d" % p2], ["zc%d" % p2])
                    bk2 = banks[2 + p2]
                    for tb in range(2):
                        S.c("pe", lambda e, bk2=bk2, zc=zc, tb=tb: e.matmul(
                            bk2[:, 0:256], lhsT=zc[:, tb, 0:128], rhs=f256[:, tb, 0:256], start=(tb == 0), stop=False),
                            ["zc%d" % p2, "f256"], ["bank%d" % (2 + p2)], pe_acc=True)
                        S.c("pe", lambda e, bk2=bk2, zc=zc, tb=tb: e.matmul(
                            bk2[:, 0:256], lhsT=zc[:, tb, 128:256], rhs=f256[:, tb, 256:512], start=False, stop=(tb == 1)),
                            ["zc%d" % p2, "f256"], ["bank%d" % (2 + p2)], pe_acc=True)
                    yc = misc[:, 3584 + p2 * 256:3584 + p2 * 256 + 256]
                    ev[0] += 1
                    evac(ev[0], yc, bk2[:, 0:256], ["bank%d" % (2 + p2)], ["yc%d" % p2])
                    if False:
                        dump(2560, yc, 256, ["yc%d" % p2])
                    S.dma("sp", lambda e, yc=yc, g=g: e.dma_start(out=br[2, g * 128:(g + 1) * 128, 0:256], in_=yc),
                          ["yc%d" % p2], [("br", 2, g, 0)])
            for sx in range(4):
                for g in range(4):
                    it = sx * 4 + g
                    p2 = it % 2
                    yin = misc[:, 4096 + p2 * 2048:4096 + p2 * 2048 + 2048] if False else \
                        arena[:, 4096 + p2 * 1024:4096 + p2 * 1024 + 1024].bitcast(BF16)
                    yin4 = yin.rearrange("p (q j t) -> p q j t", q=2, j=2)
                    for q in range(2):
                        S.dma("pool", lambda e, yin4=yin4, g=g, sx=sx, q=q: e.dma_start(
                            out=yin4[:, q, :, :],
                            in_=fy_out[q, g * 128:(g + 1) * 128, :].rearrange("c (j t) -> c j t", j=2)[:, :, sx * 512:(sx + 1) * 512]),
                            [("fy_out", 0), ("fy_out", 1)], ["yin%d_%d" % (p2, q)])
                    bk = banks[4 + p2]
                    for jj in range(4):
                        S.c("pe", lambda e, bk=bk, yin4=yin4, jj=jj: e.matmul(
                            bk[:, 0:512], lhsT=selm[:, jj, :], rhs=yin4[:, jj // 2, jj % 2, :], start=(jj == 0), stop=(jj == 3)),
                            ["selm", "yin%d_%d" % (p2, jj // 2)], ["bank%d" % (4 + p2)], pe_acc=True)
                    yo = arena[:, 8192 + p2 * 256:8192 + p2 * 256 + 256].bitcast(BF16)
                    ev[0] += 1
                    evac(ev[0], yo, bk[:, 0:512], ["bank%d" % (4 + p2)], ["yo%d" % p2])
                    t0 = CTX + sx * 512
                    if False:
                        dump(2048, yo, 512, ["yo%d" % p2])
                    S.dma("sp", lambda e, yo=yo, g=g, t0=t0: e.dma_start(out=br[2, g * 128:(g + 1) * 128, t0:t0 + 512], in_=yo),
                          ["yo%d" % p2], [("br", 2, g, t0)])
            S.barrier()


        def gla_branch(l, mtiles):
            A = arena
            aT = A[:, 0:1152].bitcast(BF16)
            spb = A[:, 1152:3456]
            U = A[:, 3456:5760]
            S32 = A[:, 5760:8192].rearrange("p (c m) -> p c m", m=128)
            qT2 = A[:, 8192:9344].bitcast(BF16)
            kT2 = A[:, 9344:10496].bitcast(BF16)
            qp = A[:, 10496:11648].bitcast(BF16)
            kpp = A[:, 11648:12800].bitcast(BF16)
            kppT = A[:, 12800:13952].bitcast(BF16).rearrange("p (c m) -> p c m", m=128)
            Sbf = A[:, 13952:15168].bitcast(BF16).rearrange("p (c m) -> p c m", m=128)
            vh = A[:, 15168:16320].bitcast(BF16).rearrange("p (c m) -> p c m", m=128)
            rT = A[:, 16320:17472].bitcast(BF16)
            ebt = A[:, 17472:17984]
            enbt = A[:, 17984:18496]
            ATs = [A[:, 18496 + i * 128:18496 + (i + 1) * 128].bitcast(BF16) for i in range(2)]
            wa2h = A[:, 18752:18816].bitcast(BF16)
            kppF = A[:, 27104:28256].bitcast(BF16)
            kppB = A[:, 28256:29408].bitcast(BF16)
            dtp = A[:, 18944:18962]
            dt2 = A[:, 18962:18980]
            dex = A[:, 18980:18998]
            dsum = A[:, 18998:18999]
            osb = A[:, 19008:19520]
            rsb = A[:, 19520:20032]
            srb = A[:, 20032:20544]
            sqb = A[:, 20544:20800].bitcast(BF16)
            ogb = A[:, 20800:21056].bitcast(BF16)
            gx = A[:, 21056:21312]
            gth = A[:, 21572:22596].rearrange("p (r m) -> p r m", m=256)
            Tc = A[:, 23636:23764]
            Lctx = A[:, 23764:23892]
            dpr = A[:, 23892:23896]
            onesf = A[:, 24000:24128]
            cs = A[:, 24800:27104]
            bfb = banks[7][:, 0:256].bitcast(BF16)
            S.c("dve", lambda e: e.memset(onesf, 1.0), [], ["onesf"])
            S.c("dve", lambda e: e.memset(A[:, 0:1152], 0.0), [], ["aT"])
            S.c("dve", lambda e: e.memset(A[:, 18752:18816], 0.0), [], ["wa2h"])
            S.c("dve", lambda e: e.memset(A[:, 27104:29408], 0.0), [], ["kppF", "kppB"])
            S.barrier()
            S.dma("pool", lambda e: e.dma_start(out=aT[0:32, :], in_=pfm[1024:1056, :]), [], ["aT"])
            for h in range(4):
                for half in range(2):
                    S.dma("pool", lambda e, half=half, h=h: e.dma_start(
                        out=qT2[half * 64:(half + 1) * 64, :], in_=pfm[64 * h:64 * h + 64, :]), [], ["qT2_%d" % half])
                    S.dma("pool", lambda e, half=half, h=h: e.dma_start(
                        out=kT2[half * 64:(half + 1) * 64, :], in_=pfm[256 + 64 * h:256 + 64 * h + 64, :]), [], ["kT2_%d" % half])
                S.dma("pool", lambda e, h=h: e.dma_start(
                    out=vh, in_=ptm[:, 128 * h:128 * (h + 1)].rearrange("(c p) d -> p c d", p=128)), [], ["vh"])
                S.dma("pool", lambda e, h=h: e.dma_start(out=rT, in_=pfm[512 + 128 * h:512 + 128 * (h + 1), :]), [], ["rT"])
                S.dma("pool", lambda e, h=h: e.dma_start(out=wa2h[0:32, :], in_=wa2_in[l, h]), [], ["wa2h"], stream="w")
                for ti, (t0, w) in enumerate(TILES):
                    bk = banks[ti % 2]
                    S.c("pe", lambda e, bk=bk, t0=t0, w=w: e.matmul(bk[:, 0:w], lhsT=wa2h, rhs=aT[:, t0:t0 + w], start=True, stop=True),
                        ["wa2h", "aT"], ["bank%d" % (ti % 2)])
                    S.c("act", lambda e, bk=bk, t0=t0, w=w, h=h: e.activation(
                        out=spb[:, t0:t0 + w], in_=bk[:, 0:w], func=AF.Exp, scale=-1.0, bias=nba[:, l, h:h + 1]),
                        ["bank%d" % (ti % 2), "nba%d" % l], ["spb"])
                S.c("act", lambda e: e.activation(out=spb, in_=spb, func=AF.Ln, scale=1.0, bias=onec[:]), ["spb", "onec"], ["spb"])
                for c in range(18):
                    S.c("dve", lambda e, c=c: e.tensor_tensor_scan(
                        out=cs[:, c * 128:(c + 1) * 128], data0=onesf, data1=spb[:, c * 128:(c + 1) * 128], initial=0.0,
                        op0=ALU.mult, op1=ALU.add), ["spb", "onesf"], ["cs"])
                S.c("dve", lambda e: e.tensor_copy(out=dtp, in_=cs[:, 127:2304:128]), ["cs"], ["dtp"])
                for c in range(18):
                    S.c("dve", lambda e, c=c: e.scalar_tensor_tensor(
                        out=cs[64:128, c * 128:(c + 1) * 128], in0=cs[64:128, c * 128:(c + 1) * 128], scalar=dtp[64:128, c:c + 1],
                        in1=spb[64:128, c * 128:(c + 1) * 128], op0=ALU.subtract, op1=ALU.subtract), ["cs", "dtp", "spb"], ["cs"])
                S.c("dve", lambda e: e.tensor_copy(out=dt2[0:64, :], in_=cs[0:64, 127:2304:128]), ["cs"], ["dt2"])
                S.c("dve", lambda e: e.tensor_copy(out=dt2[64:128, :], in_=cs[64:128, 0:2304:128]), ["cs", "dt2"], ["dt2"])
                S.c("act", lambda e: e.activation(out=dex, in_=dt2, func=AF.Exp, scale=sca[:, 0:1]), ["dt2", "sca"], ["dex"])
                S.c("dve", lambda e: e.tensor_reduce(out=dsum, in_=dt2[:, 2:18], axis=AX.X, op=ALU.add), ["dt2"], ["dsum"])
                S.c("act", lambda e: e.activation(out=gx[:, 128:129], in_=dsum, func=AF.Exp, scale=sca[:, 0:1]), ["dsum", "sca"], ["gxD"])
                for ti, (t0, w) in enumerate(TILES):
                    S.c("act", lambda e, t0=t0, w=w: e.activation(out=ebt[:, 0:w], in_=cs[:, t0:t0 + w], func=AF.Exp, scale=sca[:, 0:1]),
                        ["cs", "sca"], ["ebt"])
                    S.c("act", lambda e, t0=t0, w=w: e.activation(out=enbt[:, 0:w], in_=cs[:, t0:t0 + w], func=AF.Exp, scale=sca[:, 1:2]),
                        ["cs", "sca"], ["enbt"])
                    S.c("dve", lambda e, t0=t0, w=w: e.scalar_tensor_tensor(
                        out=qp[:, t0:t0 + w], in0=qT2[:, t0:t0 + w], scalar=0.125, in1=ebt[:, 0:w], op0=ALU.mult, op1=ALU.mult),
                        ["qT2_0", "qT2_1", "ebt"], ["qp"])
                    S.c("dve", lambda e, t0=t0, w=w: e.tensor_tensor(
                        out=kpp[:, t0:t0 + w], in0=kT2[:, t0:t0 + w], in1=enbt[:, 0:w], op=ALU.mult),
                        ["kT2_0", "kT2_1", "enbt"], ["kpp"])
                    S.c("dve", lambda e, t0=t0, w=w: e.tensor_copy(out=kppF[0:64, t0:t0 + w], in_=kpp[0:64, t0:t0 + w]), ["kpp"], ["kppF"])
                    S.c("dve", lambda e, t0=t0, w=w: e.tensor_copy(out=kppB[64:128, t0:t0 + w], in_=kpp[64:128, t0:t0 + w]), ["kpp"], ["kppB"])
                for c in range(18):
                    S.c("pe", lambda e, c=c: e.transpose(out=bfb[:, 0:128], in_=kpp[:, c * 128:(c + 1) * 128], identity=ident_bf[:]),
                        ["kpp", "ident_bf"], ["bank7"])
                    S.c("dve", lambda e, c=c: e.tensor_copy(out=kppT[:, c, :], in_=bfb[:, 0:128]), ["bank7"], ["kppT%d" % (c % 2)])
                    bk = banks[2 + c % 2]
                    S.c("pe", lambda e, c=c, bk=bk: e.matmul(bk[:, 0:128], lhsT=kppT[:, c, :], rhs=vh[:, c, :], start=True, stop=True),
                        ["kppT%d" % (c % 2), "vh"], ["bank%d" % (2 + c % 2)])
                    S.c("act", lambda e, c=c, bk=bk: e.activation(out=U[:, c * 128:(c + 1) * 128], in_=bk[:, 0:128], func=AF.Identity,
                                                                 scale=dex[:, c:c + 1]), ["bank%d" % (2 + c % 2), "dex"], ["U"])

                def chain(rows, out_ap, in_ap, c):
                    r0, r1 = rows
                    S.c("dve", lambda e: e.scalar_tensor_tensor(
                        out=out_ap[r0:r1, :], in0=in_ap[r0:r1, :], scalar=dex[r0:r1, c:c + 1], in1=U[r0:r1, c * 128:(c + 1) * 128],
                        op0=ALU.mult, op1=ALU.add), ["U", "dex", "chain%d" % r0], ["chain%d" % r0])
                FW, BW = (0, 64), (64, 128)
                chain(FW, Lctx, U[:, 0:128], 1)
                chain(BW, Lctx, U[:, 128:256], 0)
                chain(FW, gx[:, 0:128], U[:, 256:384], 3)
                for c in range(4, 18):
                    chain(FW, gx[:, 0:128], gx[:, 0:128], c)
                chain(BW, gx[:, 0:128], U[:, 17 * 128:18 * 128], 16)
                for c in range(15, 1, -1):
                    chain(BW, gx[:, 0:128], gx[:, 0:128], c)
                S.dma("pool", lambda e, h=h: e.dma_start(out=gx_d[h], in_=gx), ["chain0", "chain64", "gxD"], [("gx_d", h)])
                ag(gx_d[h], gxo_d[h], [("gx_d", h)], [("gxo_d", h)])
                S.dma("pool", lambda e, h=h: e.dma_start(out=gth, in_=gxo_d[h].rearrange("(r p) m -> p r m", p=128)),
                      [("gxo_d", h)], ["gth"])
                S.c("dve", lambda e: e.scalar_tensor_tensor(out=dpr, in0=gth[:, :, 128], scalar=-1.0, in1=flg[:, 0:4],
                                                            op0=ALU.add, op1=ALU.mult), ["gth", "flg"], ["dpr"])
                S.c("dve", lambda e: e.tensor_scalar(out=dpr, in0=dpr, scalar1=1.0, scalar2=None, op0=ALU.add), ["dpr"], ["dpr"])
                for i in range(4):
                    S.c("dve", lambda e, i=i: e.tensor_scalar(out=gth[:, i, 0:128], in0=gth[:, i, 0:128], scalar1=flg[:, i:i + 1],
                                                             scalar2=None, op0=ALU.mult), ["gth", "flg"], ["gth"])
                for (r0, r1), order in ((FW, (0, 1, 2, 3)), (BW, (3, 2, 1, 0))):
                    src = Lctx
                    for i in order:
                        S.c("dve", lambda e, r0=r0, r1=r1, i=i, src=src: e.scalar_tensor_tensor(
                            out=Tc[r0:r1, :], in0=src[r0:r1, :], scalar=dpr[r0:r1, i:i + 1], in1=gth[r0:r1, i, 0:128],
                            op0=ALU.mult, op1=ALU.add), ["gth", "dpr", "chain%d" % r0, "tc%d" % r0], ["tc%d" % r0])
                        src = Tc
                S.c("dve", lambda e: e.memset(S32[0:64, 0, :], 0.0), [], ["s32_0"])
                S.c("dve", lambda e: e.memset(S32[64:128, 1, :], 0.0), [], ["s32_64"])
                S.c("dve", lambda e: e.tensor_copy(out=S32[0:64, 1, :], in_=U[0:64, 0:128]), ["U"], ["s32_0"])
                S.c("dve", lambda e: e.tensor_copy(out=S32[64:128, 0, :], in_=U[64:128, 128:256]), ["U"], ["s32_64"])
                S.c("dve", lambda e: e.tensor_copy(out=S32[0:64, 2, :], in_=Tc[0:64, :]), ["tc0"], ["s32_0"])
                S.c("dve", lambda e: e.tensor_copy(out=S32[64:128, 17, :], in_=Tc[64:128, :]), ["tc64"], ["s32_64"])
                for c in range(2, 17):
                    S.c("dve", lambda e, c=c: e.scalar_tensor_tensor(
                        out=S32[0:64, c + 1, :], in0=S32[0:64, c, :], scalar=dex[0:64, c:c + 1], in1=U[0:64, c * 128:(c + 1) * 128],
                        op0=ALU.mult, op1=ALU.add), ["U", "dex", "s32_0"], ["s32_0"])
                for c in range(17, 2, -1):
                    S.c("dve", lambda e, c=c: e.scalar_tensor_tensor(
                        out=S32[64:128, c - 1, :], in0=S32[64:128, c, :], scalar=dex[64:128, c:c + 1], in1=U[64:128, c * 128:(c + 1) * 128],
                        op0=ALU.mult, op1=ALU.add), ["U", "dex", "s32_64"], ["s32_64"])
                S.c("dve", lambda e: e.tensor_copy(out=Sbf[:, 0:18, :], in_=S32[:, 0:18, :]), ["s32_0", "s32_64"], ["Sbf"])
                for ti, (t0, w) in enumerate(mtiles):
                    bo = banks[4 + ti % 2]
                    for cc in range(w // 128):
                        c = t0 // 128 + cc
                        ci = c % 2
                        ba = banks[ci]
                        cols = slice(c * 128, (c + 1) * 128)
                        S.c("pe", lambda e, ba=ba, cols=cols: e.matmul(ba[:, 0:128], lhsT=kppF[:, cols], rhs=qp[:, cols], start=True, stop=True),
                            ["kppF", "qp"], ["bank%d" % ci])
                        S.c("pe", lambda e, ba=ba, cols=cols: e.matmul(ba[:, 128:256], lhsT=kppB[:, cols], rhs=qp[:, cols], start=True, stop=True),
                            ["kppB", "qp"], ["bank%d" % ci])
                        at = ATs[ci]
                        S.c("dve", lambda e, ba=ba, at=at: e.tensor_tensor(out=at, in0=ba[:, 0:256], in1=gmask[:], op=ALU.mult),
                            ["bank%d" % ci, "gmask"], ["ats%d" % ci])
                        oc = bo[:, cc * 128:(cc + 1) * 128]
                        S.c("pe", lambda e, oc=oc, c=c, cols=cols: e.matmul(oc, lhsT=Sbf[:, c, :], rhs=qp[:, cols], start=True, stop=False),
                            ["Sbf", "qp"], ["bank%d" % (4 + ti % 2)], pe_acc=True)
                        S.c("pe", lambda e, oc=oc, c=c, at=at: e.matmul(oc, lhsT=vh[:, c, :], rhs=at[:, 0:128], start=False, stop=False),
                            ["vh", "ats%d" % ci], ["bank%d" % (4 + ti % 2)], pe_acc=True)
                        S.c("pe", lambda e, oc=oc, c=c, at=at: e.matmul(oc, lhsT=vh[:, c, :], rhs=at[:, 128:256], start=False, stop=True),
                            ["vh", "ats%d" % ci], ["bank%d" % (4 + ti % 2)], pe_acc=True)
                    S.c("act", lambda e, bo=bo, w=w: e.activation(out=sqb[:, 0:w], in_=bo[:, 0:w], func=AF.Square),
                        ["bank%d" % (4 + ti % 2)], ["gsq"])
                    S.c("pe", lambda e, w=w: e.matmul(banks[6][:, 0:w], lhsT=ones_bf[:], rhs=sqb[:, 0:w], start=True, stop=True),
                        ["gsq", "ones"], ["bank6"])
                    S.c("act", lambda e, w=w: e.activation(out=rsb[:, 0:w], in_=banks[6][:, 0:w], func=AF.Sqrt, scale=1.0 / 128, bias=epsc[:]),
                        ["bank6", "epsc"], ["grs"])
                    S.c("dve", lambda e, w=w: e.reciprocal(out=rsb[:, 0:w], in_=rsb[:, 0:w]), ["grs"], ["grs"])
                    S.c("dve", lambda e, bo=bo, w=w: e.tensor_tensor(out=osb[:, 0:w], in0=bo[:, 0:w], in1=rsb[:, 0:w], op=ALU.mult),
                        ["bank%d" % (4 + ti % 2), "grs"], ["gos"])
                    S.c("act", lambda e, t0=t0, w=w: e.activation(out=srb[:, 0:w], in_=rT[:, t0:t0 + w], func=AF.Silu), ["rT"], ["gsr"])
                    S.c("dve", lambda e, w=w, h=h: e.scalar_tensor_tensor(
                        out=ogb[:, 0:w], in0=osb[:, 0:w], scalar=ggo[:, l, h:h + 1], in1=srb[:, 0:w], op0=ALU.mult, op1=ALU.mult),
                        ["gos", "gsr", "ggo%d" % l], ["gog"])
                    S.dma("sp", lambda e, h=h, t0=t0, w=w: e.dma_start(out=br[0, 128 * h:128 * (h + 1), t0:t0 + w], in_=ogb[:, 0:w]),
                          ["gog"], [("br", 0, h, t0)])
                S.barrier()

        def final_out():
            for ti, (t0, w) in enumerate(TILES if debug else TILES[1:]):
                sq = bbuf[:, 24576:24576 + KC * 512].rearrange("p (k t) -> p k t", k=KC)[:, :, 0:w]
                S.c("act", lambda e, t0=t0, w=w, sq=sq: e.activation(out=sq, in_=xs[:, :, t0:t0 + w], func=AF.Square),
                    [("xs", t0, k_) for k_ in range(KC)], ["sq"])
                for kc in range(KC):
                    S.c("pe", lambda e, kc=kc, w=w, sq=sq: e.matmul(
                        banks[7][:, 0:w], lhsT=ones_bf[:], rhs=sq[:, kc, :], start=(kc == 0), stop=(kc == KC - 1)),
                        ["sq", "ones"], ["bank7"], pe_acc=True)
                rs = fbuf[:, 0:w]
                S.c("act", lambda e, w=w, rs=rs: e.activation(out=rs, in_=banks[7][:, 0:w], func=AF.Sqrt,
                                                             scale=1.0 / D, bias=epsc[:]), ["bank7", "epsc"], ["rs"])
                S.c("dve", lambda e, rs=rs: e.reciprocal(out=rs, in_=rs), ["rs"], ["rs"])
                ot = fbuf[:, 2048:2048 + KC * 512].rearrange("p (k t) -> p k t", k=KC)
                for kc in range(KC):
                    S.c("dve", lambda e, kc=kc, t0=t0, w=w, rs=rs, ot=ot: e.scalar_tensor_tensor(
                        out=ot[:, kc, 0:w], in0=xs[:, kc, t0:t0 + w], scalar=gfin[:, kc:kc + 1], in1=rs,
                        op0=ALU.mult, op1=ALU.mult), [("xs", t0, kc), "rs", "gfin"], ["otile"])
                if t0 < CTX:
                    S.dma("sp", lambda e, w=w, ot=ot: e.dma_start(
                        out=out_ctx.rearrange("(kc p) t -> p kc t", p=128), in_=ot[:, :, 0:w]), ["otile"], ["outc"])
                else:
                    S.dma("sp", lambda e, t0=t0, w=w, ot=ot: e.dma_start(
                        out=out.rearrange("(kc p) t -> p kc t", p=128)[:, :, t0 - CTX:t0 - CTX + w], in_=ot[:, :, 0:w]),
                        ["otile"], ["out"])
            S.barrier()

        def forward():
            for l in range(DEPTH):
                last = (l == DEPTH - 1)
                ffn(l, 0, TILES)
                if stop_after == ("ffn1", l):
                    return
                mtiles = TILES[1:] if last else TILES
                norm_to_hall(l, 1, TILES, "mix")
                win_proj(l, TILES)
                if "nogla" in dbg_opts:
                    zero_branch(0, mtiles)
                else:
                    gla_branch(l, mtiles)
                fnet_exchange_in()
                if "nofnet" in dbg_opts:
                    zero_branch(2, mtiles)
                sgu_branch(l, mtiles)
                conv_branch(l, mtiles)
                if "nofnet" not in dbg_opts:
                    fnet_branch(l, not last)
                if l == 0:
                    for k_, r0_ in enumerate((1568, 2080, 2592, 3104)):
                        dump(k_ * 256, pfm[r0_:r0_ + 128, 0:256], 256, [])
                        dump(1024 + k_ * 256, pfm[r0_:r0_ + 128, 256:512], 256, [])
                    S.barrier()
                merge(l, mtiles)
                if stop_after == ("mix", l):
                    return
                ffn(l, 1, TILES[1:] if last else TILES)
                if stop_after == ("ffn2", l):
                    return

        forward()
        final_out()

        sems = {e: [es.enter_context(nc.semaphore("sem_%s_%d" % (e, k_))) for k_ in range(S.epoch[e] + 1)]
                for e in Sched.ENGS}
        with nc.Block() as block:
            S.emit(block, sems, dma_sems, cc_sem)
        if debug:
            print("sem epochs", S.epoch, "max count", max(S.max_counts.values()))
    return nc


_NC_CACHE = {}


def make_in_maps(inputs):
    f = lambda a: np.ascontiguousarray(np.asarray(a, dtype=np.float32))
    x, c, ctx, c_ctx = f(inputs["x"]), f(inputs["c"]), f(inputs["ctx"]), f(inputs["c_ctx"])
    shared = {
        "ident": np.eye(128, dtype=np.float32),
        "w_ada": f(inputs["w_ada"]),
        "b_ada": f(inputs["b_ada"]).reshape(DEPTH, 72, 128),
        "g_norm": f(inputs["g_norm"]).reshape(DEPTH, 24, 128),
        "w_ff1": f(inputs["w_ff1"]), "w_ff3": f(inputs["w_ff3"]), "w_ff2": f(inputs["w_ff2"]),
        "g_final": f(inputs["g_final"]).reshape(8, 128),
        "w_in": f(inputs["w_in"]),
        "w_sguT": np.ascontiguousarray(f(inputs["w_sgu"]).transpose(0, 1, 3, 2)),
        "b_sgu": f(inputs["b_sgu"]).reshape(DEPTH, 512),
        "w_conv": f(inputs["w_conv"]).reshape(DEPTH, 12, 128),
        "w_branch": f(inputs["w_branch"]), "w_gate": f(inputs["w_gate"]),
        "b_gate": f(inputs["b_gate"]).reshape(DEPTH, 32, 128),
        "w_out": f(inputs["w_out"]),
    }
    bf = ml_dtypes.bfloat16
    cidx = np.arange(128, dtype=np.float64)
    ang = 2 * np.pi * np.outer(cidx, cidx) / 128
    shared["fc_tab"] = np.concatenate([np.cos(ang), -np.sin(ang)], 1).astype(bf)
    a64 = 2 * np.pi * np.outer(np.arange(64.0), np.arange(64.0)) / 64
    shared["fhi_tab"] = np.concatenate([np.cos(a64), -np.sin(a64), np.sin(a64), np.cos(a64)], 1).astype(bf)
    khi = np.arange(64.0)[:, None, None]
    nlo = np.arange(128.0)[None, :, None]
    klo = np.arange(128.0)[None, None, :]
    am = 2 * np.pi * nlo * (khi + 64 * klo) / 8192
    shared["m_tab"] = (np.concatenate([np.cos(am), np.sin(am)], 2) / 1024.0).astype(bf)
    a256 = 2 * np.pi * np.outer(np.arange(256.0), np.arange(256.0)) / 256
    sc = 1.0 / np.sqrt(256.0 * 128.0)
    shared["f256_tab"] = (np.concatenate([np.cos(a256), np.sin(a256)], 1) * sc).reshape(2, 128, 512).astype(bf)
    wa = f(inputs["w_gla_a2"]); ba = f(inputs["b_gla_a2"])
    wa2 = np.zeros((DEPTH, 4, 32, 128), np.float32)
    ba2 = np.zeros((DEPTH, 4, 128), np.float32)
    for h_ in range(4):
        wa2[:, h_, 0:16, 0:64] = wa[:, 0, :, 64 * h_:64 * h_ + 64]
        wa2[:, h_, 16:32, 64:128] = wa[:, 1, :, 64 * h_:64 * h_ + 64]
        ba2[:, h_, 0:64] = ba[:, 0, 64 * h_:64 * h_ + 64]
        ba2[:, h_, 64:128] = ba[:, 1, 64 * h_:64 * h_ + 64]
    shared["wa2"] = wa2
    shared["ba2"] = ba2
    shared["ggo"] = f(inputs["g_gla_norm"])
    jj, ii = np.meshgrid(np.arange(128), np.arange(128), indexing="ij")
    shared["gmask"] = np.concatenate([(jj <= ii), (jj >= ii)], 1).astype(np.float32).astype(bf)
    sc_ = np.zeros((128, 2), np.float32)
    sc_[0:64, 0] = -1.0 / 16; sc_[64:128, 0] = 1.0 / 16
    sc_[:, 1] = -sc_[:, 0]
    shared["sca"] = sc_
    maps = []
    for r in range(8):
        b, j = r // 4, r % 4
        xf = np.concatenate([ctx[b], x[b, j * TL:(j + 1) * TL]], axis=0).T
        m = dict(shared)
        m["x_fm"] = np.ascontiguousarray(xf)
        sel = np.zeros((4, 128, 128), np.float32)
        sel[j] = np.eye(128, dtype=np.float32)
        m["selm"] = sel.astype(ml_dtypes.bfloat16)
        fl = np.zeros((128, 8), np.float32)
        for i_ in range(4):
            fl[0:64, i_] = 1.0 if i_ < j else 0.0
            fl[64:128, i_] = 1.0 if i_ > j else 0.0
        m["flags"] = fl
        m["cvec"] = np.ascontiguousarray(np.concatenate([c[b].reshape(8, 128), c_ctx.reshape(8, 128)], 0))
        maps.append(m)
    return maps


def kernel(**inputs):
    if "nc" not in _NC_CACHE:
        _NC_CACHE["nc"] = build_program()
    nc = _NC_CACHE["nc"]
    maps = make_in_maps(inputs)
    res = run_bass_kernel_spmd(nc, maps, core_ids=list(range(8)))
    outp = np.empty((2, SEQ, D), dtype=np.float32)
    for r in range(8):
        b, j = r // 4, r % 4
        outp[b, j * TL:(j + 1) * TL, :] = res.results[r]["out_fm"].T
    return outp
```

```python
import os
import numpy as np
import ml_dtypes
import concourse.bass as bass
import concourse.mybir as mybir
from concourse.bass_utils import run_bass_kernel_spmd

F32 = mybir.dt.float32
BF16 = mybir.dt.bfloat16
AF = mybir.ActivationFunctionType
ALU = mybir.AluOpType
AX = mybir.AxisListType

D = 1024
KC = 8
DEPTH = 2
SEQ = 8192
CTX = 256
TL = 2048
T = CTX + TL
DFF = 2816
NFC = 22
INW = 4640
EPS = 1e-6
TILES = [(0, 256), (256, 512), (768, 512), (1280, 512), (1792, 512)]
GROUPS4 = [[0, 1, 2, 3], [4, 5, 6, 7]]


class Op:
    __slots__ = ("eng", "fn", "kind", "stream", "deps", "count", "sig", "idx", "pe_acc", "semi", "epoch")


class Sched:
    ENGS = ("pe", "act", "dve", "pool", "sp")
    KDMA = 8

    def __init__(self, nc):
        self.nc = nc
        self.ops = {e: [] for e in self.ENGS}
        self.allops = []
        self.last_w = {}
        self.readers = {}
        self.nstreams = {}
        self.epoch = {e: 0 for e in self.ENGS}
        self.epoch_ops = {e: 0 for e in self.ENGS}

    def _add(self, eng, fn, reads, writes, kind="c", stream=None, pe_acc=False):
        op = Op()
        op.eng, op.fn, op.kind, op.stream, op.pe_acc = eng, fn, kind, stream, pe_acc
        op.deps = []
        op.sig = False
        op.count = None
        op.idx = len(self.allops)
        op.epoch = self.epoch[eng]
        self.epoch_ops[eng] += 1
        deps = {}
        for k in reads:
            w = self.last_w.get(k)
            if w is not None:
                deps[w.idx] = (w, "raw")
        for k in writes:
            w = self.last_w.get(k)
            if w is not None and w.idx not in deps:
                deps[w.idx] = (w, "waw")
            for r in self.readers.get(k, ()):
                if r.idx not in deps:
                    deps[r.idx] = (r, "war")
        for k in reads:
            lst = self.readers.setdefault(k, [])
            if kind == "c":
                lst[:] = [r for r in lst if not (r.kind == "c" and r.eng == eng)]
            lst.append(op)
        for k in writes:
            self.last_w[k] = op
            self.readers[k] = []
        for w, typ in deps.values():
            if w is op:
                continue
            if w.kind == "c" and op.kind == "c" and w.eng == eng:
                if eng == "pe":
                    continue
                if typ != "raw":
                    continue
            op.deps.append(w)
            w.sig = True
        self.ops[eng].append(op)
        self.allops.append(op)
        return op

    def c(self, eng, fn, reads=(), writes=(), pe_acc=False):
        return self._add(eng, fn, reads, writes, "c", pe_acc=pe_acc)

    def dma(self, eng, fn, reads=(), writes=(), stream="d"):
        return self._add(eng, fn, reads, writes, "d", stream=(eng, stream))

    def cc(self, fn, reads=(), writes=()):
        return self._add("pool", fn, reads, writes, "cc", stream=("pool", "cc"))

    def barrier(self):
        lasts = []
        for e in self.ENGS:
            for op in reversed(self.ops[e]):
                if op.kind == "c":
                    lasts.append(op)
                    break
        seen = {}
        for op in reversed(self.allops):
            if op.kind in ("d", "cc"):
                k = seen.get(op.stream, 0)
                if k < (self.KDMA if op.kind == "d" else 1):
                    seen[op.stream] = k + 1
                    lasts.append(op)
        for e in self.ENGS:
            op = self._add(e, None, [], [], "c")
            for w in lasts:
                if w.kind == "c" and w.eng == e:
                    continue
                op.deps.append(w)
                w.sig = True
        for e in self.ENGS:
            if self.epoch_ops[e] > 1500:
                self.epoch[e] += 1
                self.epoch_ops[e] = 0

    def emit(self, block, sems, dma_sems, cc_sem):
        nc = self.nc
        cnt = {}
        scnt = {}
        for e in self.ENGS:
            pending = []
            for op in self.ops[e]:
                if op.kind == "c":
                    if op.sig:
                        cnt[(e, op.epoch)] = cnt.get((e, op.epoch), 0) + 1
                        op.count = cnt[(e, op.epoch)]
                    else:
                        op.count = None
                elif op.kind == "d":
                    st = op.stream
                    n = scnt.get(st, 0)
                    scnt[st] = n + 1
                    op.semi = n % self.KDMA
                    op.count = 16 * (n // self.KDMA + 1)
                else:
                    st = op.stream
                    scnt[st] = scnt.get(st, 0) + 1
                    op.count = scnt[st]
        engobj = {"pe": "tensor", "act": "scalar", "dve": "vector", "pool": "gpsimd", "sp": "sync"}

        def semof(op):
            if op.kind == "c":
                return sems[op.eng][op.epoch]
            if op.kind == "d":
                return dma_sems[op.stream][op.semi]
            return cc_sem

        def body(e):
            def f(eng):
                waited = {}
                for op in self.ops[e]:
                    need = {}
                    for d in op.deps:
                        s = semof(d)
                        assert d.count is not None, "dependency on non-signalling op"
                        key = id(s)
                        if waited.get(key, 0) >= d.count:
                            continue
                        if key not in need or need[key][1] < d.count:
                            need[key] = (s, d.count)
                    for key, (s, v) in need.items():
                        eng.wait_ge(s, v)
                        waited[key] = v
                    if op.fn is None:
                        if op.sig:
                            ins = eng.nop()
                            ins.then_inc(sems[e][op.epoch], 1)
                        continue
                    ins = op.fn(eng)
                    if op.kind == "c":
                        if op.sig:
                            ins.then_inc(sems[e][op.epoch], 1)
                    elif op.kind == "d":
                        ins.then_inc(dma_sems[op.stream][op.semi], 16)
                    else:
                        ins.then_inc(cc_sem)
            return f

        block.tensor(body("pe"))
        block.scalar(body("act"))
        block.vector(body("dve"))
        block.gpsimd(body("pool"))
        block.sync(body("sp"))
        self.max_counts = cnt
        return scnt


def build_program(stop_after=None, debug=False, dbg_opts=()):
    nc = bass.Bass("TRN2", target_bir_lowering=False)
    S = Sched(nc)

    def din(name, shape, dt=F32):
        return nc.dram_tensor(name, list(shape), dt, kind="ExternalInput").ap()

    x_in = din("x_fm", [D, T])
    cvec = din("cvec", [16, 128])
    ident_in = din("ident", [128, 128])
    w_ada = din("w_ada", [DEPTH, D, 9 * D])
    b_ada = din("b_ada", [DEPTH, 72, 128])
    g_norm = din("g_norm", [DEPTH, 24, 128])
    w_ff1 = din("w_ff1", [DEPTH, 2, D, DFF])
    w_ff3 = din("w_ff3", [DEPTH, 2, D, DFF])
    w_ff2 = din("w_ff2", [DEPTH, 2, DFF, D])
    g_final = din("g_final", [8, 128])
    w_in = din("w_in", [DEPTH, D, INW])
    w_sguT = din("w_sguT", [DEPTH, 4, 128, 128])
    b_sgu = din("b_sgu", [DEPTH, 512])
    w_conv = din("w_conv", [DEPTH, 12, 128])
    w_branch = din("w_branch", [DEPTH, 4, 512, D])
    w_gate = din("w_gate", [DEPTH, 4, D, D])
    b_gate = din("b_gate", [DEPTH, 32, 128])
    w_out = din("w_out", [DEPTH, D, D])
    selm_in = din("selm", [4, 128, 128], BF16)
    flags_in = din("flags", [128, 8])
    pfm = nc.dram_tensor("pfm", [3616, T], BF16).ap()
    ptm = nc.dram_tensor("ptm", [T, 1024], BF16).ap()
    br = nc.dram_tensor("br", [4, 512, T], BF16).ap()
    fx_in = nc.dram_tensor("fx_in", [2, 256, TL], BF16).ap()
    fx_out = nc.dram_tensor("fx_out", [2, 1024, TL], BF16).ap()
    fy_in = nc.dram_tensor("fy_in", [2, 128, 4096], BF16).ap()
    fy_out = nc.dram_tensor("fy_out", [2, 512, 4096], BF16).ap()
    wa2_in = din("wa2", [DEPTH, 4, 32, 128])
    ba2_in = din("ba2", [DEPTH, 4, 128])
    ggo_in = din("ggo", [DEPTH, 4, 128])
    gmask_in = din("gmask", [128, 256], BF16)
    sca_in = din("sca", [128, 2])
    gx_d = nc.dram_tensor("gx_d", [4, 128, 256], F32).ap()
    gxo_d = nc.dram_tensor("gxo_d", [4, 512, 256], F32).ap()
    fence_in = nc.dram_tensor("fence_in", [128, 64], F32).ap()
    fence_out = nc.dram_tensor("fence_out", [512, 64], F32).ap()
    fc_in = din("fc_tab", [128, 256], BF16)
    fhi_in = din("fhi_tab", [64, 256], BF16)
    m_tab = din("m_tab", [64, 128, 256], BF16)
    f256_in = din("f256_tab", [2, 128, 512], BF16)
    out = nc.dram_tensor("out_fm", [D, TL], F32, kind="ExternalOutput").ap()
    dbg = nc.dram_tensor("dbg", [128, 4096], F32, kind="ExternalOutput").ap() if debug else None
    out_ctx = nc.dram_tensor("out_ctx", [D, CTX], F32, kind="ExternalOutput").ap() if debug else None

    def dump(c0, ap, n, keys, np_=128):
        if debug:
            S.dma("pool", lambda e: e.dma_start(out=dbg[0:np_, c0:c0 + n], in_=ap), keys, ["dbg%d" % c0], stream="d")

    import contextlib
    es = contextlib.ExitStack()

    def sb(name, shape, dt):
        return es.enter_context(nc.sbuf_tensor(name, list(shape), dt))

    def ps(name, shape, dt=F32):
        return es.enter_context(nc.psum_tensor(name, list(shape), dt))

    with es:
        xs = sb("xs", [128, KC, T], F32)
        arena = sb("arena", [128, 32768], F32)
        hall = arena[:, 0:9216].bitcast(BF16).rearrange("p (k t) -> p k t", k=KC)
        bbuf = arena[:, 9216:25600].bitcast(BF16)
        fbuf = arena[:, 25600:31744]
        xbuf = arena[:, 31744:32768]
        hflat = arena[:, 0:9216]
        ident = sb("ident_sb", [128, 128], F32)
        ones_bf = sb("ones_bf", [128, 128], BF16)
        epsc = sb("epsc", [128, 1], F32)
        mods = sb("mods", [128, DEPTH, 72, 2], F32)
        gsh = sb("gsh", [128, DEPTH, 3, 3, KC, 2], F32)
        gn = sb("gn", [128, DEPTH, 24], F32)
        gfin = sb("gfin", [128, 8], F32)
        csil = sb("csil", [128, 16], BF16)
        small = sb("small", [128, 256], F32)
        wcv = sb("wcv", [128, DEPTH, 12], F32)
        bgt = sb("bgt", [128, DEPTH, 32], F32)
        selm = sb("selm_sb", [128, 4, 128], BF16)
        flg = sb("flg", [128, 8], F32)
        nba = sb("nba", [128, DEPTH, 4], F32)
        ggo = sb("ggo_sb", [128, DEPTH, 4], F32)
        sca = sb("sca_sb", [128, 2], F32)
        onec = sb("onec", [128, 1], F32)
        ident_bf = sb("ident_bf", [128, 128], BF16)
        gmask = sb("gmask_sb", [128, 256], BF16)
        banks = [ps("bank%d" % i, [128, 512]) for i in range(8)]

        dma_streams = [("sp", "w"), ("sp", "d"), ("pool", "d"), ("pool", "w"), ("act", "d")]
        dma_sems = {st: [es.enter_context(nc.semaphore("dsem_%s_%s_%d" % (st[0], st[1], i_))) for i_ in range(Sched.KDMA)]
                    for st in dma_streams}
        cc_sem = es.enter_context(nc.semaphore("cc_sem"))

        S.dma("sp", lambda e: e.dma_start(out=ident[:], in_=ident_in[:, :]), [], ["ident"])
        S.c("dve", lambda e: e.memset(ones_bf[:], 1.0), [], ["ones"])
        S.c("dve", lambda e: e.memset(epsc[:], EPS), [], ["epsc"])
        S.dma("sp", lambda e: e.dma_start(
            out=xs[:], in_=x_in.rearrange("(kc p) t -> p kc t", p=128)), [], [("xs", t0_, k_) for (t0_, _w) in TILES for k_ in range(KC)])

        def load_cols(src_ap, rows, dst_ap, key):
            stg = fbuf[0:rows, 0:128]
            S.dma("sp", lambda e: e.dma_start(out=stg, in_=src_ap), ["stg_free"], ["stg"])
            S.c("pe", lambda e: e.transpose(out=banks[7][:, 0:rows], in_=stg, identity=ident[0:rows, 0:rows]),
                ["stg", "ident"], ["bank7"])
            S.c("dve", lambda e: e.tensor_copy(out=dst_ap, in_=banks[7][:, 0:rows]), ["bank7"], [key, "stg_free"])

        S.dma("sp", lambda e: e.dma_start(out=selm[:], in_=selm_in.rearrange("g p m -> p g m")), [], ["selm"])
        S.dma("sp", lambda e: e.dma_start(out=flg[:], in_=flags_in[:, :]), [], ["flg"])
        S.dma("sp", lambda e: e.dma_start(out=sca[:], in_=sca_in[:, :]), [], ["sca"])
        S.dma("sp", lambda e: e.dma_start(out=gmask[:], in_=gmask_in[:, :]), [], ["gmask"])
        S.c("dve", lambda e: e.memset(onec[:], 1.0), [], ["onec"])
        S.c("dve", lambda e: e.tensor_copy(out=ident_bf[:], in_=ident[:]), ["ident"], ["ident_bf"])
        load_cols(cvec[:, :], 16, small[:, 0:16], "cv")
        S.c("act", lambda e: e.activation(out=csil[:].rearrange("p (k j) -> p j k", j=2), in_=small[:, 0:16].rearrange("p (j k) -> p j k", j=2), func=AF.Silu), ["cv"], ["csil"])
        load_cols(g_final[:, :], 8, gfin[:], "gfin")
        dump(3072, small[:, 0:16], 16, ["cv"])
        dump(3088, csil[:, 0:16], 16, ["csil"])
        dump(3104, gfin[:, 0:8], 8, ["gfin"])
        def setup_layer(l):
            load_cols(g_norm[l], 24, gn[:, l, :], "gn%d" % l)
            load_cols(w_conv[l], 12, wcv[:, l, :], "wcv%d" % l)
            load_cols(ba2_in[l], 4, nba[:, l, :], "nba%d" % l)
            S.c("dve", lambda e: e.tensor_scalar(out=nba[:, l, :], in0=nba[:, l, :], scalar1=-1.0, scalar2=None, op0=ALU.mult),
                ["nba%d" % l], ["nba%d" % l])
            load_cols(ggo_in[l], 4, ggo[:, l, :], "ggo%d" % l)
            load_cols(b_gate[l], 32, bgt[:, l, :], "bgt%d" % l)
            load_cols(b_ada[l], 72, small[:, 16:88], "bada")
            NB = 1152
            for blk in range(8):
                wv = bbuf[:, 0:KC * NB].rearrange("p (k n) -> p k n", k=KC)
                for kc in range(KC):
                    S.dma("pool", lambda e, blk=blk, wv=wv, kc=kc: e.dma_start(
                        out=wv[:, kc, :], in_=w_ada[l, kc * 128:(kc + 1) * 128, blk * NB:(blk + 1) * NB]),
                        [], ["wada%d" % kc], stream="w")
                for nch in range(9):
                    ncol = blk * 9 + nch
                    for kc in range(KC):
                        S.c("pe", lambda e, nch=nch, kc=kc, ncol=ncol, wv=wv: e.matmul(
                            banks[6][:, 2 * ncol:2 * ncol + 2], lhsT=wv[:, kc, nch * 128:(nch + 1) * 128],
                            rhs=csil[:, 2 * kc:2 * kc + 2], start=(kc == 0), stop=(kc == KC - 1)),
                            ["wada%d" % kc, "csil"], ["bank6"], pe_acc=True)
            if l == 0:
                dump(3500, wv[:, 0, 0:512], 512, ["wada0"])
                S.c("dve", lambda e: e.tensor_copy(out=fbuf[:, 4096:4240], in_=banks[6][:, 0:144]), ["bank6"], ["b6copy"])
                dump(3200, fbuf[:, 4096:4240], 144, ["b6copy"])
                dump(3400, small[:, 16:88], 72, ["bada"])
            for j in range(2):
                S.c("dve", lambda e, l=l, j=j: e.tensor_tensor(
                    out=mods[:, l, :, j], in0=banks[6][:, j:144:2], in1=small[:, 16:88], op=ALU.add),
                    ["bank6", "bada"], ["mods%d" % l])
            for sub in range(3):
                for j in range(2):
                    sc = mods[:, l, (3 * sub + 1) * 8:(3 * sub + 2) * 8, j]
                    S.c("dve", lambda e, l=l, sub=sub, j=j, sc=sc: e.scalar_tensor_tensor(
                        out=gsh[:, l, sub, 0, :, j], in0=sc, scalar=1.0, in1=gn[:, l, sub * 8:(sub + 1) * 8],
                        op0=ALU.add, op1=ALU.mult), ["mods%d" % l, "gn%d" % l], ["gsh%d" % l])
                    S.c("dve", lambda e, l=l, sub=sub, j=j: e.tensor_copy(
                        out=gsh[:, l, sub, 1, :, j], in_=mods[:, l, (3 * sub) * 8:(3 * sub + 1) * 8, j]),
                        ["mods%d" % l], ["gsh%d" % l])
                    S.c("dve", lambda e, l=l, sub=sub, j=j: e.tensor_scalar(
                        out=gsh[:, l, sub, 2, :, j], in0=mods[:, l, (3 * sub + 2) * 8:(3 * sub + 3) * 8, j],
                        scalar1=(1.0 if sub == 1 else 0.5), scalar2=None, op0=ALU.mult),
                        ["mods%d" % l], ["gsh%d" % l])
        for l_ in range(DEPTH):
            setup_layer(l_)
        S.barrier()

        def norm_to_hall(l, sub, tiles, tag):
            for ti, (t0, w) in enumerate(tiles):
                j = 1 if t0 < CTX else 0
                sq = bbuf[:, 24576:24576 + KC * 512].rearrange("p (k t) -> p k t", k=KC)[:, :, 0:w]
                S.c("act", lambda e, t0=t0, w=w, sq=sq: e.activation(out=sq, in_=xs[:, :, t0:t0 + w], func=AF.Square),
                    [("xs", t0, k_) for k_ in range(KC)], ["sq"])
                for kc in range(KC):
                    S.c("pe", lambda e, kc=kc, w=w, sq=sq: e.matmul(
                        banks[7][:, 0:w], lhsT=ones_bf[:], rhs=sq[:, kc, :], start=(kc == 0), stop=(kc == KC - 1)),
                        ["sq", "ones"], ["bank7"], pe_acc=True)
                rs = fbuf[:, 0:w]
                S.c("act", lambda e, w=w, rs=rs: e.activation(out=rs, in_=banks[7][:, 0:w], func=AF.Sqrt,
                                                             scale=1.0 / D, bias=epsc[:]), ["bank7", "epsc"], ["rs"])
                S.c("dve", lambda e, rs=rs: e.reciprocal(out=rs, in_=rs), ["rs"], ["rs"])
                for kc in range(KC):
                    tmp = fbuf[:, 512 + (kc % 2) * 512:512 + (kc % 2) * 512 + w]
                    S.c("dve", lambda e, kc=kc, t0=t0, w=w, tmp=tmp, rs=rs: e.tensor_tensor(
                        out=tmp, in0=xs[:, kc, t0:t0 + w], in1=rs, op=ALU.mult), [("xs", t0, kc), "rs"], ["ntmp%d" % (kc % 2)])
                    S.c("act", lambda e, kc=kc, t0=t0, w=w, tmp=tmp, j=j: e.activation(
                        out=hall[:, kc, t0:t0 + w], in_=tmp, func=AF.Identity,
                        scale=gsh[:, l, sub, 0, kc, j:j + 1], bias=gsh[:, l, sub, 1, kc, j:j + 1]),
                        ["ntmp%d" % (kc % 2), "gsh%d" % l], ["hall%d" % ti])

        def ffn(l, i, tiles):
            sub = 0 if i == 0 else 2
            norm_to_hall(l, sub, tiles, "ffn")
            if False:
                dump(0, hall[:, 0, 256:768], 512, ["hall1"])
                dump(1024, gsh[:, 0, 2, :, :, :], 48, ["gsh0"])
                S.c("dve", lambda e: e.tensor_copy(out=fbuf[:, 4096:4608], in_=xs[:, 0, 256:768]), [("xs", 256, 0)], ["xdump"])
                dump(3072, fbuf[:, 4096:4608], 512, ["xdump"])
            fgroups = [(0, 4), (4, 4), (8, 4), (12, 4), (16, 4), (20, 2)]
            W1o, W3o, W2o = 0, 4096, 8192
            for gi, (f0, nf) in enumerate(fgroups):
                par = gi % 2
                base = par * 12288
                w1 = bbuf[:, base + W1o:base + W1o + KC * 512].rearrange("p (k f) -> p k f", k=KC)
                w3 = bbuf[:, base + W3o:base + W3o + KC * 512].rearrange("p (k f) -> p k f", k=KC)
                w2 = bbuf[:, base + W2o:base + W2o + 4 * 1024].rearrange("p (f d) -> p f d", f=4)
                wk = "ffw%d" % par
                nfc = nf * 128
                S.dma("pool", lambda e, w1=w1, f0=f0, nfc=nfc: e.dma_start(
                    out=w1[:, :, 0:nfc], in_=w_ff1[l, i].rearrange("(kc p) f -> p kc f", p=128)[:, :, f0 * 128:f0 * 128 + nfc]),
                    [], [wk + "a"], stream="w")
                S.dma("pool", lambda e, w3=w3, f0=f0, nfc=nfc: e.dma_start(
                    out=w3[:, :, 0:nfc], in_=w_ff3[l, i].rearrange("(kc p) f -> p kc f", p=128)[:, :, f0 * 128:f0 * 128 + nfc]),
                    [], [wk + "b"], stream="w")
                S.dma("pool", lambda e, w2=w2, f0=f0, nf=nf: e.dma_start(
                    out=w2[:, 0:nf, :], in_=w_ff2[l, i][f0 * 128:(f0 + nf) * 128, :].rearrange("(fc p) d -> p fc d", p=128)),
                    [], [wk + "c"], stream="w")
                for ti, (t0, w) in enumerate(tiles):
                    j = 1 if t0 < CTX else 0
                    gk = "gt%d" % (ti % 2)
                    gt = bbuf[:, 28672 + (ti % 2) * 2048:28672 + (ti % 2) * 2048 + 2048].rearrange(
                        "p (f t) -> p f t", f=4)
                    for fc in range(nf):
                        pp = (fc % 2)
                        b1, b3 = banks[pp * 2], banks[pp * 2 + 1]
                        for kc in range(KC):
                            S.c("pe", lambda e, kc=kc, fc=fc, b1=b1, w1=w1, t0=t0, w=w: e.matmul(
                                b1[:, 0:w], lhsT=w1[:, kc, fc * 128:(fc + 1) * 128], rhs=hall[:, kc, t0:t0 + w],
                                start=(kc == 0), stop=(kc == KC - 1)), [wk + "a", "hall%d" % ti], ["bank%d" % (pp * 2)], pe_acc=True)
                        for kc in range(KC):
                            S.c("pe", lambda e, kc=kc, fc=fc, b3=b3, w3=w3, t0=t0, w=w: e.matmul(
                                b3[:, 0:w], lhsT=w3[:, kc, fc * 128:(fc + 1) * 128], rhs=hall[:, kc, t0:t0 + w],
                                start=(kc == 0), stop=(kc == KC - 1)), [wk + "b", "hall%d" % ti], ["bank%d" % (pp * 2 + 1)], pe_acc=True)
                        st = fbuf[:, 2048 + pp * 512:2048 + pp * 512 + w]
                        S.c("act", lambda e, b1=b1, st=st, w=w: e.activation(out=st, in_=b1[:, 0:w], func=AF.Silu),
                            ["bank%d" % (pp * 2)], ["silu%d" % pp])
                        S.c("dve", lambda e, b3=b3, st=st, w=w, gt=gt, fc=fc: e.tensor_tensor(
                            out=gt[:, fc, 0:w], in0=b3[:, 0:w], in1=st, op=ALU.mult),
                            ["bank%d" % (pp * 2 + 1), "silu%d" % pp], [gk])
                    if False:
                        dump(512, w1[:, 0, :], 512, [wk + "a"])
                        dump(1536, w2[:, 0, 0:512], 512, [wk + "c"])
                        dump(2048, gt[:, 0, :], 512, [gk])
                        dump(2560, w3[:, 7, :], 512, [wk + "b"])
                    for dc in range(KC):
                        pb = banks[4 + dc % 2]
                        for fc in range(nf):
                            S.c("pe", lambda e, dc=dc, fc=fc, pb=pb, w2=w2, gt=gt, w=w: e.matmul(
                                pb[:, 0:w], lhsT=w2[:, fc, dc * 128:(dc + 1) * 128], rhs=gt[:, fc, 0:w],
                                start=(fc == 0), stop=(fc == nf - 1)), [wk + "c", gk], ["bank%d" % (4 + dc % 2)], pe_acc=True)
                        S.c("dve", lambda e, dc=dc, pb=pb, t0=t0, w=w, j=j: e.scalar_tensor_tensor(
                            out=xs[:, dc, t0:t0 + w], in0=pb[:, 0:w], scalar=gsh[:, l, sub, 2, dc, j:j + 1],
                            in1=xs[:, dc, t0:t0 + w], op0=ALU.mult, op1=ALU.add),
                            ["bank%d" % (4 + dc % 2), ("xs", t0, dc), "gsh%d" % l], [("xs", t0, dc)])
            S.barrier()


        PF = {"q": 0, "k": 256, "r": 512, "a": 1024, "su": 1056, "fn": 1568, "cb": 2080, "cc": 2592, "cx": 3104}

        def evac(idx, out_ap, in_ap, reads, writes):
            if idx % 2 == 0:
                S.c("act", lambda e: e.activation(out=out_ap, in_=in_ap, func=AF.Copy), reads, writes)
            else:
                S.c("dve", lambda e: e.tensor_copy(out=out_ap, in_=in_ap), reads, writes)

        def win_proj(l, tiles):
            blocks = [(0, 512, "fm", 0), (512, 512, "tm", 0), (1024, 512, "fm", 512), (1536, 32, "fm", 1024),
                      (1568, 512, "fm", 1056), (2080, 512, "tm", 512), (2592, 512, "fm", 1568),
                      (3104, 512, "fm", 2080), (3616, 512, "fm", 2592), (4128, 512, "fm", 3104)]
            cnt = [0]
            for bi, (c0, ncol, kind, dst) in enumerate(blocks):
                par = bi % 2
                wv = bbuf[:, par * 12288: par * 12288 + 4096].rearrange("p (k n) -> p k n", k=KC)
                wk = "winw%d" % par
                S.dma("pool", lambda e, wv=wv, c0=c0, ncol=ncol: e.dma_start(
                    out=wv[:, :, 0:ncol], in_=w_in[l].rearrange("(kc p) n -> p kc n", p=128)[:, :, c0:c0 + ncol]),
                    [], [wk], stream="w")
                for ti, (t0, w) in enumerate(tiles):
                    if kind == "fm":
                        for ch in range((ncol + 127) // 128):
                            m = min(128, ncol - ch * 128)
                            i4 = cnt[0] % 4
                            cnt[0] += 1
                            bk = banks[i4]
                            for kc in range(KC):
                                S.c("pe", lambda e, bk=bk, m=m, w=w, wv=wv, kc=kc, ch=ch, t0=t0: e.matmul(
                                    bk[0:m, 0:w], lhsT=wv[:, kc, ch * 128:ch * 128 + m], rhs=hall[:, kc, t0:t0 + w],
                                    start=(kc == 0), stop=(kc == KC - 1)), [wk, "hall%d" % ti], ["bank%d" % i4], pe_acc=True)
                            stg = bbuf[:, 24576 + i4 * 512:24576 + i4 * 512 + w]
                            evac(cnt[0], stg[0:m, :], bk[0:m, 0:w], ["bank%d" % i4], ["wstg%d" % i4])
                            if c0 == 2592 and t0 >= CTX:
                                dst_ap = fx_in[ch // 2, (ch % 2) * 128:(ch % 2) * 128 + m, t0 - CTX:t0 - CTX + w]
                                dkey = ("fx_in", ch // 2)
                            else:
                                dst_ap = pfm[dst + ch * 128:dst + ch * 128 + m, t0:t0 + w]
                                dkey = ("pfm", dst + ch * 128, t0)
                            S.dma("sp", lambda e, dst_ap=dst_ap, stg=stg, m=m: e.dma_start(out=dst_ap, in_=stg[0:m, :]),
                                  ["wstg%d" % i4], [dkey])
                    else:
                        for sidx in range(w // 128):
                            i4 = cnt[0] % 4
                            cnt[0] += 1
                            bk = banks[i4]
                            ts = t0 + sidx * 128
                            for kc in range(KC):
                                S.c("pe", lambda e, bk=bk, wv=wv, kc=kc, ts=ts: e.matmul(
                                    bk[:, 0:512], lhsT=hall[:, kc, ts:ts + 128], rhs=wv[:, kc, 0:512],
                                    start=(kc == 0), stop=(kc == KC - 1)), [wk, "hall%d" % ti], ["bank%d" % i4], pe_acc=True)
                            stg = bbuf[:, 24576 + i4 * 512:24576 + i4 * 512 + 512]
                            evac(cnt[0], stg, bk[:, 0:512], ["bank%d" % i4], ["wstg%d" % i4])
                            S.dma("sp", lambda e, stg=stg, ts=ts, dst=dst: e.dma_start(
                                out=ptm[ts:ts + 128, dst:dst + 512], in_=stg), ["wstg%d" % i4], [("ptm", dst, ts)])
            S.barrier()

        def conv_branch(l, tiles):
            for ti, (t0, w) in enumerate(tiles):
                rl = 256 if t0 < CTX else 64
                for ch in range(4):
                    it = ti * 4 + ch
                    p2 = it % 2
                    inb = bbuf[:, p2 * 2048:p2 * 2048 + 3 * 512].rearrange("p (s t) -> p s t", s=3)
                    S.dma("sp", lambda e, inb=inb, ch=ch, t0=t0, w=w: e.dma_start(
                        out=inb[:, :, 0:w],
                        in_=pfm[2080:3616, :].rearrange("(s g p) t -> p s g t", s=3, g=4)[:, :, ch, t0:t0 + w]),
                        [("pfm", 2080 + s_ * 512 + ch * 128, t0) for s_ in range(3)], ["cvin%d" % p2])
                    z = fbuf[:, p2 * 1024:p2 * 1024 + w]
                    y = fbuf[:, p2 * 1024 + 512:p2 * 1024 + 512 + w]
                    z3 = z.rearrange("p (r c) -> p r c", c=rl)
                    y3 = y.rearrange("p (r c) -> p r c", c=rl)
                    S.c("dve", lambda e, z=z, inb=inb, w=w: e.tensor_tensor(out=z, in0=inb[:, 1, 0:w], in1=inb[:, 2, 0:w], op=ALU.mult),
                        ["cvin%d" % p2], ["cvz%d" % p2])
                    S.c("act", lambda e, y=y, z=z, ch=ch: e.activation(out=y, in_=z, func=AF.Identity, scale=wcv[:, l, 4 + ch:5 + ch]),
                        ["cvz%d" % p2, "wcv%d" % l], ["cvy%d" % p2])
                    S.c("dve", lambda e, y3=y3, z3=z3, ch=ch, rl=rl: e.scalar_tensor_tensor(
                        out=y3[:, :, 1:rl], in0=z3[:, :, 0:rl - 1], scalar=wcv[:, l, ch:ch + 1], in1=y3[:, :, 1:rl],
                        op0=ALU.mult, op1=ALU.add), ["cvz%d" % p2, "cvy%d" % p2, "wcv%d" % l], ["cvy%d" % p2])
                    S.c("dve", lambda e, y3=y3, z3=z3, ch=ch, rl=rl: e.scalar_tensor_tensor(
                        out=y3[:, :, 0:rl - 1], in0=z3[:, :, 1:rl], scalar=wcv[:, l, 8 + ch:9 + ch], in1=y3[:, :, 0:rl - 1],
                        op0=ALU.mult, op1=ALU.add), ["cvz%d" % p2, "cvy%d" % p2, "wcv%d" % l], ["cvy%d" % p2])
                    ob = bbuf[:, 4096 + p2 * 512:4096 + p2 * 512 + w]
                    S.c("dve", lambda e, ob=ob, inb=inb, y=y, w=w: e.tensor_tensor(out=ob, in0=inb[:, 0, 0:w], in1=y, op=ALU.mult),
                        ["cvin%d" % p2, "cvy%d" % p2], ["cvo%d" % p2])
                    S.dma("sp", lambda e, ob=ob, ch=ch, t0=t0, w=w: e.dma_start(
                        out=br[3, ch * 128:(ch + 1) * 128, t0:t0 + w], in_=ob), ["cvo%d" % p2], [("br", 3, ch, t0)])
            S.barrier()

        def sgu_branch(l, tiles):
            wst = bbuf[:, 8192:8192 + 512].rearrange("p (g i) -> p g i", g=4)
            S.dma("pool", lambda e: e.dma_start(out=wst, in_=w_sguT[l].rearrange("g j i -> j g i")), [], ["wst"], stream="w")
            bbc = fbuf[:, 4096:4608]
            S.dma("sp", lambda e: e.dma_start(out=bbc, in_=b_sgu[l].partition_broadcast(128)), [], ["bbc"])
            ci = 0
            for ti, (t0, w) in enumerate(tiles):
                for sidx in range(w // 128):
                    ts = t0 + sidx * 128
                    p2 = ci % 2
                    ci += 1
                    svt = bbuf[:, 9216 + p2 * 512:9216 + p2 * 512 + 512]
                    sut = bbuf[:, 10240 + p2 * 512:10240 + p2 * 512 + 512].rearrange("p (g t) -> p g t", g=4)
                    S.dma("sp", lambda e, svt=svt, ts=ts: e.dma_start(out=svt, in_=ptm[ts:ts + 128, 512:1024]),
                          [("ptm", 512, ts)], ["svt%d" % p2])
                    S.dma("sp", lambda e, sut=sut, ts=ts: e.dma_start(
                        out=sut, in_=pfm[1056:1568, ts:ts + 128].rearrange("(g c) t -> c g t", c=128)),
                        [("pfm", 1056 + g_ * 128, t0) for g_ in range(4)], ["sut%d" % p2])
                    st = xbuf[:, p2 * 32:p2 * 32 + 32]
                    sq = fbuf[:, p2 * 512:p2 * 512 + 512]
                    sv3 = svt.rearrange("p (g c) -> p g c", g=4)
                    S.c("dve", lambda e, st=st, sv3=sv3: e.tensor_reduce(out=st[:, 0:4], in_=sv3, axis=AX.X, op=ALU.add),
                        ["svt%d" % p2], ["sgst%d" % p2])
                    S.c("act", lambda e, sq=sq, svt=svt: e.activation(out=sq, in_=svt, func=AF.Square), ["svt%d" % p2], ["sgsq%d" % p2])
                    S.c("dve", lambda e, st=st, sq=sq: e.tensor_reduce(out=st[:, 4:8], in_=sq.rearrange("p (g c) -> p g c", g=4),
                                                                     axis=AX.X, op=ALU.add), ["sgsq%d" % p2, "sgst%d" % p2], ["sgst%d" % p2])
                    S.c("dve", lambda e, st=st: e.tensor_scalar(out=st[:, 8:12], in0=st[:, 0:4], scalar1=1.0 / 128, scalar2=None, op0=ALU.mult),
                        ["sgst%d" % p2], ["sgst%d" % p2])
                    S.c("dve", lambda e, st=st: e.tensor_tensor(out=st[:, 16:20], in0=st[:, 8:12], in1=st[:, 8:12], op=ALU.mult),
                        ["sgst%d" % p2], ["sgst%d" % p2])
                    S.c("dve", lambda e, st=st: e.scalar_tensor_tensor(out=st[:, 12:16], in0=st[:, 4:8], scalar=1.0 / 128, in1=st[:, 16:20],
                                                                       op0=ALU.mult, op1=ALU.subtract), ["sgst%d" % p2], ["sgst%d" % p2])
                    S.c("act", lambda e, st=st: e.activation(out=st[:, 12:16], in_=st[:, 12:16], func=AF.Sqrt, scale=1.0, bias=epsc[:]),
                        ["sgst%d" % p2, "epsc"], ["sgst%d" % p2])
                    S.c("dve", lambda e, st=st: e.reciprocal(out=st[:, 12:16], in_=st[:, 12:16]), ["sgst%d" % p2], ["sgst%d" % p2])
                    zt = bbuf[:, 11264 + p2 * 512:11264 + p2 * 512 + 512]
                    for g in range(4):
                        S.c("dve", lambda e, zt=zt, svt=svt, st=st, g=g: e.tensor_scalar(
                            out=zt[:, g * 128:(g + 1) * 128], in0=svt[:, g * 128:(g + 1) * 128], scalar1=st[:, 8 + g:9 + g],
                            scalar2=st[:, 12 + g:13 + g], op0=ALU.subtract, op1=ALU.mult), ["svt%d" % p2, "sgst%d" % p2], ["sgz%d" % p2])
                    bk = banks[p2]
                    for g in range(4):
                        S.c("pe", lambda e, bk=bk, zt=zt, g=g: e.matmul(
                            bk[:, g * 128:(g + 1) * 128], lhsT=zt[:, g * 128:(g + 1) * 128], rhs=wst[:, g, :], start=True, stop=True),
                            ["sgz%d" % p2, "wst"], ["bank%d" % p2])
                    tmp = fbuf[:, 1024 + p2 * 512:1024 + p2 * 512 + 512]
                    S.c("dve", lambda e, tmp=tmp, bk=bk: e.tensor_tensor(out=tmp, in0=bk[:, 0:512], in1=bbc, op=ALU.add),
                        ["bank%d" % p2, "bbc"], ["sgt%d" % p2])
                    ob = bbuf[:, 12288 + p2 * 512:12288 + p2 * 512 + 512].rearrange("p (g t) -> p g t", g=4)
                    S.c("pool", lambda e, ob=ob, tmp=tmp, sut=sut: e.tensor_tensor(
                        out=ob, in0=tmp.rearrange("p (g t) -> p g t", g=4), in1=sut, op=ALU.mult), ["sgt%d" % p2, "sut%d" % p2], ["sgo%d" % p2])
                    S.dma("sp", lambda e, ob=ob, ts=ts: e.dma_start(
                        out=br[1, :, ts:ts + 128].rearrange("(g c) t -> c g t", c=128), in_=ob), ["sgo%d" % p2],
                        [("br", 1, g_, t0) for g_ in range(4)] if sidx == w // 128 - 1 else [("brpart", ts)])
            S.barrier()

        def zero_branch(k, tiles):
            zt = bbuf[:, 0:512]
            S.c("dve", lambda e: e.memset(zt, 0.0), [], ["zt"])
            for ti, (t0, w) in enumerate(tiles):
                for ch in range(4):
                    S.dma("sp", lambda e, ch=ch, t0=t0, w=w: e.dma_start(out=br[k, ch * 128:(ch + 1) * 128, t0:t0 + w], in_=zt[:, 0:w]),
                          ["zt"], [("br", k, ch, t0)])
            S.barrier()

        def merge(l, tiles):
            norm_to_hall(l, 1, tiles, "mix")
            S.barrier()
            mall = bbuf[:, 0:18432].rearrange("p (k t) -> p k t", k=KC)
            for dc in range(KC):
                par = dc % 2
                wg = bbuf[:, 18432 + par * 6144:18432 + par * 6144 + 4096].rearrange("p (k c n) -> p k c n", k=4, c=KC)
                wb = bbuf[:, 18432 + par * 6144 + 4096:18432 + par * 6144 + 6144].rearrange("p (k c n) -> p k c n", k=4, c=4)
                for k in range(4):
                    S.dma("pool", lambda e, wg=wg, k=k, dc=dc: e.dma_start(
                        out=wg[:, k, :, :], in_=w_gate[l, k].rearrange("(kc p) n -> p kc n", p=128)[:, :, dc * 128:(dc + 1) * 128]),
                        [], ["wg%d_%d" % (par, k)], stream="w")
                    S.dma("pool", lambda e, wb=wb, k=k, dc=dc: e.dma_start(
                        out=wb[:, k, :, :], in_=w_branch[l, k].rearrange("(cc p) n -> p cc n", p=128)[:, :, dc * 128:(dc + 1) * 128]),
                        [], ["wb%d_%d" % (par, k)], stream="w")
                for ti, (t0, w) in enumerate(tiles):
                    it = dc * len(tiles) + ti
                    p2 = it % 2
                    brt = fbuf[:, 0:4096].bitcast(BF16).rearrange("p (k c t) -> p k c t", k=4, c=4)
                    for k in range(4):
                        S.dma("sp", lambda e, brt=brt, k=k, t0=t0, w=w: e.dma_start(
                            out=brt[:, k, :, 0:w], in_=br[k, :, t0:t0 + w].rearrange("(c p) t -> p c t", p=128)),
                            [("br", k, c_, t0) for c_ in range(4)], ["brt_%d" % k])
                    macc = fbuf[:, 4096 + p2 * 512:4096 + p2 * 512 + w]
                    for k in range(4):
                        pg, pb = banks[(k % 2) * 2], banks[(k % 2) * 2 + 1]
                        for kc in range(KC):
                            S.c("pe", lambda e, pg=pg, wg=wg, k=k, kc=kc, t0=t0, w=w: e.matmul(
                                pg[:, 0:w], lhsT=wg[:, k, kc, :], rhs=hall[:, kc, t0:t0 + w], start=(kc == 0), stop=(kc == KC - 1)),
                                ["wg%d_%d" % (par, k), "hall%d" % ti], ["bank%d" % ((k % 2) * 2)], pe_acc=True)
                        for cc in range(4):
                            S.c("pe", lambda e, pb=pb, wb=wb, k=k, cc=cc, brt=brt, w=w: e.matmul(
                                pb[:, 0:w], lhsT=wb[:, k, cc, :], rhs=brt[:, k, cc, 0:w], start=(cc == 0), stop=(cc == 3)),
                                ["wb%d_%d" % (par, k), "brt_%d" % k], ["bank%d" % ((k % 2) * 2 + 1)], pe_acc=True)
                        sg = fbuf[:, 5120 + (k % 2) * 512:5120 + (k % 2) * 512 + w]
                        S.c("act", lambda e, sg=sg, pg=pg, k=k, dc=dc, w=w: e.activation(
                            out=sg, in_=pg[:, 0:w], func=AF.Sigmoid, bias=bgt[:, l, k * 8 + dc:k * 8 + dc + 1], scale=1.0),
                            ["bank%d" % ((k % 2) * 2), "bgt%d" % l], ["msg%d" % (k % 2)])
                        if k == 0:
                            S.c("dve", lambda e, macc=macc, pb=pb, sg=sg, w=w: e.tensor_tensor(out=macc, in0=pb[:, 0:w], in1=sg, op=ALU.mult),
                                ["bank%d" % ((k % 2) * 2 + 1), "msg%d" % (k % 2)], ["macc%d" % p2])
                        else:
                            S.c("dve", lambda e, sg=sg, pb=pb, w=w: e.tensor_tensor(out=sg, in0=pb[:, 0:w], in1=sg, op=ALU.mult),
                                ["bank%d" % ((k % 2) * 2 + 1), "msg%d" % (k % 2)], ["msg%d" % (k % 2)])
                            if k < 3:
                                S.c("pool", lambda e, macc=macc, sg=sg: e.tensor_tensor(out=macc, in0=macc, in1=sg, op=ALU.add),
                                    ["macc%d" % p2, "msg%d" % (k % 2)], ["macc%d" % p2])
                            else:
                                S.c("pool", lambda e, macc=macc, sg=sg, dc=dc, t0=t0, w=w: e.tensor_tensor(
                                    out=mall[:, dc, t0:t0 + w], in0=macc, in1=sg, op=ALU.add),
                                    ["macc%d" % p2, "msg%d" % (k % 2)], [("mall", ti)])
            S.barrier()
            wo = bbuf[:, 18432:18432 + 8192].rearrange("p (k n) -> p k n", k=KC)
            S.dma("pool", lambda e: e.dma_start(out=wo, in_=w_out[l].rearrange("(kc p) n -> p kc n", p=128)), [], ["wo"], stream="w")
            for ti, (t0, w) in enumerate(tiles):
                j = 1 if t0 < CTX else 0
                for dc in range(KC):
                    pb = banks[4 + dc % 2]
                    for kc in range(KC):
                        S.c("pe", lambda e, pb=pb, kc=kc, dc=dc, t0=t0, w=w: e.matmul(
                            pb[:, 0:w], lhsT=wo[:, kc, dc * 128:(dc + 1) * 128], rhs=mall[:, kc, t0:t0 + w],
                            start=(kc == 0), stop=(kc == KC - 1)), ["wo", ("mall", ti)], ["bank%d" % (4 + dc % 2)], pe_acc=True)
                    S.c("dve", lambda e, dc=dc, pb=pb, t0=t0, w=w, j=j: e.scalar_tensor_tensor(
                        out=xs[:, dc, t0:t0 + w], in0=pb[:, 0:w], scalar=gsh[:, l, 1, 2, dc, j:j + 1],
                        in1=xs[:, dc, t0:t0 + w], op0=ALU.mult, op1=ALU.add),
                        ["bank%d" % (4 + dc % 2), ("xs", t0, dc), "gsh%d" % l], [("xs", t0, dc)])
            S.barrier()


        def ag(src2d, dst2d, rkeys, wkeys):
            S.cc(lambda e: e.collective_compute("AllGather", ALU.bypass, replica_groups=GROUPS4,
                                                ins=[src2d], outs=[dst2d]), rkeys, wkeys)

        def ag_fence(keys):
            S.cc(lambda e: e.collective_compute("AllGather", ALU.bypass, replica_groups=GROUPS4,
                                                ins=[fence_in[:, :]], outs=[fence_out[:, :]]), keys, keys)

        def fnet_exchange_in():
            for q in range(2):
                ag(fx_in[q], fx_out[q], [("fx_in", q)], [("fx_out", q)])

        def fnet_branch(l, with_ctx):
            xsel = arena[:, 0:4096].bitcast(BF16)
            zsb = arena[:, 4096:20480].bitcast(BF16)
            z3 = zsb.rearrange("p (n k) -> p n k", k=256)
            bsb = arena[:, 20480:28672].bitcast(BF16)
            b3 = bsb.rearrange("p (k m) -> p k m", m=128)
            ysb = xsel
            misc = arena[:, 28672:32768].bitcast(BF16)
            fc = misc[:, 0:256]
            fhi = misc[0:64, 256:512]
            f256 = misc[:, 512:1536].rearrange("p (b m) -> p b m", b=2)
            S.dma("sp", lambda e: e.dma_start(out=fc, in_=fc_in[:, :]), [], ["fc"])
            S.dma("sp", lambda e: e.dma_start(out=fhi, in_=fhi_in[:, :]), [], ["fhi"])
            S.dma("sp", lambda e: e.dma_start(out=f256, in_=f256_in.rearrange("b p m -> p b m")), [], ["f256"])
            ev = [0]
            for r in range(4):
                for sx in range(4):
                    it = r * 4 + sx
                    p2 = it % 2
                    xin = misc[:, 1536 + p2 * 2048:1536 + p2 * 2048 + 2048].rearrange("p (g t) -> p g t", g=4)
                    for q in range(2):
                        S.dma("pool", lambda e, xin=xin, q=q, r=r, sx=sx: e.dma_start(
                            out=xin[:, 2 * q:2 * q + 2, :],
                            in_=fx_out[q, r * 256:(r + 1) * 256, sx * 512:(sx + 1) * 512].rearrange("(h c) t -> c h t", c=128)),
                            [("fx_out", 0), ("fx_out", 1)], ["xin%d_%d" % (p2, q)])
                    bk = banks[p2]
                    for g in range(4):
                        S.c("pe", lambda e, bk=bk, xin=xin, g=g: e.matmul(
                            bk[:, 0:512], lhsT=selm[:, g, :], rhs=xin[:, g, :], start=(g == 0), stop=(g == 3)),
                            ["selm", "xin%d_%d" % (p2, g // 2)], ["bank%d" % p2], pe_acc=True)
                    n0 = 2048 * r + 512 * sx
                    ev[0] += 1
                    evac(ev[0], xsel[:, n0:n0 + 512], bk[:, 0:512], ["bank%d" % p2], ["xsel"])
            S.barrier()
            if False:
                dump(0, xsel[:, 0:512], 512, ["xsel"])
            for pr in range(64):
                p2 = pr % 2
                bk = banks[2 + p2]
                for hh in range(2):
                    nlo = 2 * pr + hh
                    S.c("pe", lambda e, bk=bk, hh=hh, nlo=nlo: e.matmul(
                        bk[0:64, hh * 256:(hh + 1) * 256], lhsT=xsel[:, nlo:8192:128], rhs=fc, start=True, stop=True),
                        ["xsel", "fc"], ["bank%d" % (2 + p2)])
                ev[0] += 1
                evac(ev[0], zsb[0:64, pr * 512:(pr + 1) * 512], bk[0:64, 0:512], ["bank%d" % (2 + p2)], ["zsb"])
            S.barrier()
            if False:
                dump(512, zsb[0:64, 0:512], 512, ["zsb"], 64)
            for grp in range(32):
                p2 = grp % 2
                bk = banks[4 + p2]
                for i4 in range(4):
                    k2c = grp * 4 + i4
                    S.c("pe", lambda e, bk=bk, i4=i4, k2c=k2c: e.matmul(
                        bk[:, i4 * 128:(i4 + 1) * 128], lhsT=z3[0:64, :, k2c], rhs=fhi[:, 0:128], start=True, stop=False),
                        ["zsb", "fhi"], ["bank%d" % (4 + p2)], pe_acc=True)
                    S.c("pe", lambda e, bk=bk, i4=i4, k2c=k2c: e.matmul(
                        bk[:, i4 * 128:(i4 + 1) * 128], lhsT=z3[0:64, :, 128 + k2c], rhs=fhi[:, 128:256], start=False, stop=True),
                        ["zsb", "fhi"], ["bank%d" % (4 + p2)], pe_acc=True)
                ev[0] += 1
                evac(ev[0], bsb[:, grp * 512:(grp + 1) * 512], bk[:, 0:512], ["bank%d" % (4 + p2)], ["bsb"])
            S.barrier()
            if False:
                dump(1024, bsb[:, 0:512], 512, ["bsb"])
            yv = ysb.rearrange("p (kl kh) -> p kl kh", kh=64)
            for grp in range(16):
                p2 = grp % 2
                mt = misc[:, 5632 + p2 * 1024:5632 + p2 * 1024 + 1024].rearrange("p (k m) -> p k m", k=4)
                S.dma("sp", lambda e, mt=mt, grp=grp: e.dma_start(
                    out=mt, in_=m_tab[grp * 4:(grp + 1) * 4].rearrange("k p m -> p k m")), [], ["mt%d" % p2])
                bk = banks[6 + p2]
                for i4 in range(4):
                    khi = grp * 4 + i4
                    S.c("pe", lambda e, bk=bk, i4=i4, khi=khi, mt=mt: e.matmul(
                        bk[:, i4 * 128:(i4 + 1) * 128], lhsT=b3[:, :, khi], rhs=mt[:, i4, 0:128], start=True, stop=False),
                        ["bsb", "mt%d" % p2], ["bank%d" % (6 + p2)], pe_acc=True)
                    S.c("pe", lambda e, bk=bk, i4=i4, khi=khi, mt=mt: e.matmul(
                        bk[:, i4 * 128:(i4 + 1) * 128], lhsT=b3[:, :, 64 + khi], rhs=mt[:, i4, 128:256], start=False, stop=True),
                        ["bsb", "mt%d" % p2], ["bank%d" % (6 + p2)], pe_acc=True)
                ev[0] += 1
                evac(ev[0], yv[:, :, grp * 4:(grp + 1) * 4], bk[:, 0:512].rearrange("p (i kl) -> p kl i", i=4),
                     ["bank%d" % (6 + p2)], ["ysb"])
            S.barrier()
            if False:
                dump(1536, ysb[:, 0:512], 512, ["ysb"])
            for q in range(2):
                S.dma("sp", lambda e, q=q: e.dma_start(out=fy_in[q], in_=ysb[:, q * 4096:(q + 1) * 4096]), ["ysb"], [("fy_in", q)])
                ag(fy_in[q], fy_out[q], [("fy_in", q)], [("fy_out", q)])
            if with_ctx:
                for g in range(4):
                    p2 = g % 2
                    xc = misc[:, 1536 + p2 * 256:1536 + p2 * 256 + 256]
                    S.dma("sp", lambda e, xc=xc, g=g: e.dma_start(out=xc, in_=pfm[1568 + g * 128:1568 + (g + 1) * 128, 0:256]),
                          [("pfm", 1568 + g * 128, 0)], ["xc%d" % p2])
                    bk = banks[p2]
                    zc = misc[:, 2560 + p2 * 512:2560 + p2 * 512 + 512].rearrange("p (b m) -> p b m", b=2)
                    for tb in range(2):
                        S.c("pe", lambda e, bk=bk, xc=xc, tb=tb: e.matmul(
                            bk[:, tb * 256:(tb + 1) * 256], lhsT=xc[:, tb * 128:(tb + 1) * 128], rhs=fc, start=True, stop=True),
                            ["xc%d" % p2, "fc"], ["bank%d" % p2])
                    ev[0] += 1
                    evac(ev[0], zc, bk[:, 0:512].rearrange("p (b m) -> p b m", b=2), ["bank%d" % p2], ["zc%d" % p2])
                    bk2 = banks[2 + p2]
                    for tb in range(2):
                        S.c("pe", lambda e, bk2=bk2, zc=zc, tb=tb: e.matmul(
                            bk2[:, 0:256], lhsT=zc[:, tb, 0:128], rhs=f256[:, tb, 0:256], start=(tb == 0), stop=False),
                            ["zc%d" % p2, "f256"], ["bank%d" % (2 + p2)], pe_acc=True)
                        S.c("pe", lambda e, bk2=bk2, zc=zc, tb=tb: e.matmul(
                            bk2[:, 0:256], lhsT=zc[:, tb, 128:256], rhs=f256[:, tb, 256:512], start=False, stop=(tb == 1)),
                            ["zc%d" % p2, "f256"], ["bank%d" % (2 + p2)], pe_acc=True)
                    yc = misc[:, 3584 + p2 * 256:3584 + p2 * 256 + 256]
                    ev[0] += 1
                    evac(ev[0], yc, bk2[:, 0:256], ["bank%d" % (2 + p2)], ["yc%d" % p2])
                    if False:
                        dump(2560, yc, 256, ["yc%d" % p2])
                    S.dma("sp", lambda e, yc=yc, g=g: e.dma_start(out=br[2, g * 128:(g + 1) * 128, 0:256], in_=yc),
                          ["yc%d" % p2], [("br", 2, g, 0)])
            for sx in range(4):
                for g in range(4):
                    it = sx * 4 + g
                    p2 = it % 2
                    yin = misc[:, 4096 + p2 * 2048:4096 + p2 * 2048 + 2048] if False else \
                        arena[:, 4096 + p2 * 1024:4096 + p2 * 1024 + 1024].bitcast(BF16)
                    yin4 = yin.rearrange("p (q j t) -> p q j t", q=2, j=2)
                    for q in range(2):
                        S.dma("pool", lambda e, yin4=yin4, g=g, sx=sx, q=q: e.dma_start(
                            out=yin4[:, q, :, :],
                            in_=fy_out[q, g * 128:(g + 1) * 128, :].rearrange("c (j t) -> c j t", j=2)[:, :, sx * 512:(sx + 1) * 512]),
                            [("fy_out", 0), ("fy_out", 1)], ["yin%d_%d" % (p2, q)])
                    bk = banks[4 + p2]
                    for jj in range(4):
                        S.c("pe", lambda e, bk=bk, yin4=yin4, jj=jj: e.matmul(
                            bk[:, 0:512], lhsT=selm[:, jj, :], rhs=yin4[:, jj // 2, jj % 2, :], start=(jj == 0), stop=(jj == 3)),
                            ["selm", "yin%d_%d" % (p2, jj // 2)], ["bank%d" % (4 + p2)], pe_acc=True)
                    yo = arena[:, 8192 + p2 * 256:8192 + p2 * 256 + 256].bitcast(BF16)
                    ev[0] += 1
                    evac(ev[0], yo, bk[:, 0:512], ["bank%d" % (4 + p2)], ["yo%d" % p2])
                    t0 = CTX + sx * 512
                    if False:
                        dump(2048, yo, 512, ["yo%d" % p2])
                    S.dma("sp", lambda e, yo=yo, g=g, t0=t0: e.dma_start(out=br[2, g * 128:(g + 1) * 128, t0:t0 + 512], in_=yo),
                          ["yo%d" % p2], [("br", 2, g, t0)])
            S.barrier()


        def gla_branch(l, mtiles):
            A = arena
            aT = A[:, 0:1152].bitcast(BF16)
            spb = A[:, 1152:3456]
            U = A[:, 3456:5760]
            S32 = A[:, 5760:8192].rearrange("p (c m) -> p c m", m=128)
            qT2 = A[:, 8192:9344].bitcast(BF16)
            kT2 = A[:, 9344:10496].bitcast(BF16)
            qp = A[:, 10496:11648].bitcast(BF16)
            kpp = A[:, 11648:12800].bitcast(BF16)
            kppT = A[:, 12800:13952].bitcast(BF16).rearrange("p (c m) -> p c m", m=128)
            Sbf = A[:, 13952:15168].bitcast(BF16).rearrange("p (c m) -> p c m", m=128)
            vh = A[:, 15168:16320].bitcast(BF16).rearrange("p (c m) -> p c m", m=128)
            rT = A[:, 16320:17472].bitcast(BF16)
            ebt = A[:, 17472:17984]
            enbt = A[:, 17984:18496]
            ATs = [A[:, 18496 + i * 128:18496 + (i + 1) * 128].bitcast(BF16) for i in range(2)]
            wa2h = A[:, 18752:18816].bitcast(BF16)
            kppF = A[:, 27104:28256].bitcast(BF16)
            kppB = A[:, 28256:29408].bitcast(BF16)
            ATall = A[:, 29408:31712].bitcast(BF16).rearrange("p (c m) -> p c m", m=256)
            dtp = A[:, 18944:18962]
            dt2 = A[:, 18962:18980]
            dex = A[:, 18980:18998]
            dsum = A[:, 18998:18999]
            osb = A[:, 19008:19520]
            rsb = A[:, 19520:20032]
            srb = A[:, 20032:20544]
            sqb = A[:, 20544:20800].bitcast(BF16)
            ogb = A[:, 20800:21056].bitcast(BF16)
            gx = A[:, 21056:21312]
            gth = A[:, 21572:22596].rearrange("p (r m) -> p r m", m=256)
            Tc = A[:, 23636:23764]
            Lctx = A[:, 23764:23892]
            dpr = A[:, 23892:23896]
            onesf = A[:, 24000:24128]
            cs = A[:, 24800:27104]
            bfb = banks[7][:, 0:256].bitcast(BF16)
            S.c("dve", lambda e: e.memset(onesf, 1.0), [], ["onesf"])
            S.c("dve", lambda e: e.memset(A[:, 0:1152], 0.0), [], ["aT"])
            S.c("dve", lambda e: e.memset(A[:, 18752:18816], 0.0), [], ["wa2h"])
            S.c("dve", lambda e: e.memset(A[:, 27104:29408], 0.0), [], ["kppF", "kppB"])
            S.barrier()
            S.dma("pool", lambda e: e.dma_start(out=aT[0:32, :], in_=pfm[1024:1056, :]), [], ["aT"])
            for h in range(4):
                for half in range(2):
                    S.dma("pool", lambda e, half=half, h=h: e.dma_start(
                        out=qT2[half * 64:(half + 1) * 64, :], in_=pfm[64 * h:64 * h + 64, :]), [], ["qT2_%d" % half])
                    S.dma("pool", lambda e, half=half, h=h: e.dma_start(
                        out=kT2[half * 64:(half + 1) * 64, :], in_=pfm[256 + 64 * h:256 + 64 * h + 64, :]), [], ["kT2_%d" % half])
                S.dma("pool", lambda e, h=h: e.dma_start(
                    out=vh, in_=ptm[:, 128 * h:128 * (h + 1)].rearrange("(c p) d -> p c d", p=128)), [], ["vh"])
                S.dma("pool", lambda e, h=h: e.dma_start(out=rT, in_=pfm[512 + 128 * h:512 + 128 * (h + 1), :]), [], ["rT"])
                S.dma("pool", lambda e, h=h: e.dma_start(out=wa2h[0:32, :], in_=wa2_in[l, h]), [], ["wa2h"], stream="w")
                for ti, (t0, w) in enumerate(TILES):
                    bk = banks[ti % 2]
                    S.c("pe", lambda e, bk=bk, t0=t0, w=w: e.matmul(bk[:, 0:w], lhsT=wa2h, rhs=aT[:, t0:t0 + w], start=True, stop=True),
                        ["wa2h", "aT"], ["bank%d" % (ti % 2)])
                    S.c("act", lambda e, bk=bk, t0=t0, w=w, h=h: e.activation(
                        out=spb[:, t0:t0 + w], in_=bk[:, 0:w], func=AF.Exp, scale=-1.0, bias=nba[:, l, h:h + 1]),
                        ["bank%d" % (ti % 2), "nba%d" % l], ["spb"])
                S.c("act", lambda e: e.activation(out=spb, in_=spb, func=AF.Ln, scale=1.0, bias=onec[:]), ["spb", "onec"], ["spb"])
                for c in range(18):
                    S.c("dve", lambda e, c=c: e.tensor_tensor_scan(
                        out=cs[:, c * 128:(c + 1) * 128], data0=onesf, data1=spb[:, c * 128:(c + 1) * 128], initial=0.0,
                        op0=ALU.mult, op1=ALU.add), ["spb", "onesf"], ["cs"])
                S.c("dve", lambda e: e.tensor_copy(out=dtp, in_=cs[:, 127:2304:128]), ["cs"], ["dtp"])
                for c in range(18):
                    S.c("dve", lambda e, c=c: e.scalar_tensor_tensor(
                        out=cs[64:128, c * 128:(c + 1) * 128], in0=cs[64:128, c * 128:(c + 1) * 128], scalar=dtp[64:128, c:c + 1],
                        in1=spb[64:128, c * 128:(c + 1) * 128], op0=ALU.subtract, op1=ALU.subtract), ["cs", "dtp", "spb"], ["cs"])
                S.c("dve", lambda e: e.tensor_copy(out=dt2[0:64, :], in_=cs[0:64, 127:2304:128]), ["cs"], ["dt2"])
                S.c("dve", lambda e: e.tensor_copy(out=dt2[64:128, :], in_=cs[64:128, 0:2304:128]), ["cs", "dt2"], ["dt2"])
                S.c("act", lambda e: e.activation(out=dex, in_=dt2, func=AF.Exp, scale=sca[:, 0:1]), ["dt2", "sca"], ["dex"])
                S.c("dve", lambda e: e.tensor_reduce(out=dsum, in_=dt2[:, 2:18], axis=AX.X, op=ALU.add), ["dt2"], ["dsum"])
                S.c("act", lambda e: e.activation(out=gx[:, 128:129], in_=dsum, func=AF.Exp, scale=sca[:, 0:1]), ["dsum", "sca"], ["gxD"])
                for ti, (t0, w) in enumerate(TILES):
                    S.c("act", lambda e, t0=t0, w=w: e.activation(out=ebt[:, 0:w], in_=cs[:, t0:t0 + w], func=AF.Exp, scale=sca[:, 0:1]),
                        ["cs", "sca"], ["ebt"])
                    S.c("act", lambda e, t0=t0, w=w: e.activation(out=enbt[:, 0:w], in_=cs[:, t0:t0 + w], func=AF.Exp, scale=sca[:, 1:2]),
                        ["cs", "sca"], ["enbt"])
                    S.c("dve", lambda e, t0=t0, w=w: e.scalar_tensor_tensor(
                        out=qp[:, t0:t0 + w], in0=qT2[:, t0:t0 + w], scalar=0.125, in1=ebt[:, 0:w], op0=ALU.mult, op1=ALU.mult),
                        ["qT2_0", "qT2_1", "ebt"], ["qp"])
                    S.c("dve", lambda e, t0=t0, w=w: e.tensor_tensor(
                        out=kpp[:, t0:t0 + w], in0=kT2[:, t0:t0 + w], in1=enbt[:, 0:w], op=ALU.mult),
                        ["kT2_0", "kT2_1", "enbt"], ["kpp"])
                    S.c("dve", lambda e, t0=t0, w=w: e.tensor_copy(out=kppF[0:64, t0:t0 + w], in_=kpp[0:64, t0:t0 + w]), ["kpp"], ["kppF"])
                    S.c("dve", lambda e, t0=t0, w=w: e.tensor_copy(out=kppB[64:128, t0:t0 + w], in_=kpp[64:128, t0:t0 + w]), ["kpp"], ["kppB"])
                for c in range(18):
                    S.c("pe", lambda e, c=c: e.transpose(out=bfb[:, 0:128], in_=kpp[:, c * 128:(c + 1) * 128], identity=ident_bf[:]),
                        ["kpp", "ident_bf"], ["bank7"])
                    S.c("dve", lambda e, c=c: e.tensor_copy(out=kppT[:, c, :], in_=bfb[:, 0:128]), ["bank7"], ["kppT%d" % (c % 2)])
                    bk = banks[2 + c % 2]
                    S.c("pe", lambda e, c=c, bk=bk: e.matmul(bk[:, 0:128], lhsT=kppT[:, c, :], rhs=vh[:, c, :], start=True, stop=True),
                        ["kppT%d" % (c % 2), "vh"], ["bank%d" % (2 + c % 2)])
                    S.c("act", lambda e, c=c, bk=bk: e.activation(out=U[:, c * 128:(c + 1) * 128], in_=bk[:, 0:128], func=AF.Identity,
                                                                 scale=dex[:, c:c + 1]), ["bank%d" % (2 + c % 2), "dex"], ["U"])

                def chain(rows, out_ap, in_ap, c):
                    r0, r1 = rows
                    S.c("dve", lambda e: e.scalar_tensor_tensor(
                        out=out_ap[r0:r1, :], in0=in_ap[r0:r1, :], scalar=dex[r0:r1, c:c + 1], in1=U[r0:r1, c * 128:(c + 1) * 128],
                        op0=ALU.mult, op1=ALU.add), ["U", "dex", "chain%d" % r0], ["chain%d" % r0])
                FW, BW = (0, 64), (64, 128)
                chain(FW, Lctx, U[:, 0:128], 1)
                chain(BW, Lctx, U[:, 128:256], 0)
                chain(FW, gx[:, 0:128], U[:, 256:384], 3)
                for c in range(4, 18):
                    chain(FW, gx[:, 0:128], gx[:, 0:128], c)
                chain(BW, gx[:, 0:128], U[:, 17 * 128:18 * 128], 16)
                for c in range(15, 1, -1):
                    chain(BW, gx[:, 0:128], gx[:, 0:128], c)
                S.dma("pool", lambda e, h=h: e.dma_start(out=gx_d[h], in_=gx), ["chain0", "chain64", "gxD"], [("gx_d", h)])
                ag(gx_d[h], gxo_d[h], [("gx_d", h)], [("gxo_d", h)])
                c_lo = mtiles[0][0] // 128
                for c in range(c_lo, 18):
                    ci = c % 2
                    ba = banks[ci]
                    cols = slice(c * 128, (c + 1) * 128)
                    S.c("pe", lambda e, ba=ba, cols=cols: e.matmul(ba[:, 0:128], lhsT=kppF[:, cols], rhs=qp[:, cols], start=True, stop=True),
                        ["kppF", "qp"], ["bank%d" % ci])
                    S.c("pe", lambda e, ba=ba, cols=cols: e.matmul(ba[:, 128:256], lhsT=kppB[:, cols], rhs=qp[:, cols], start=True, stop=True),
                        ["kppB", "qp"], ["bank%d" % ci])
                    S.c("dve", lambda e, ba=ba, c=c: e.tensor_tensor(out=ATall[:, c, :], in0=ba[:, 0:256], in1=gmask[:], op=ALU.mult),
                        ["bank%d" % ci, "gmask"], ["atall"])
                S.dma("pool", lambda e, h=h: e.dma_start(out=gth, in_=gxo_d[h].rearrange("(r p) m -> p r m", p=128)),
                      [("gxo_d", h)], ["gth"])
                S.c("dve", lambda e: e.scalar_tensor_tensor(out=dpr, in0=gth[:, :, 128], scalar=-1.0, in1=flg[:, 0:4],
                                                            op0=ALU.add, op1=ALU.mult), ["gth", "flg"], ["dpr"])
                S.c("dve", lambda e: e.tensor_scalar(out=dpr, in0=dpr, scalar1=1.0, scalar2=None, op0=ALU.add), ["dpr"], ["dpr"])
                for i in range(4):
                    S.c("dve", lambda e, i=i: e.tensor_scalar(out=gth[:, i, 0:128], in0=gth[:, i, 0:128], scalar1=flg[:, i:i + 1],
                                                             scalar2=None, op0=ALU.mult), ["gth", "flg"], ["gth"])
                for (r0, r1), order in ((FW, (0, 1, 2, 3)), (BW, (3, 2, 1, 0))):
                    src = Lctx
                    for i in order:
                        S.c("dve", lambda e, r0=r0, r1=r1, i=i, src=src: e.scalar_tensor_tensor(
                            out=Tc[r0:r1, :], in0=src[r0:r1, :], scalar=dpr[r0:r1, i:i + 1], in1=gth[r0:r1, i, 0:128],
                            op0=ALU.mult, op1=ALU.add), ["gth", "dpr", "chain%d" % r0, "tc%d" % r0], ["tc%d" % r0])
                        src = Tc
                S.c("dve", lambda e: e.memset(S32[0:64, 0, :], 0.0), [], ["s32_0"])
                S.c("dve", lambda e: e.memset(S32[64:128, 1, :], 0.0), [], ["s32_64"])
                S.c("dve", lambda e: e.tensor_copy(out=S32[0:64, 1, :], in_=U[0:64, 0:128]), ["U"], ["s32_0"])
                S.c("dve", lambda e: e.tensor_copy(out=S32[64:128, 0, :], in_=U[64:128, 128:256]), ["U"], ["s32_64"])
                S.c("dve", lambda e: e.tensor_copy(out=S32[0:64, 2, :], in_=Tc[0:64, :]), ["tc0"], ["s32_0"])
                S.c("dve", lambda e: e.tensor_copy(out=S32[64:128, 17, :], in_=Tc[64:128, :]), ["tc64"], ["s32_64"])
                for c in range(2, 17):
                    S.c("dve", lambda e, c=c: e.scalar_tensor_tensor(
                        out=S32[0:64, c + 1, :], in0=S32[0:64, c, :], scalar=dex[0:64, c:c + 1], in1=U[0:64, c * 128:(c + 1) * 128],
                        op0=ALU.mult, op1=ALU.add), ["U", "dex", "s32_0"], ["s32_0"])
                for c in range(17, 2, -1):
                    S.c("dve", lambda e, c=c: e.scalar_tensor_tensor(
                        out=S32[64:128, c - 1, :], in0=S32[64:128, c, :], scalar=dex[64:128, c:c + 1], in1=U[64:128, c * 128:(c + 1) * 128],
                        op0=ALU.mult, op1=ALU.add), ["U", "dex", "s32_64"], ["s32_64"])
                S.c("dve", lambda e: e.tensor_copy(out=Sbf[:, 0:18, :], in_=S32[:, 0:18, :]), ["s32_0", "s32_64"], ["Sbf"])
                for ti, (t0, w) in enumerate(mtiles):
                    bo = banks[4 + ti % 2]
                    for cc in range(w // 128):
                        c = t0 // 128 + cc
                        cols = slice(c * 128, (c + 1) * 128)
                        at = ATall[:, c, :]
                        oc = bo[:, cc * 128:(cc + 1) * 128]
                        S.c("pe", lambda e, oc=oc, c=c, cols=cols: e.matmul(oc, lhsT=Sbf[:, c, :], rhs=qp[:, cols], start=True, stop=False),
                            ["Sbf", "qp"], ["bank%d" % (4 + ti % 2)], pe_acc=True)
                        S.c("pe", lambda e, oc=oc, c=c, at=at: e.matmul(oc, lhsT=vh[:, c, :], rhs=at[:, 0:128], start=False, stop=False),
                            ["vh", "atall"], ["bank%d" % (4 + ti % 2)], pe_acc=True)
                        S.c("pe", lambda e, oc=oc, c=c, at=at: e.matmul(oc, lhsT=vh[:, c, :], rhs=at[:, 128:256], start=False, stop=True),
                            ["vh", "atall"], ["bank%d" % (4 + ti % 2)], pe_acc=True)
                    S.c("act", lambda e, bo=bo, w=w: e.activation(out=sqb[:, 0:w], in_=bo[:, 0:w], func=AF.Square),
                        ["bank%d" % (4 + ti % 2)], ["gsq"])
                    S.c("pe", lambda e, w=w: e.matmul(banks[6][:, 0:w], lhsT=ones_bf[:], rhs=sqb[:, 0:w], start=True, stop=True),
                        ["gsq", "ones"], ["bank6"])
                    S.c("act", lambda e, w=w: e.activation(out=rsb[:, 0:w], in_=banks[6][:, 0:w], func=AF.Sqrt, scale=1.0 / 128, bias=epsc[:]),
                        ["bank6", "epsc"], ["grs"])
                    S.c("dve", lambda e, w=w: e.reciprocal(out=rsb[:, 0:w], in_=rsb[:, 0:w]), ["grs"], ["grs"])
                    S.c("dve", lambda e, bo=bo, w=w: e.tensor_tensor(out=osb[:, 0:w], in0=bo[:, 0:w], in1=rsb[:, 0:w], op=ALU.mult),
                        ["bank%d" % (4 + ti % 2), "grs"], ["gos"])
                    S.c("act", lambda e, t0=t0, w=w: e.activation(out=srb[:, 0:w], in_=rT[:, t0:t0 + w], func=AF.Silu), ["rT"], ["gsr"])
                    S.c("dve", lambda e, w=w, h=h: e.scalar_tensor_tensor(
                        out=ogb[:, 0:w], in0=osb[:, 0:w], scalar=ggo[:, l, h:h + 1], in1=srb[:, 0:w], op0=ALU.mult, op1=ALU.mult),
                        ["gos", "gsr", "ggo%d" % l], ["gog"])
                    S.dma("sp", lambda e, h=h, t0=t0, w=w: e.dma_start(out=br[0, 128 * h:128 * (h + 1), t0:t0 + w], in_=ogb[:, 0:w]),
                          ["gog"], [("br", 0, h, t0)])
                S.barrier()

        def final_out():
            for ti, (t0, w) in enumerate(TILES if debug else TILES[1:]):
                sq = bbuf[:, 24576:24576 + KC * 512].rearrange("p (k t) -> p k t", k=KC)[:, :, 0:w]
                S.c("act", lambda e, t0=t0, w=w, sq=sq: e.activation(out=sq, in_=xs[:, :, t0:t0 + w], func=AF.Square),
                    [("xs", t0, k_) for k_ in range(KC)], ["sq"])
                for kc in range(KC):
                    S.c("pe", lambda e, kc=kc, w=w, sq=sq: e.matmul(
                        banks[7][:, 0:w], lhsT=ones_bf[:], rhs=sq[:, kc, :], start=(kc == 0), stop=(kc == KC - 1)),
                        ["sq", "ones"], ["bank7"], pe_acc=True)
                rs = fbuf[:, 0:w]
                S.c("act", lambda e, w=w, rs=rs: e.activation(out=rs, in_=banks[7][:, 0:w], func=AF.Sqrt,
                                                             scale=1.0 / D, bias=epsc[:]), ["bank7", "epsc"], ["rs"])
                S.c("dve", lambda e, rs=rs: e.reciprocal(out=rs, in_=rs), ["rs"], ["rs"])
                ot = fbuf[:, 2048:2048 + KC * 512].rearrange("p (k t) -> p k t", k=KC)
                for kc in range(KC):
                    S.c("dve", lambda e, kc=kc, t0=t0, w=w, rs=rs, ot=ot: e.scalar_tensor_tensor(
                        out=ot[:, kc, 0:w], in0=xs[:, kc, t0:t0 + w], scalar=gfin[:, kc:kc + 1], in1=rs,
                        op0=ALU.mult, op1=ALU.mult), [("xs", t0, kc), "rs", "gfin"], ["otile"])
                if t0 < CTX:
                    S.dma("sp", lambda e, w=w, ot=ot: e.dma_start(
                        out=out_ctx.rearrange("(kc p) t -> p kc t", p=128), in_=ot[:, :, 0:w]), ["otile"], ["outc"])
                else:
                    S.dma("sp", lambda e, t0=t0, w=w, ot=ot: e.dma_start(
                        out=out.rearrange("(kc p) t -> p kc t", p=128)[:, :, t0 - CTX:t0 - CTX + w], in_=ot[:, :, 0:w]),
                        ["otile"], ["out"])
            S.barrier()

        def forward():
            for l in range(DEPTH):
                last = (l == DEPTH - 1)
                ffn(l, 0, TILES)
                if stop_after == ("ffn1", l):
                    return
                mtiles = TILES[1:] if last else TILES
                norm_to_hall(l, 1, TILES, "mix")
                win_proj(l, TILES)
                if "nogla" in dbg_opts:
                    zero_branch(0, mtiles)
                else:
                    gla_branch(l, mtiles)
                fnet_exchange_in()
                if "nofnet" in dbg_opts:
                    zero_branch(2, mtiles)
                sgu_branch(l, mtiles)
                conv_branch(l, mtiles)
                if "nofnet" not in dbg_opts:
                    fnet_branch(l, not last)
                if l == 0:
                    for k_, r0_ in enumerate((1568, 2080, 2592, 3104)):
                        dump(k_ * 256, pfm[r0_:r0_ + 128, 0:256], 256, [])
                        dump(1024 + k_ * 256, pfm[r0_:r0_ + 128, 256:512], 256, [])
                    S.barrier()
                merge(l, mtiles)
                if stop_after == ("mix", l):
                    return
                ffn(l, 1, TILES[1:] if last else TILES)
                if stop_after == ("ffn2", l):
                    return

        forward()
        final_out()

        sems = {e: [es.enter_context(nc.semaphore("sem_%s_%d" % (e, k_))) for k_ in range(S.epoch[e] + 1)]
                for e in Sched.ENGS}
        with nc.Block() as block:
            S.emit(block, sems, dma_sems, cc_sem)
        if debug:
            print("sem epochs", S.epoch, "max count", max(S.max_counts.values()))
    return nc


_NC_CACHE = {}


def make_in_maps(inputs):
    f = lambda a: np.ascontiguousarray(np.asarray(a, dtype=np.float32))
    x, c, ctx, c_ctx = f(inputs["x"]), f(inputs["c"]), f(inputs["ctx"]), f(inputs["c_ctx"])
    shared = {
        "ident": np.eye(128, dtype=np.float32),
        "w_ada": f(inputs["w_ada"]),
        "b_ada": f(inputs["b_ada"]).reshape(DEPTH, 72, 128),
        "g_norm": f(inputs["g_norm"]).reshape(DEPTH, 24, 128),
        "w_ff1": f(inputs["w_ff1"]), "w_ff3": f(inputs["w_ff3"]), "w_ff2": f(inputs["w_ff2"]),
        "g_final": f(inputs["g_final"]).reshape(8, 128),
        "w_in": f(inputs["w_in"]),
        "w_sguT": np.ascontiguousarray(f(inputs["w_sgu"]).transpose(0, 1, 3, 2)),
        "b_sgu": f(inputs["b_sgu"]).reshape(DEPTH, 512),
        "w_conv": f(inputs["w_conv"]).reshape(DEPTH, 12, 128),
        "w_branch": f(inputs["w_branch"]), "w_gate": f(inputs["w_gate"]),
        "b_gate": f(inputs["b_gate"]).reshape(DEPTH, 32, 128),
        "w_out": f(inputs["w_out"]),
    }
    bf = ml_dtypes.bfloat16
    cidx = np.arange(128, dtype=np.float64)
    ang = 2 * np.pi * np.outer(cidx, cidx) / 128
    shared["fc_tab"] = np.concatenate([np.cos(ang), -np.sin(ang)], 1).astype(bf)
    a64 = 2 * np.pi * np.outer(np.arange(64.0), np.arange(64.0)) / 64
    shared["fhi_tab"] = np.concatenate([np.cos(a64), -np.sin(a64), np.sin(a64), np.cos(a64)], 1).astype(bf)
    khi = np.arange(64.0)[:, None, None]
    nlo = np.arange(128.0)[None, :, None]
    klo = np.arange(128.0)[None, None, :]
    am = 2 * np.pi * nlo * (khi + 64 * klo) / 8192
    shared["m_tab"] = (np.concatenate([np.cos(am), np.sin(am)], 2) / 1024.0).astype(bf)
    a256 = 2 * np.pi * np.outer(np.arange(256.0), np.arange(256.0)) / 256
    sc = 1.0 / np.sqrt(256.0 * 128.0)
    shared["f256_tab"] = (np.concatenate([np.cos(a256), np.sin(a256)], 1) * sc).reshape(2, 128, 512).astype(bf)
    wa = f(inputs["w_gla_a2"]); ba = f(inputs["b_gla_a2"])
    wa2 = np.zeros((DEPTH, 4, 32, 128), np.float32)
    ba2 = np.zeros((DEPTH, 4, 128), np.float32)
    for h_ in range(4):
        wa2[:, h_, 0:16, 0:64] = wa[:, 0, :, 64 * h_:64 * h_ + 64]
        wa2[:, h_, 16:32, 64:128] = wa[:, 1, :, 64 * h_:64 * h_ + 64]
        ba2[:, h_, 0:64] = ba[:, 0, 64 * h_:64 * h_ + 64]
        ba2[:, h_, 64:128] = ba[:, 1, 64 * h_:64 * h_ + 64]
    shared["wa2"] = wa2
    shared["ba2"] = ba2
    shared["ggo"] = f(inputs["g_gla_norm"])
    jj, ii = np.meshgrid(np.arange(128), np.arange(128), indexing="ij")
    shared["gmask"] = np.concatenate([(jj <= ii), (jj >= ii)], 1).astype(np.float32).astype(bf)
    sc_ = np.zeros((128, 2), np.float32)
    sc_[0:64, 0] = -1.0 / 16; sc_[64:128, 0] = 1.0 / 16
    sc_[:, 1] = -sc_[:, 0]
    shared["sca"] = sc_
    maps = []
    for r in range(8):
        b, j = r // 4, r % 4
        xf = np.concatenate([ctx[b], x[b, j * TL:(j + 1) * TL]], axis=0).T
        m = dict(shared)
        m["x_fm"] = np.ascontiguousarray(xf)
        sel = np.zeros((4, 128, 128), np.float32)
        sel[j] = np.eye(128, dtype=np.float32)
        m["selm"] = sel.astype(ml_dtypes.bfloat16)
        fl = np.zeros((128, 8), np.float32)
        for i_ in range(4):
            fl[0:64, i_] = 1.0 if i_ < j else 0.0
            fl[64:128, i_] = 1.0 if i_ > j else 0.0
        m["flags"] = fl
        m["cvec"] = np.ascontiguousarray(np.concatenate([c[b].reshape(8, 128), c_ctx.reshape(8, 128)], 0))
        maps.append(m)
    return maps


def kernel(**inputs):
    if "nc" not in _NC_CACHE:
        _NC_CACHE["nc"] = build_program()
    nc = _NC_CACHE["nc"]
    maps = make_in_maps(inputs)
    res = run_bass_kernel_spmd(nc, maps, core_ids=list(range(8)))
    outp = np.empty((2, SEQ, D), dtype=np.float32)
    for r in range(8):
        b, j = r // 4, r % 4
        outp[b, j * TL:(j + 1) * TL, :] = res.results[r]["out_fm"].T
    return outp
```

```python
import os
import numpy as np
import ml_dtypes
import concourse.bass as bass
import concourse.mybir as mybir
from concourse.bass_utils import run_bass_kernel_spmd

F32 = mybir.dt.float32
BF16 = mybir.dt.bfloat16
AF = mybir.ActivationFunctionType
ALU = mybir.AluOpType
AX = mybir.AxisListType

D = 1024
KC = 8
DEPTH = 2
SEQ = 8192
CTX = 256
TL = 2048
T = CTX + TL
DFF = 2816
NFC = 22
INW = 4640
EPS = 1e-6
TILES = [(0, 256), (256, 512), (768, 512), (1280, 512), (1792, 512)]
GROUPS4 = [[0, 1, 2, 3], [4, 5, 6, 7]]


class Op:
    __slots__ = ("eng", "fn", "kind", "stream", "deps", "count", "sig", "idx", "pe_acc", "semi", "epoch")


class Sched:
    ENGS = ("pe", "act", "dve", "pool", "sp")
    KDMA = 8

    def __init__(self, nc):
        self.nc = nc
        self.ops = {e: [] for e in self.ENGS}
        self.allops = []
        self.last_w = {}
        self.readers = {}
        self.nstreams = {}
        self.epoch = {e: 0 for e in self.ENGS}
        self.epoch_ops = {e: 0 for e in self.ENGS}

    def _add(self, eng, fn, reads, writes, kind="c", stream=None, pe_acc=False):
        op = Op()
        op.eng, op.fn, op.kind, op.stream, op.pe_acc = eng, fn, kind, stream, pe_acc
        op.deps = []
        op.sig = False
        op.count = None
        op.idx = len(self.allops)
        op.epoch = self.epoch[eng]
        self.epoch_ops[eng] += 1
        deps = {}
        for k in reads:
            w = self.last_w.get(k)
            if w is not None:
                deps[w.idx] = (w, "raw")
        for k in writes:
            w = self.last_w.get(k)
            if w is not None and w.idx not in deps:
                deps[w.idx] = (w, "waw")
            for r in self.readers.get(k, ()):
                if r.idx not in deps:
                    deps[r.idx] = (r, "war")
        for k in reads:
            lst = self.readers.setdefault(k, [])
            if kind == "c":
                lst[:] = [r for r in lst if not (r.kind == "c" and r.eng == eng)]
            lst.append(op)
        for k in writes:
            self.last_w[k] = op
            self.readers[k] = []
        for w, typ in deps.values():
            if w is op:
                continue
            if w.kind == "c" and op.kind == "c" and w.eng == eng:
                if eng == "pe":
                    continue
                if typ != "raw":
                    continue
            op.deps.append(w)
            w.sig = True
        self.ops[eng].append(op)
        self.allops.append(op)
        return op

    def c(self, eng, fn, reads=(), writes=(), pe_acc=False):
        return self._add(eng, fn, reads, writes, "c", pe_acc=pe_acc)

    def dma(self, eng, fn, reads=(), writes=(), stream="d"):
        return self._add(eng, fn, reads, writes, "d", stream=(eng, stream))

    def cc(self, fn, reads=(), writes=()):
        return self._add("pool", fn, reads, writes, "cc", stream=("pool", "cc"))

    def barrier(self):
        lasts = []
        for e in self.ENGS:
            for op in reversed(self.ops[e]):
                if op.kind == "c":
                    lasts.append(op)
                    break
        seen = {}
        for op in reversed(self.allops):
            if op.kind in ("d", "cc"):
                k = seen.get(op.stream, 0)
                if k < (self.KDMA if op.kind == "d" else 1):
                    seen[op.stream] = k + 1
                    lasts.append(op)
        for e in self.ENGS:
            op = self._add(e, None, [], [], "c")
            for w in lasts:
                if w.kind == "c" and w.eng == e:
                    continue
                op.deps.append(w)
                w.sig = True
        for e in self.ENGS:
            if self.epoch_ops[e] > 1500:
                self.epoch[e] += 1
                self.epoch_ops[e] = 0

    def emit(self, block, sems, dma_sems, cc_sem):
        nc = self.nc
        cnt = {}
        scnt = {}
        for e in self.ENGS:
            pending = []
            for op in self.ops[e]:
                if op.kind == "c":
                    if op.sig:
                        cnt[(e, op.epoch)] = cnt.get((e, op.epoch), 0) + 1
                        op.count = cnt[(e, op.epoch)]
                    else:
                        op.count = None
                elif op.kind == "d":
                    st = op.stream
                    n = scnt.get(st, 0)
                    scnt[st] = n + 1
                    op.semi = n % self.KDMA
                    op.count = 16 * (n // self.KDMA + 1)
                else:
                    st = op.stream
                    scnt[st] = scnt.get(st, 0) + 1
                    op.count = scnt[st]
        engobj = {"pe": "tensor", "act": "scalar", "dve": "vector", "pool": "gpsimd", "sp": "sync"}

        def semof(op):
            if op.kind == "c":
                return sems[op.eng][op.epoch]
            if op.kind == "d":
                return dma_sems[op.stream][op.semi]
            return cc_sem

        def body(e):
            def f(eng):
                waited = {}
                for op in self.ops[e]:
                    need = {}
                    for d in op.deps:
                        s = semof(d)
                        assert d.count is not None, "dependency on non-signalling op"
                        key = id(s)
                        if waited.get(key, 0) >= d.count:
                            continue
                        if key not in need or need[key][1] < d.count:
                            need[key] = (s, d.count)
                    for key, (s, v) in need.items():
                        eng.wait_ge(s, v)
                        waited[key] = v
                    if op.fn is None:
                        if op.sig:
                            ins = eng.nop()
                            ins.then_inc(sems[e][op.epoch], 1)
                        continue
                    ins = op.fn(eng)
                    if op.kind == "c":
                        if op.sig:
                            ins.then_inc(sems[e][op.epoch], 1)
                    elif op.kind == "d":
                        ins.then_inc(dma_sems[op.stream][op.semi], 16)
                    else:
                        ins.then_inc(cc_sem)
            return f

        block.tensor(body("pe"))
        block.scalar(body("act"))
        block.vector(body("dve"))
        block.gpsimd(body("pool"))
        block.sync(body("sp"))
        self.max_counts = cnt
        return scnt


def build_program(stop_after=None, debug=False, dbg_opts=()):
    nc = bass.Bass("TRN2", target_bir_lowering=False)
    S = Sched(nc)

    def din(name, shape, dt=F32):
        return nc.dram_tensor(name, list(shape), dt, kind="ExternalInput").ap()

    x_in = din("x_fm", [D, T])
    cvec = din("cvec3", [24, 128])
    bsel_in = din("bsel", [128, 2])
    ada_in = nc.dram_tensor("ada_in", [128, 64], F32).ap()
    ada_out = nc.dram_tensor("ada_out", [1024, 64], F32).ap()
    ident_in = din("ident", [128, 128])
    w_ada = din("wada_sh", [DEPTH, D, 1152])
    b_ada = din("bada_sh", [DEPTH, 9, 128])
    g_norm = din("g_norm", [DEPTH, 24, 128])
    w_ff1 = din("w_ff1", [DEPTH, 2, D, DFF])
    w_ff3 = din("w_ff3", [DEPTH, 2, D, DFF])
    w_ff2 = din("w_ff2", [DEPTH, 2, DFF, D])
    g_final = din("g_final", [8, 128])
    w_in = din("w_in", [DEPTH, D, INW])
    w_sguT = din("w_sguT", [DEPTH, 4, 128, 128])
    b_sgu = din("b_sgu", [DEPTH, 512])
    w_conv = din("w_conv", [DEPTH, 12, 128])
    w_branch = din("w_branch", [DEPTH, 4, 512, D])
    w_gate = din("w_gate", [DEPTH, 4, D, D])
    b_gate = din("b_gate", [DEPTH, 32, 128])
    w_out = din("w_out", [DEPTH, D, D])
    selm_in = din("selm", [4, 128, 128], BF16)
    flags_in = din("flags", [128, 8])
    pfm = nc.dram_tensor("pfm", [3616, T], BF16).ap()
    ptm = nc.dram_tensor("ptm", [T, 1024], BF16).ap()
    br = nc.dram_tensor("br", [4, 512, T], BF16).ap()
    fx_in = nc.dram_tensor("fx_in", [2, 256, TL], BF16).ap()
    fx_out = nc.dram_tensor("fx_out", [2, 1024, TL], BF16).ap()
    fy_in = nc.dram_tensor("fy_in", [2, 128, 4096], BF16).ap()
    fy_out = nc.dram_tensor("fy_out", [2, 512, 4096], BF16).ap()
    wa2_in = din("wa2", [DEPTH, 4, 32, 128])
    ba2_in = din("ba2", [DEPTH, 4, 128])
    ggo_in = din("ggo", [DEPTH, 4, 128])
    gmask_in = din("gmask", [128, 256], BF16)
    sca_in = din("sca", [128, 2])
    gx_d = nc.dram_tensor("gx_d", [4, 128, 256], F32).ap()
    gxo_d = nc.dram_tensor("gxo_d", [4, 512, 256], F32).ap()
    fence_in = nc.dram_tensor("fence_in", [128, 64], F32).ap()
    fence_out = nc.dram_tensor("fence_out", [512, 64], F32).ap()
    fc_in = din("fc_tab", [128, 256], BF16)
    fhi_in = din("fhi_tab", [64, 256], BF16)
    m_tab = din("m_tab", [64, 128, 256], BF16)
    f256_in = din("f256_tab", [2, 128, 512], BF16)
    out = nc.dram_tensor("out_fm", [D, TL], F32, kind="ExternalOutput").ap()
    dbg = nc.dram_tensor("dbg", [128, 4096], F32, kind="ExternalOutput").ap() if debug else None
    out_ctx = nc.dram_tensor("out_ctx", [D, CTX], F32, kind="ExternalOutput").ap() if debug else None

    def dump(c0, ap, n, keys, np_=128):
        if debug:
            S.dma("pool", lambda e: e.dma_start(out=dbg[0:np_, c0:c0 + n], in_=ap), keys, ["dbg%d" % c0], stream="d")

    import contextlib
    es = contextlib.ExitStack()

    def sb(name, shape, dt):
        return es.enter_context(nc.sbuf_tensor(name, list(shape), dt))

    def ps(name, shape, dt=F32):
        return es.enter_context(nc.psum_tensor(name, list(shape), dt))

    with es:
        xs = sb("xs", [128, KC, T], F32)
        arena = sb("arena", [128, 32768], F32)
        hall = arena[:, 0:9216].bitcast(BF16).rearrange("p (k t) -> p k t", k=KC)
        bbuf = arena[:, 9216:25600].bitcast(BF16)
        fbuf = arena[:, 25600:31744]
        xbuf = arena[:, 31744:32768]
        hflat = arena[:, 0:9216]
        ident = sb("ident_sb", [128, 128], F32)
        ones_bf = sb("ones_bf", [128, 128], BF16)
        epsc = sb("epsc", [128, 1], F32)
        mods = sb("mods", [128, DEPTH, 72, 2], F32)
        gsh = sb("gsh", [128, DEPTH, 3, 3, KC, 2], F32)
        gn = sb("gn", [128, DEPTH, 24], F32)
        gfin = sb("gfin", [128, 8], F32)
        csil = sb("csil", [128, 24], BF16)
        bsel = sb("bsel_sb", [128, 2], F32)
        msh = sb("msh", [128, 64], F32)
        small = sb("small", [128, 256], F32)
        wcv = sb("wcv", [128, DEPTH, 12], F32)
        bgt = sb("bgt", [128, DEPTH, 32], F32)
        selm = sb("selm_sb", [128, 4, 128], BF16)
        flg = sb("flg", [128, 8], F32)
        nba = sb("nba", [128, DEPTH, 4], F32)
        ggo = sb("ggo_sb", [128, DEPTH, 4], F32)
        sca = sb("sca_sb", [128, 2], F32)
        onec = sb("onec", [128, 1], F32)
        ident_bf = sb("ident_bf", [128, 128], BF16)
        gmask = sb("gmask_sb", [128, 256], BF16)
        banks = [ps("bank%d" % i, [128, 512]) for i in range(8)]

        dma_streams = [("sp", "w"), ("sp", "d"), ("pool", "d"), ("pool", "w"), ("act", "d")]
        dma_sems = {st: [es.enter_context(nc.semaphore("dsem_%s_%s_%d" % (st[0], st[1], i_))) for i_ in range(Sched.KDMA)]
                    for st in dma_streams}
        cc_sem = es.enter_context(nc.semaphore("cc_sem"))

        S.dma("sp", lambda e: e.dma_start(out=ident[:], in_=ident_in[:, :]), [], ["ident"])
        S.c("dve", lambda e: e.memset(ones_bf[:], 1.0), [], ["ones"])
        S.c("dve", lambda e: e.memset(epsc[:], EPS), [], ["epsc"])
        S.dma("sp", lambda e: e.dma_start(
            out=xs[:], in_=x_in.rearrange("(kc p) t -> p kc t", p=128)), [], [("xs", t0_, k_) for (t0_, _w) in TILES for k_ in range(KC)])

        def load_cols(src_ap, rows, dst_ap, key):
            stg = fbuf[0:rows, 0:128]
            S.dma("sp", lambda e: e.dma_start(out=stg, in_=src_ap), ["stg_free"], ["stg"])
            S.c("pe", lambda e: e.transpose(out=banks[7][:, 0:rows], in_=stg, identity=ident[0:rows, 0:rows]),
                ["stg", "ident"], ["bank7"])
            S.c("dve", lambda e: e.tensor_copy(out=dst_ap, in_=banks[7][:, 0:rows]), ["bank7"], [key, "stg_free"])

        S.dma("sp", lambda e: e.dma_start(out=selm[:], in_=selm_in.rearrange("g p m -> p g m")), [], ["selm"])
        S.dma("sp", lambda e: e.dma_start(out=flg[:], in_=flags_in[:, :]), [], ["flg"])
        S.dma("sp", lambda e: e.dma_start(out=sca[:], in_=sca_in[:, :]), [], ["sca"])
        S.dma("sp", lambda e: e.dma_start(out=gmask[:], in_=gmask_in[:, :]), [], ["gmask"])
        S.c("dve", lambda e: e.memset(onec[:], 1.0), [], ["onec"])
        S.c("dve", lambda e: e.memset(msh[:], 0.0), [], ["msh"])
        S.c("dve", lambda e: e.tensor_copy(out=ident_bf[:], in_=ident[:]), ["ident"], ["ident_bf"])
        load_cols(cvec[:, :], 24, small[:, 0:24], "cv")
        S.dma("sp", lambda e: e.dma_start(out=bsel[:], in_=bsel_in[:, :]), [], ["bsel"])
        S.c("act", lambda e: e.activation(out=csil[:].rearrange("p (k j) -> p j k", j=3), in_=small[:, 0:24].rearrange("p (j k) -> p j k", j=3), func=AF.Silu), ["cv"], ["csil"])
        load_cols(g_final[:, :], 8, gfin[:], "gfin")
        dump(3104, gfin[:, 0:8], 8, ["gfin"])
        def setup_layer(l):
            load_cols(g_norm[l], 24, gn[:, l, :], "gn%d" % l)
            load_cols(w_conv[l], 12, wcv[:, l, :], "wcv%d" % l)
            load_cols(ba2_in[l], 4, nba[:, l, :], "nba%d" % l)
            S.c("dve", lambda e: e.tensor_scalar(out=nba[:, l, :], in0=nba[:, l, :], scalar1=-1.0, scalar2=None, op0=ALU.mult),
                ["nba%d" % l], ["nba%d" % l])
            load_cols(ggo_in[l], 4, ggo[:, l, :], "ggo%d" % l)
            load_cols(b_gate[l], 32, bgt[:, l, :], "bgt%d" % l)
            load_cols(b_ada[l], 9, small[:, 32:41], "bada")
            wv = bbuf[:, 0:KC * 1152].rearrange("p (k n) -> p k n", k=KC)
            for kc in range(KC):
                S.dma("pool", lambda e, wv=wv, kc=kc: e.dma_start(
                    out=wv[:, kc, :], in_=w_ada[l, kc * 128:(kc + 1) * 128, :]), [], ["wada%d" % kc], stream="w")
            for nch in range(9):
                for kc in range(KC):
                    S.c("pe", lambda e, nch=nch, kc=kc, wv=wv: e.matmul(
                        banks[6][:, l * 27 + 3 * nch:l * 27 + 3 * nch + 3], lhsT=wv[:, kc, nch * 128:(nch + 1) * 128],
                        rhs=csil[:, 3 * kc:3 * kc + 3], start=(kc == 0), stop=(kc == KC - 1)),
                        ["wada%d" % kc, "csil"], ["bank6"], pe_acc=True)
            for v in range(3):
                S.c("dve", lambda e, v=v: e.tensor_tensor(
                    out=msh[:, l * 27 + v:l * 27 + 27:3], in0=banks[6][:, l * 27 + v:l * 27 + 27:3], in1=small[:, 32:41], op=ALU.add),
                    ["bank6", "bada"], ["msh"])

        def finish_ada(l):
            mfull = fbuf[:, 0:512].rearrange("p (r m) -> p r m", r=8)
            tmp = small[:, 96:168].rearrange("p (r n) -> p r n", r=8)
            srcv = [mfull[:, :, l * 27 + v:l * 27 + 27:3] for v in range(3)]
            S.c("dve", lambda e: e.tensor_copy(out=mods[:, l, :, 1].rearrange("p (r n) -> p r n", r=8), in_=srcv[2]),
                ["mfull"], ["mods%d" % l])
            S.c("dve", lambda e: e.tensor_scalar(out=tmp, in0=srcv[0], scalar1=bsel[:, 0:1], scalar2=None, op0=ALU.mult),
                ["mfull", "bsel"], ["adatmp"])
            S.c("dve", lambda e: e.scalar_tensor_tensor(out=mods[:, l, :, 0].rearrange("p (r n) -> p r n", r=8), in0=srcv[1],
                                                        scalar=bsel[:, 1:2], in1=tmp, op0=ALU.mult, op1=ALU.add),
                ["mfull", "bsel", "adatmp", "mods%d" % l], ["mods%d" % l])
            for sub in range(3):
                for j in range(2):
                    sc = mods[:, l, (3 * sub + 1) * 8:(3 * sub + 2) * 8, j]
                    S.c("dve", lambda e, l=l, sub=sub, j=j, sc=sc: e.scalar_tensor_tensor(
                        out=gsh[:, l, sub, 0, :, j], in0=sc, scalar=1.0, in1=gn[:, l, sub * 8:(sub + 1) * 8],
                        op0=ALU.add, op1=ALU.mult), ["mods%d" % l, "gn%d" % l], ["gsh%d" % l])
                    S.c("dve", lambda e, l=l, sub=sub, j=j: e.tensor_copy(
                        out=gsh[:, l, sub, 1, :, j], in_=mods[:, l, (3 * sub) * 8:(3 * sub + 1) * 8, j]),
                        ["mods%d" % l], ["gsh%d" % l])
                    S.c("dve", lambda e, l=l, sub=sub, j=j: e.tensor_scalar(
                        out=gsh[:, l, sub, 2, :, j], in0=mods[:, l, (3 * sub + 2) * 8:(3 * sub + 3) * 8, j],
                        scalar1=(1.0 if sub == 1 else 0.5), scalar2=None, op0=ALU.mult),
                        ["mods%d" % l], ["gsh%d" % l])
        for l_ in range(DEPTH):
            setup_layer(l_)
        S.dma("sp", lambda e: e.dma_start(out=ada_in[:, :], in_=msh[:]), ["msh"], ["ada_in"])
        S.cc(lambda e: e.collective_compute("AllGather", ALU.bypass, replica_groups=[list(range(8))],
                                            ins=[ada_in[:, :]], outs=[ada_out[:, :]]), ["ada_in"], ["ada_out"])
        S.dma("pool", lambda e: e.dma_start(out=fbuf[:, 0:512].rearrange("p (r m) -> p r m", r=8),
                                             in_=ada_out.rearrange("(r p) m -> p r m", p=128)), ["ada_out"], ["mfull"])
        for l_ in range(DEPTH):
            finish_ada(l_)
        S.barrier()

        def norm_to_hall(l, sub, tiles, tag):
            for ti, (t0, w) in enumerate(tiles):
                j = 1 if t0 < CTX else 0
                sq = bbuf[:, 24576:24576 + KC * 512].rearrange("p (k t) -> p k t", k=KC)[:, :, 0:w]
                S.c("act", lambda e, t0=t0, w=w, sq=sq: e.activation(out=sq, in_=xs[:, :, t0:t0 + w], func=AF.Square),
                    [("xs", t0, k_) for k_ in range(KC)], ["sq"])
                for kc in range(KC):
                    S.c("pe", lambda e, kc=kc, w=w, sq=sq: e.matmul(
                        banks[7][:, 0:w], lhsT=ones_bf[:], rhs=sq[:, kc, :], start=(kc == 0), stop=(kc == KC - 1)),
                        ["sq", "ones"], ["bank7"], pe_acc=True)
                rs = fbuf[:, 0:w]
                S.c("act", lambda e, w=w, rs=rs: e.activation(out=rs, in_=banks[7][:, 0:w], func=AF.Sqrt,
                                                             scale=1.0 / D, bias=epsc[:]), ["bank7", "epsc"], ["rs"])
                S.c("dve", lambda e, rs=rs: e.reciprocal(out=rs, in_=rs), ["rs"], ["rs"])
                for kc in range(KC):
                    tmp = fbuf[:, 512 + (kc % 2) * 512:512 + (kc % 2) * 512 + w]
                    S.c("dve", lambda e, kc=kc, t0=t0, w=w, tmp=tmp, rs=rs: e.tensor_tensor(
                        out=tmp, in0=xs[:, kc, t0:t0 + w], in1=rs, op=ALU.mult), [("xs", t0, kc), "rs"], ["ntmp%d" % (kc % 2)])
                    S.c("act", lambda e, kc=kc, t0=t0, w=w, tmp=tmp, j=j: e.activation(
                        out=hall[:, kc, t0:t0 + w], in_=tmp, func=AF.Identity,
                        scale=gsh[:, l, sub, 0, kc, j:j + 1], bias=gsh[:, l, sub, 1, kc, j:j + 1]),
                        ["ntmp%d" % (kc % 2), "gsh%d" % l], ["hall%d" % ti])

        def ffn(l, i, tiles):
            sub = 0 if i == 0 else 2
            norm_to_hall(l, sub, tiles, "ffn")
            if False:
                dump(0, hall[:, 0, 256:768], 512, ["hall1"])
                dump(1024, gsh[:, 0, 2, :, :, :], 48, ["gsh0"])
                S.c("dve", lambda e: e.tensor_copy(out=fbuf[:, 4096:4608], in_=xs[:, 0, 256:768]), [("xs", 256, 0)], ["xdump"])
                dump(3072, fbuf[:, 4096:4608], 512, ["xdump"])
            fgroups = [(0, 4), (4, 4), (8, 4), (12, 4), (16, 4), (20, 2)]
            W1o, W3o, W2o = 0, 4096, 8192
            for gi, (f0, nf) in enumerate(fgroups):
                par = gi % 2
                base = par * 12288
                w1 = bbuf[:, base + W1o:base + W1o + KC * 512].rearrange("p (k f) -> p k f", k=KC)
                w3 = bbuf[:, base + W3o:base + W3o + KC * 512].rearrange("p (k f) -> p k f", k=KC)
                w2 = bbuf[:, base + W2o:base + W2o + 4 * 1024].rearrange("p (f d) -> p f d", f=4)
                wk = "ffw%d" % par
                nfc = nf * 128
                S.dma("pool", lambda e, w1=w1, f0=f0, nfc=nfc: e.dma_start(
                    out=w1[:, :, 0:nfc], in_=w_ff1[l, i].rearrange("(kc p) f -> p kc f", p=128)[:, :, f0 * 128:f0 * 128 + nfc]),
                    [], [wk + "a"], stream="w")
                S.dma("pool", lambda e, w3=w3, f0=f0, nfc=nfc: e.dma_start(
                    out=w3[:, :, 0:nfc], in_=w_ff3[l, i].rearrange("(kc p) f -> p kc f", p=128)[:, :, f0 * 128:f0 * 128 + nfc]),
                    [], [wk + "b"], stream="w")
                S.dma("pool", lambda e, w2=w2, f0=f0, nf=nf: e.dma_start(
                    out=w2[:, 0:nf, :], in_=w_ff2[l, i][f0 * 128:(f0 + nf) * 128, :].rearrange("(fc p) d -> p fc d", p=128)),
                    [], [wk + "c"], stream="w")
                for ti, (t0, w) in enumerate(tiles):
                    j = 1 if t0 < CTX else 0
                    gk = "gt%d" % (ti % 2)
                    gt = bbuf[:, 28672 + (ti % 2) * 2048:28672 + (ti % 2) * 2048 + 2048].rearrange(
                        "p (f t) -> p f t", f=4)
                    for fc in range(nf):
                        pp = (fc % 2)
                        b1, b3 = banks[pp * 2], banks[pp * 2 + 1]
                        for kc in range(KC):
                            S.c("pe", lambda e, kc=kc, fc=fc, b1=b1, w1=w1, t0=t0, w=w: e.matmul(
                                b1[:, 0:w], lhsT=w1[:, kc, fc * 128:(fc + 1) * 128], rhs=hall[:, kc, t0:t0 + w],
                                start=(kc == 0), stop=(kc == KC - 1)), [wk + "a", "hall%d" % ti], ["bank%d" % (pp * 2)], pe_acc=True)
                        for kc in range(KC):
                            S.c("pe", lambda e, kc=kc, fc=fc, b3=b3, w3=w3, t0=t0, w=w: e.matmul(
                                b3[:, 0:w], lhsT=w3[:, kc, fc * 128:(fc + 1) * 128], rhs=hall[:, kc, t0:t0 + w],
                                start=(kc == 0), stop=(kc == KC - 1)), [wk + "b", "hall%d" % ti], ["bank%d" % (pp * 2 + 1)], pe_acc=True)
                        st = fbuf[:, 2048 + pp * 512:2048 + pp * 512 + w]
                        S.c("act", lambda e, b1=b1, st=st, w=w: e.activation(out=st, in_=b1[:, 0:w], func=AF.Silu),
                            ["bank%d" % (pp * 2)], ["silu%d" % pp])
                        S.c("dve", lambda e, b3=b3, st=st, w=w, gt=gt, fc=fc: e.tensor_tensor(
                            out=gt[:, fc, 0:w], in0=b3[:, 0:w], in1=st, op=ALU.mult),
                            ["bank%d" % (pp * 2 + 1), "silu%d" % pp], [gk])
                    if False:
                        dump(512, w1[:, 0, :], 512, [wk + "a"])
                        dump(1536, w2[:, 0, 0:512], 512, [wk + "c"])
                        dump(2048, gt[:, 0, :], 512, [gk])
                        dump(2560, w3[:, 7, :], 512, [wk + "b"])
                    for dc in range(KC):
                        pb = banks[4 + dc % 2]
                        for fc in range(nf):
                            S.c("pe", lambda e, dc=dc, fc=fc, pb=pb, w2=w2, gt=gt, w=w, nf=nf: e.matmul(
                                pb[:, 0:w], lhsT=w2[:, fc, dc * 128:(dc + 1) * 128], rhs=gt[:, fc, 0:w],
                                start=(fc == 0), stop=(fc == nf - 1)), [wk + "c", gk], ["bank%d" % (4 + dc % 2)], pe_acc=True)
                        S.c("dve", lambda e, dc=dc, pb=pb, t0=t0, w=w, j=j: e.scalar_tensor_tensor(
                            out=xs[:, dc, t0:t0 + w], in0=pb[:, 0:w], scalar=gsh[:, l, sub, 2, dc, j:j + 1],
                            in1=xs[:, dc, t0:t0 + w], op0=ALU.mult, op1=ALU.add),
                            ["bank%d" % (4 + dc % 2), ("xs", t0, dc), "gsh%d" % l], [("xs", t0, dc)])
            S.barrier()


        PF = {"q": 0, "k": 256, "r": 512, "a": 1024, "su": 1056, "fn": 1568, "cb": 2080, "cc": 2592, "cx": 3104}

        def evac(idx, out_ap, in_ap, reads, writes):
            if idx % 2 == 0:
                S.c("act", lambda e: e.activation(out=out_ap, in_=in_ap, func=AF.Copy), reads, writes)
            else:
                S.c("dve", lambda e: e.tensor_copy(out=out_ap, in_=in_ap), reads, writes)

        def win_proj(l, tiles):
            blocks = [(0, 512, "fm", 0), (512, 512, "tm", 0), (1024, 512, "fm", 512), (1536, 32, "fm", 1024),
                      (1568, 512, "fm", 1056), (2080, 512, "tm", 512), (2592, 512, "fm", 1568),
                      (3104, 512, "fm", 2080), (3616, 512, "fm", 2592), (4128, 512, "fm", 3104)]
            cnt = [0]
            for bi, (c0, ncol, kind, dst) in enumerate(blocks):
                par = bi % 2
                wv = bbuf[:, par * 12288: par * 12288 + 4096].rearrange("p (k n) -> p k n", k=KC)
                wk = "winw%d" % par
                S.dma("pool", lambda e, wv=wv, c0=c0, ncol=ncol: e.dma_start(
                    out=wv[:, :, 0:ncol], in_=w_in[l].rearrange("(kc p) n -> p kc n", p=128)[:, :, c0:c0 + ncol]),
                    [], [wk], stream="w")
                for ti, (t0, w) in enumerate(tiles):
                    if kind == "fm":
                        for ch in range((ncol + 127) // 128):
                            m = min(128, ncol - ch * 128)
                            i4 = cnt[0] % 4
                            cnt[0] += 1
                            bk = banks[i4]
                            for kc in range(KC):
                                S.c("pe", lambda e, bk=bk, m=m, w=w, wv=wv, kc=kc, ch=ch, t0=t0: e.matmul(
                                    bk[0:m, 0:w], lhsT=wv[:, kc, ch * 128:ch * 128 + m], rhs=hall[:, kc, t0:t0 + w],
                                    start=(kc == 0), stop=(kc == KC - 1)), [wk, "hall%d" % ti], ["bank%d" % i4], pe_acc=True)
                            stg = bbuf[:, 24576 + i4 * 512:24576 + i4 * 512 + w]
                            evac(cnt[0], stg[0:m, :], bk[0:m, 0:w], ["bank%d" % i4], ["wstg%d" % i4])
                            if c0 == 2592 and t0 >= CTX:
                                dst_ap = fx_in[ch // 2, (ch % 2) * 128:(ch % 2) * 128 + m, t0 - CTX:t0 - CTX + w]
                                dkey = ("fx_in", ch // 2)
                            else:
                                dst_ap = pfm[dst + ch * 128:dst + ch * 128 + m, t0:t0 + w]
                                dkey = ("pfm", dst + ch * 128, t0)
                            S.dma("sp", lambda e, dst_ap=dst_ap, stg=stg, m=m: e.dma_start(out=dst_ap, in_=stg[0:m, :]),
                                  ["wstg%d" % i4], [dkey])
                    else:
                        for sidx in range(w // 128):
                            i4 = cnt[0] % 4
                            cnt[0] += 1
                            bk = banks[i4]
                            ts = t0 + sidx * 128
                            for kc in range(KC):
                                S.c("pe", lambda e, bk=bk, wv=wv, kc=kc, ts=ts: e.matmul(
                                    bk[:, 0:512], lhsT=hall[:, kc, ts:ts + 128], rhs=wv[:, kc, 0:512],
                                    start=(kc == 0), stop=(kc == KC - 1)), [wk, "hall%d" % ti], ["bank%d" % i4], pe_acc=True)
                            stg = bbuf[:, 24576 + i4 * 512:24576 + i4 * 512 + 512]
                            evac(cnt[0], stg, bk[:, 0:512], ["bank%d" % i4], ["wstg%d" % i4])
                            S.dma("sp", lambda e, stg=stg, ts=ts, dst=dst: e.dma_start(
                                out=ptm[ts:ts + 128, dst:dst + 512], in_=stg), ["wstg%d" % i4], [("ptm", dst, ts)])
            S.barrier()

        def conv_branch(l, tiles):
            for ti, (t0, w) in enumerate(tiles):
                rl = 256 if t0 < CTX else 64
                for ch in range(4):
                    it = ti * 4 + ch
                    p2 = it % 2
                    inb = bbuf[:, p2 * 2048:p2 * 2048 + 3 * 512].rearrange("p (s t) -> p s t", s=3)
                    S.dma("sp", lambda e, inb=inb, ch=ch, t0=t0, w=w: e.dma_start(
                        out=inb[:, :, 0:w],
                        in_=pfm[2080:3616, :].rearrange("(s g p) t -> p s g t", s=3, g=4)[:, :, ch, t0:t0 + w]),
                        [("pfm", 2080 + s_ * 512 + ch * 128, t0) for s_ in range(3)], ["cvin%d" % p2])
                    z = fbuf[:, p2 * 1024:p2 * 1024 + w]
                    y = fbuf[:, p2 * 1024 + 512:p2 * 1024 + 512 + w]
                    z3 = z.rearrange("p (r c) -> p r c", c=rl)
                    y3 = y.rearrange("p (r c) -> p r c", c=rl)
                    S.c("dve", lambda e, z=z, inb=inb, w=w: e.tensor_tensor(out=z, in0=inb[:, 1, 0:w], in1=inb[:, 2, 0:w], op=ALU.mult),
                        ["cvin%d" % p2], ["cvz%d" % p2])
                    S.c("act", lambda e, y=y, z=z, ch=ch: e.activation(out=y, in_=z, func=AF.Identity, scale=wcv[:, l, 4 + ch:5 + ch]),
                        ["cvz%d" % p2, "wcv%d" % l], ["cvy%d" % p2])
                    S.c("dve", lambda e, y3=y3, z3=z3, ch=ch, rl=rl: e.scalar_tensor_tensor(
                        out=y3[:, :, 1:rl], in0=z3[:, :, 0:rl - 1], scalar=wcv[:, l, ch:ch + 1], in1=y3[:, :, 1:rl],
                        op0=ALU.mult, op1=ALU.add), ["cvz%d" % p2, "cvy%d" % p2, "wcv%d" % l], ["cvy%d" % p2])
                    S.c("dve", lambda e, y3=y3, z3=z3, ch=ch, rl=rl: e.scalar_tensor_tensor(
                        out=y3[:, :, 0:rl - 1], in0=z3[:, :, 1:rl], scalar=wcv[:, l, 8 + ch:9 + ch], in1=y3[:, :, 0:rl - 1],
                        op0=ALU.mult, op1=ALU.add), ["cvz%d" % p2, "cvy%d" % p2, "wcv%d" % l], ["cvy%d" % p2])
                    ob = bbuf[:, 4096 + p2 * 512:4096 + p2 * 512 + w]
                    S.c("dve", lambda e, ob=ob, inb=inb, y=y, w=w: e.tensor_tensor(out=ob, in0=inb[:, 0, 0:w], in1=y, op=ALU.mult),
                        ["cvin%d" % p2, "cvy%d" % p2], ["cvo%d" % p2])
                    S.dma("sp", lambda e, ob=ob, ch=ch, t0=t0, w=w: e.dma_start(
                        out=br[3, ch * 128:(ch + 1) * 128, t0:t0 + w], in_=ob), ["cvo%d" % p2], [("br", 3, ch, t0)])
            S.barrier()

        def sgu_branch(l, tiles):
            wst = bbuf[:, 8192:8192 + 512].rearrange("p (g i) -> p g i", g=4)
            S.dma("pool", lambda e: e.dma_start(out=wst, in_=w_sguT[l].rearrange("g j i -> j g i")), [], ["wst"], stream="w")
            bbc = fbuf[:, 4096:4608]
            S.dma("sp", lambda e: e.dma_start(out=bbc, in_=b_sgu[l].partition_broadcast(128)), [], ["bbc"])
            ci = 0
            for ti, (t0, w) in enumerate(tiles):
                for sidx in range(w // 128):
                    ts = t0 + sidx * 128
                    p2 = ci % 2
                    ci += 1
                    svt = bbuf[:, 9216 + p2 * 512:9216 + p2 * 512 + 512]
                    sut = bbuf[:, 10240 + p2 * 512:10240 + p2 * 512 + 512].rearrange("p (g t) -> p g t", g=4)
                    S.dma("sp", lambda e, svt=svt, ts=ts: e.dma_start(out=svt, in_=ptm[ts:ts + 128, 512:1024]),
                          [("ptm", 512, ts)], ["svt%d" % p2])
                    S.dma("sp", lambda e, sut=sut, ts=ts: e.dma_start(
                        out=sut, in_=pfm[1056:1568, ts:ts + 128].rearrange("(g c) t -> c g t", c=128)),
                        [("pfm", 1056 + g_ * 128, t0) for g_ in range(4)], ["sut%d" % p2])
                    st = xbuf[:, p2 * 32:p2 * 32 + 32]
                    sq = fbuf[:, p2 * 512:p2 * 512 + 512]
                    sv3 = svt.rearrange("p (g c) -> p g c", g=4)
                    S.c("dve", lambda e, st=st, sv3=sv3: e.tensor_reduce(out=st[:, 0:4], in_=sv3, axis=AX.X, op=ALU.add),
                        ["svt%d" % p2], ["sgst%d" % p2])
                    S.c("act", lambda e, sq=sq, svt=svt: e.activation(out=sq, in_=svt, func=AF.Square), ["svt%d" % p2], ["sgsq%d" % p2])
                    S.c("dve", lambda e, st=st, sq=sq: e.tensor_reduce(out=st[:, 4:8], in_=sq.rearrange("p (g c) -> p g c", g=4),
                                                                     axis=AX.X, op=ALU.add), ["sgsq%d" % p2, "sgst%d" % p2], ["sgst%d" % p2])
                    S.c("dve", lambda e, st=st: e.tensor_scalar(out=st[:, 8:12], in0=st[:, 0:4], scalar1=1.0 / 128, scalar2=None, op0=ALU.mult),
                        ["sgst%d" % p2], ["sgst%d" % p2])
                    S.c("dve", lambda e, st=st: e.tensor_tensor(out=st[:, 16:20], in0=st[:, 8:12], in1=st[:, 8:12], op=ALU.mult),
                        ["sgst%d" % p2], ["sgst%d" % p2])
                    S.c("dve", lambda e, st=st: e.scalar_tensor_tensor(out=st[:, 12:16], in0=st[:, 4:8], scalar=1.0 / 128, in1=st[:, 16:20],
                                                                       op0=ALU.mult, op1=ALU.subtract), ["sgst%d" % p2], ["sgst%d" % p2])
                    S.c("act", lambda e, st=st: e.activation(out=st[:, 12:16], in_=st[:, 12:16], func=AF.Sqrt, scale=1.0, bias=epsc[:]),
                        ["sgst%d" % p2, "epsc"], ["sgst%d" % p2])
                    S.c("dve", lambda e, st=st: e.reciprocal(out=st[:, 12:16], in_=st[:, 12:16]), ["sgst%d" % p2], ["sgst%d" % p2])
                    zt = bbuf[:, 11264 + p2 * 512:11264 + p2 * 512 + 512]
                    for g in range(4):
                        S.c("dve", lambda e, zt=zt, svt=svt, st=st, g=g: e.tensor_scalar(
                            out=zt[:, g * 128:(g + 1) * 128], in0=svt[:, g * 128:(g + 1) * 128], scalar1=st[:, 8 + g:9 + g],
                            scalar2=st[:, 12 + g:13 + g], op0=ALU.subtract, op1=ALU.mult), ["svt%d" % p2, "sgst%d" % p2], ["sgz%d" % p2])
                    bk = banks[p2]
                    for g in range(4):
                        S.c("pe", lambda e, bk=bk, zt=zt, g=g: e.matmul(
                            bk[:, g * 128:(g + 1) * 128], lhsT=zt[:, g * 128:(g + 1) * 128], rhs=wst[:, g, :], start=True, stop=True),
                            ["sgz%d" % p2, "wst"], ["bank%d" % p2])
                    tmp = fbuf[:, 1024 + p2 * 512:1024 + p2 * 512 + 512]
                    S.c("dve", lambda e, tmp=tmp, bk=bk: e.tensor_tensor(out=tmp, in0=bk[:, 0:512], in1=bbc, op=ALU.add),
                        ["bank%d" % p2, "bbc"], ["sgt%d" % p2])
                    ob = bbuf[:, 12288 + p2 * 512:12288 + p2 * 512 + 512].rearrange("p (g t) -> p g t", g=4)
                    S.c("pool", lambda e, ob=ob, tmp=tmp, sut=sut: e.tensor_tensor(
                        out=ob, in0=tmp.rearrange("p (g t) -> p g t", g=4), in1=sut, op=ALU.mult), ["sgt%d" % p2, "sut%d" % p2], ["sgo%d" % p2])
                    S.dma("sp", lambda e, ob=ob, ts=ts: e.dma_start(
                        out=br[1, :, ts:ts + 128].rearrange("(g c) t -> c g t", c=128), in_=ob), ["sgo%d" % p2],
                        [("br", 1, g_, t0) for g_ in range(4)] if sidx == w // 128 - 1 else [("brpart", ts)])
            S.barrier()

        def zero_branch(k, tiles):
            zt = bbuf[:, 0:512]
            S.c("dve", lambda e: e.memset(zt, 0.0), [], ["zt"])
            for ti, (t0, w) in enumerate(tiles):
                for ch in range(4):
                    S.dma("sp", lambda e, ch=ch, t0=t0, w=w: e.dma_start(out=br[k, ch * 128:(ch + 1) * 128, t0:t0 + w], in_=zt[:, 0:w]),
                          ["zt"], [("br", k, ch, t0)])
            S.barrier()

        def merge(l, tiles):
            norm_to_hall(l, 1, tiles, "mix")
            S.barrier()
            mall = bbuf[:, 0:18432].rearrange("p (k t) -> p k t", k=KC)
            for dc in range(KC):
                par = dc % 2
                wg = bbuf[:, 18432 + par * 6144:18432 + par * 6144 + 4096].rearrange("p (k c n) -> p k c n", k=4, c=KC)
                wb = bbuf[:, 18432 + par * 6144 + 4096:18432 + par * 6144 + 6144].rearrange("p (k c n) -> p k c n", k=4, c=4)
                for k in range(4):
                    S.dma("pool", lambda e, wg=wg, k=k, dc=dc: e.dma_start(
                        out=wg[:, k, :, :], in_=w_gate[l, k].rearrange("(kc p) n -> p kc n", p=128)[:, :, dc * 128:(dc + 1) * 128]),
                        [], ["wg%d_%d" % (par, k)], stream="w")
                    S.dma("pool", lambda e, wb=wb, k=k, dc=dc: e.dma_start(
                        out=wb[:, k, :, :], in_=w_branch[l, k].rearrange("(cc p) n -> p cc n", p=128)[:, :, dc * 128:(dc + 1) * 128]),
                        [], ["wb%d_%d" % (par, k)], stream="w")
                for ti, (t0, w) in enumerate(tiles):
                    it = dc * len(tiles) + ti
                    p2 = it % 2
                    brt = fbuf[:, 0:4096].bitcast(BF16).rearrange("p (k c t) -> p k c t", k=4, c=4)
                    for k in range(4):
                        S.dma("sp", lambda e, brt=brt, k=k, t0=t0, w=w: e.dma_start(
                            out=brt[:, k, :, 0:w], in_=br[k, :, t0:t0 + w].rearrange("(c p) t -> p c t", p=128)),
                            [("br", k, c_, t0) for c_ in range(4)], ["brt_%d" % k])
                    macc = fbuf[:, 4096 + p2 * 512:4096 + p2 * 512 + w]
                    for k in range(4):
                        pg, pb = banks[(k % 2) * 2], banks[(k % 2) * 2 + 1]
                        for kc in range(KC):
                            S.c("pe", lambda e, pg=pg, wg=wg, k=k, kc=kc, t0=t0, w=w: e.matmul(
                                pg[:, 0:w], lhsT=wg[:, k, kc, :], rhs=hall[:, kc, t0:t0 + w], start=(kc == 0), stop=(kc == KC - 1)),
                                ["wg%d_%d" % (par, k), "hall%d" % ti], ["bank%d" % ((k % 2) * 2)], pe_acc=True)
                        for cc in range(4):
                            S.c("pe", lambda e, pb=pb, wb=wb, k=k, cc=cc, brt=brt, w=w: e.matmul(
                                pb[:, 0:w], lhsT=wb[:, k, cc, :], rhs=brt[:, k, cc, 0:w], start=(cc == 0), stop=(cc == 3)),
                                ["wb%d_%d" % (par, k), "brt_%d" % k], ["bank%d" % ((k % 2) * 2 + 1)], pe_acc=True)
                        sg = fbuf[:, 5120 + (k % 2) * 512:5120 + (k % 2) * 512 + w]
                        S.c("act", lambda e, sg=sg, pg=pg, k=k, dc=dc, w=w: e.activation(
                            out=sg, in_=pg[:, 0:w], func=AF.Sigmoid, bias=bgt[:, l, k * 8 + dc:k * 8 + dc + 1], scale=1.0),
                            ["bank%d" % ((k % 2) * 2), "bgt%d" % l], ["msg%d" % (k % 2)])
                        if k == 0:
                            S.c("dve", lambda e, macc=macc, pb=pb, sg=sg, w=w: e.tensor_tensor(out=macc, in0=pb[:, 0:w], in1=sg, op=ALU.mult),
                                ["bank%d" % ((k % 2) * 2 + 1), "msg%d" % (k % 2)], ["macc%d" % p2])
                        else:
                            S.c("dve", lambda e, sg=sg, pb=pb, w=w: e.tensor_tensor(out=sg, in0=pb[:, 0:w], in1=sg, op=ALU.mult),
                                ["bank%d" % ((k % 2) * 2 + 1), "msg%d" % (k % 2)], ["msg%d" % (k % 2)])
                            if k < 3:
                                S.c("pool", lambda e, macc=macc, sg=sg: e.tensor_tensor(out=macc, in0=macc, in1=sg, op=ALU.add),
                                    ["macc%d" % p2, "msg%d" % (k % 2)], ["macc%d" % p2])
                            else:
                                S.c("pool", lambda e, macc=macc, sg=sg, dc=dc, t0=t0, w=w: e.tensor_tensor(
                                    out=mall[:, dc, t0:t0 + w], in0=macc, in1=sg, op=ALU.add),
                                    ["macc%d" % p2, "msg%d" % (k % 2)], [("mall", ti)])
            S.barrier()
            wo = bbuf[:, 18432:18432 + 8192].rearrange("p (k n) -> p k n", k=KC)
            S.dma("pool", lambda e: e.dma_start(out=wo, in_=w_out[l].rearrange("(kc p) n -> p kc n", p=128)), [], ["wo"], stream="w")
            for ti, (t0, w) in enumerate(tiles):
                j = 1 if t0 < CTX else 0
                for dc in range(KC):
                    pb = banks[4 + dc % 2]
                    for kc in range(KC):
                        S.c("pe", lambda e, pb=pb, kc=kc, dc=dc, t0=t0, w=w: e.matmul(
                            pb[:, 0:w], lhsT=wo[:, kc, dc * 128:(dc + 1) * 128], rhs=mall[:, kc, t0:t0 + w],
                            start=(kc == 0), stop=(kc == KC - 1)), ["wo", ("mall", ti)], ["bank%d" % (4 + dc % 2)], pe_acc=True)
                    S.c("dve", lambda e, dc=dc, pb=pb, t0=t0, w=w, j=j: e.scalar_tensor_tensor(
                        out=xs[:, dc, t0:t0 + w], in0=pb[:, 0:w], scalar=gsh[:, l, 1, 2, dc, j:j + 1],
                        in1=xs[:, dc, t0:t0 + w], op0=ALU.mult, op1=ALU.add),
                        ["bank%d" % (4 + dc % 2), ("xs", t0, dc), "gsh%d" % l], [("xs", t0, dc)])
            S.barrier()


        def ag(src2d, dst2d, rkeys, wkeys):
            S.cc(lambda e: e.collective_compute("AllGather", ALU.bypass, replica_groups=GROUPS4,
                                                ins=[src2d], outs=[dst2d]), rkeys, wkeys)

        def ag_fence(keys):
            S.cc(lambda e: e.collective_compute("AllGather", ALU.bypass, replica_groups=GROUPS4,
                                                ins=[fence_in[:, :]], outs=[fence_out[:, :]]), keys, keys)

        def fnet_exchange_in():
            for q in range(2):
                ag(fx_in[q], fx_out[q], [("fx_in", q)], [("fx_out", q)])

        def fnet_branch(l, with_ctx):
            xsel = arena[:, 0:4096].bitcast(BF16)
            zsb = arena[:, 4096:20480].bitcast(BF16)
            z3 = zsb.rearrange("p (n k) -> p n k", k=256)
            bsb = arena[:, 20480:28672].bitcast(BF16)
            b3 = bsb.rearrange("p (k m) -> p k m", m=128)
            ysb = xsel
            misc = arena[:, 28672:32768].bitcast(BF16)
            fc = misc[:, 0:256]
            fhi = misc[0:64, 256:512]
            f256 = misc[:, 512:1536].rearrange("p (b m) -> p b m", b=2)
            S.dma("sp", lambda e: e.dma_start(out=fc, in_=fc_in[:, :]), [], ["fc"])
            S.dma("sp", lambda e: e.dma_start(out=fhi, in_=fhi_in[:, :]), [], ["fhi"])
            S.dma("sp", lambda e: e.dma_start(out=f256, in_=f256_in.rearrange("b p m -> p b m")), [], ["f256"])
            ev = [0]
            for r in range(4):
                for sx in range(4):
                    it = r * 4 + sx
                    p2 = it % 2
                    xin = misc[:, 1536 + p2 * 2048:1536 + p2 * 2048 + 2048].rearrange("p (g t) -> p g t", g=4)
                    for q in range(2):
                        S.dma("pool", lambda e, xin=xin, q=q, r=r, sx=sx: e.dma_start(
                            out=xin[:, 2 * q:2 * q + 2, :],
                            in_=fx_out[q, r * 256:(r + 1) * 256, sx * 512:(sx + 1) * 512].rearrange("(h c) t -> c h t", c=128)),
                            [("fx_out", 0), ("fx_out", 1)], ["xin%d_%d" % (p2, q)])
                    bk = banks[p2]
                    for g in range(4):
                        S.c("pe", lambda e, bk=bk, xin=xin, g=g: e.matmul(
                            bk[:, 0:512], lhsT=selm[:, g, :], rhs=xin[:, g, :], start=(g == 0), stop=(g == 3)),
                            ["selm", "xin%d_%d" % (p2, g // 2)], ["bank%d" % p2], pe_acc=True)
                    n0 = 2048 * r + 512 * sx
                    ev[0] += 1
                    evac(ev[0], xsel[:, n0:n0 + 512], bk[:, 0:512], ["bank%d" % p2], ["xsel"])
            S.barrier()
            if False:
                dump(0, xsel[:, 0:512], 512, ["xsel"])
            for pr in range(64):
                p2 = pr % 2
                bk = banks[2 + p2]
                for hh in range(2):
                    nlo = 2 * pr + hh
                    S.c("pe", lambda e, bk=bk, hh=hh, nlo=nlo: e.matmul(
                        bk[0:64, hh * 256:(hh + 1) * 256], lhsT=xsel[:, nlo:8192:128], rhs=fc, start=True, stop=True),
                        ["xsel", "fc"], ["bank%d" % (2 + p2)])
                ev[0] += 1
                evac(ev[0], zsb[0:64, pr * 512:(pr + 1) * 512], bk[0:64, 0:512], ["bank%d" % (2 + p2)], ["zsb"])
            S.barrier()
            if False:
                dump(512, zsb[0:64, 0:512], 512, ["zsb"], 64)
            for grp in range(32):
                p2 = grp % 2
                bk = banks[4 + p2]
                for i4 in range(4):
                    k2c = grp * 4 + i4
                    S.c("pe", lambda e, bk=bk, i4=i4, k2c=k2c: e.matmul(
                        bk[:, i4 * 128:(i4 + 1) * 128], lhsT=z3[0:64, :, k2c], rhs=fhi[:, 0:128], start=True, stop=False),
                        ["zsb", "fhi"], ["bank%d" % (4 + p2)], pe_acc=True)
                    S.c("pe", lambda e, bk=bk, i4=i4, k2c=k2c: e.matmul(
                        bk[:, i4 * 128:(i4 + 1) * 128], lhsT=z3[0:64, :, 128 + k2c], rhs=fhi[:, 128:256], start=False, stop=True),
                        ["zsb", "fhi"], ["bank%d" % (4 + p2)], pe_acc=True)
                ev[0] += 1
                evac(ev[0], bsb[:, grp * 512:(grp + 1) * 512], bk[:, 0:512], ["bank%d" % (4 + p2)], ["bsb"])
            S.barrier()
            if False:
                dump(1024, bsb[:, 0:512], 512, ["bsb"])
            yv = ysb.rearrange("p (kl kh) -> p kl kh", kh=64)
            for grp in range(16):
                p2 = grp % 2
                mt = misc[:, 5632 + p2 * 1024:5632 + p2 * 1024 + 1024].rearrange("p (k m) -> p k m", k=4)
                S.dma("sp", lambda e, mt=mt, grp=grp: e.dma_start(
                    out=mt, in_=m_tab[grp * 4:(grp + 1) * 4].rearrange("k p m -> p k m")), [], ["mt%d" % p2])
                bk = banks[6 + p2]
                for i4 in range(4):
                    khi = grp * 4 + i4
                    S.c("pe", lambda e, bk=bk, i4=i4, khi=khi, mt=mt: e.matmul(
                        bk[:, i4 * 128:(i4 + 1) * 128], lhsT=b3[:, :, khi], rhs=mt[:, i4, 0:128], start=True, stop=False),
                        ["bsb", "mt%d" % p2], ["bank%d" % (6 + p2)], pe_acc=True)
                    S.c("pe", lambda e, bk=bk, i4=i4, khi=khi, mt=mt: e.matmul(
                        bk[:, i4 * 128:(i4 + 1) * 128], lhsT=b3[:, :, 64 + khi], rhs=mt[:, i4, 128:256], start=False, stop=True),
                        ["bsb", "mt%d" % p2], ["bank%d" % (6 + p2)], pe_acc=True)
                ev[0] += 1
                evac(ev[0], yv[:, :, grp * 4:(grp + 1) * 4], bk[:, 0:512].rearrange("p (i kl) -> p kl i", i=4),
                     ["bank%d" % (6 + p2)], ["ysb"])
            S.barrier()
            if False:
                dump(1536, ysb[:, 0:512], 512, ["ysb"])
            for q in range(2):
                S.dma("sp", lambda e, q=q: e.dma_start(out=fy_in[q], in_=ysb[:, q * 4096:(q + 1) * 4096]), ["ysb"], [("fy_in", q)])
                ag(fy_in[q], fy_out[q], [("fy_in", q)], [("fy_out", q)])
            if with_ctx:
                for g in range(4):
                    p2 = g % 2
                    xc = misc[:, 1536 + p2 * 256:1536 + p2 * 256 + 256]
                    S.dma("sp", lambda e, xc=xc, g=g: e.dma_start(out=xc, in_=pfm[1568 + g * 128:1568 + (g + 1) * 128, 0:256]),
                          [("pfm", 1568 + g * 128, 0)], ["xc%d" % p2])
                    bk = banks[p2]
                    zc = misc[:, 2560 + p2 * 512:2560 + p2 * 512 + 512].rearrange("p (b m) -> p b m", b=2)
                    for tb in range(2):
                        S.c("pe", lambda e, bk=bk, xc=xc, tb=tb: e.matmul(
                            bk[:, tb * 256:(tb + 1) * 256], lhsT=xc[:, tb * 128:(tb + 1) * 128], rhs=fc, start=True, stop=True),
                            ["xc%d" % p2, "fc"], ["bank%d" % p2])
                    ev[0] += 1
                    evac(ev[0], zc, bk[:, 0:512].rearrange("p (b m) -> p b m", b=2), ["bank%d" % p2], ["zc%d" % p2])
                    bk2 = banks[2 + p2]
                    for tb in range(2):
                        S.c("pe", lambda e, bk2=bk2, zc=zc, tb=tb: e.matmul(
                            bk2[:, 0:256], lhsT=zc[:, tb, 0:128], rhs=f256[:, tb, 0:256], start=(tb == 0), stop=False),
                            ["zc%d" % p2, "f256"], ["bank%d" % (2 + p2)], pe_acc=True)
                        S.c("pe", lambda e, bk2=bk2, zc=zc, tb=tb: e.matmul(
                            bk2[:, 0:256], lhsT=zc[:, tb, 128:256], rhs=f256[:, tb, 256:512], start=False, stop=(tb == 1)),
                            ["zc%d" % p2, "f256"], ["bank%d" % (2 + p2)], pe_acc=True)
                    yc = misc[:, 3584 + p2 * 256:3584 + p2 * 256 + 256]
                    ev[0] += 1
                    evac(ev[0], yc, bk2[:, 0:256], ["bank%d" % (2 + p2)], ["yc%d" % p2])
                    if False:
                        dump(2560, yc, 256, ["yc%d" % p2])
                    S.dma("sp", lambda e, yc=yc, g=g: e.dma_start(out=br[2, g * 128:(g + 1) * 128, 0:256], in_=yc),
                          ["yc%d" % p2], [("br", 2, g, 0)])
            for sx in range(4):
                for g in range(4):
                    it = sx * 4 + g
                    p2 = it % 2
                    yin = misc[:, 4096 + p2 * 2048:4096 + p2 * 2048 + 2048] if False else \
                        arena[:, 4096 + p2 * 1024:4096 + p2 * 1024 + 1024].bitcast(BF16)
                    yin4 = yin.rearrange("p (q j t) -> p q j t", q=2, j=2)
                    for q in range(2):
                        S.dma("pool", lambda e, yin4=yin4, g=g, sx=sx, q=q: e.dma_start(
                            out=yin4[:, q, :, :],
                            in_=fy_out[q, g * 128:(g + 1) * 128, :].rearrange("c (j t) -> c j t", j=2)[:, :, sx * 512:(sx + 1) * 512]),
                            [("fy_out", 0), ("fy_out", 1)], ["yin%d_%d" % (p2, q)])
                    bk = banks[4 + p2]
                    for jj in range(4):
                        S.c("pe", lambda e, bk=bk, yin4=yin4, jj=jj: e.matmul(
                            bk[:, 0:512], lhsT=selm[:, jj, :], rhs=yin4[:, jj // 2, jj % 2, :], start=(jj == 0), stop=(jj == 3)),
                            ["selm", "yin%d_%d" % (p2, jj // 2)], ["bank%d" % (4 + p2)], pe_acc=True)
                    yo = arena[:, 8192 + p2 * 256:8192 + p2 * 256 + 256].bitcast(BF16)
                    ev[0] += 1
                    evac(ev[0], yo, bk[:, 0:512], ["bank%d" % (4 + p2)], ["yo%d" % p2])
                    t0 = CTX + sx * 512
                    if False:
                        dump(2048, yo, 512, ["yo%d" % p2])
                    S.dma("sp", lambda e, yo=yo, g=g, t0=t0: e.dma_start(out=br[2, g * 128:(g + 1) * 128, t0:t0 + 512], in_=yo),
                          ["yo%d" % p2], [("br", 2, g, t0)])
            S.barrier()


        def gla_branch(l, mtiles):
            A = arena
            aT = A[:, 0:1152].bitcast(BF16)
            spb = A[:, 1152:3456]
            U = A[:, 3456:5760]
            S32 = A[:, 5760:8192].rearrange("p (c m) -> p c m", m=128)
            qT2 = A[:, 8192:9344].bitcast(BF16)
            kT2 = A[:, 9344:10496].bitcast(BF16)
            qp = A[:, 10496:11648].bitcast(BF16)
            kpp = A[:, 11648:12800].bitcast(BF16)
            kppT = A[:, 12800:13952].bitcast(BF16).rearrange("p (c m) -> p c m", m=128)
            Sbf = A[:, 13952:15168].bitcast(BF16).rearrange("p (c m) -> p c m", m=128)
            vh = A[:, 15168:16320].bitcast(BF16).rearrange("p (c m) -> p c m", m=128)
            rT = A[:, 16320:17472].bitcast(BF16)
            ebt = A[:, 17472:17984]
            enbt = A[:, 17984:18496]
            ATs = [A[:, 18496 + i * 128:18496 + (i + 1) * 128].bitcast(BF16) for i in range(2)]
            wa2h = A[:, 18752:18816].bitcast(BF16)
            kppF = A[:, 27104:28256].bitcast(BF16)
            kppB = A[:, 28256:29408].bitcast(BF16)
            ATall = A[:, 29408:31712].bitcast(BF16).rearrange("p (c m) -> p c m", m=256)
            dtp = A[:, 18944:18962]
            dt2 = A[:, 18962:18980]
            dex = A[:, 18980:18998]
            dsum = A[:, 18998:18999]
            osb = A[:, 19008:19520]
            rsb = A[:, 19520:20032]
            srb = A[:, 20032:20544]
            sqb = A[:, 20544:20800].bitcast(BF16)
            ogb = A[:, 20800:21056].bitcast(BF16)
            gx = A[:, 21056:21312]
            gth = A[:, 21572:22596].rearrange("p (r m) -> p r m", m=256)
            Tc = A[:, 23636:23764]
            Lctx = A[:, 23764:23892]
            dpr = A[:, 23892:23896]
            onesf = A[:, 24000:24128]
            cs = A[:, 24800:27104]
            bfb = banks[7][:, 0:256].bitcast(BF16)
            S.c("dve", lambda e: e.memset(onesf, 1.0), [], ["onesf"])
            S.c("dve", lambda e: e.memset(A[:, 0:1152], 0.0), [], ["aT"])
            S.c("dve", lambda e: e.memset(A[:, 18752:18816], 0.0), [], ["wa2h"])
            S.c("dve", lambda e: e.memset(A[:, 27104:29408], 0.0), [], ["kppF", "kppB"])
            S.c("dve", lambda e: e.memset(gx, 0.0), [], ["chain0", "chain64", "gxD"])
            S.barrier()
            S.dma("pool", lambda e: e.dma_start(out=aT[0:32, :], in_=pfm[1024:1056, :]), [], ["aT"])
            for h in range(4):
                for half in range(2):
                    S.dma("pool", lambda e, half=half, h=h: e.dma_start(
                        out=qT2[half * 64:(half + 1) * 64, :], in_=pfm[64 * h:64 * h + 64, :]), [], ["qT2_%d" % half])
                    S.dma("pool", lambda e, half=half, h=h: e.dma_start(
                        out=kT2[half * 64:(half + 1) * 64, :], in_=pfm[256 + 64 * h:256 + 64 * h + 64, :]), [], ["kT2_%d" % half])
                S.dma("pool", lambda e, h=h: e.dma_start(
                    out=vh, in_=ptm[:, 128 * h:128 * (h + 1)].rearrange("(c p) d -> p c d", p=128)), [], ["vh"])
                S.dma("pool", lambda e, h=h: e.dma_start(out=rT, in_=pfm[512 + 128 * h:512 + 128 * (h + 1), :]), [], ["rT"])
                S.dma("pool", lambda e, h=h: e.dma_start(out=wa2h[0:32, :], in_=wa2_in[l, h]), [], ["wa2h"], stream="w")
                for ti, (t0, w) in enumerate(TILES):
                    bk = banks[ti % 2]
                    S.c("pe", lambda e, bk=bk, t0=t0, w=w: e.matmul(bk[:, 0:w], lhsT=wa2h, rhs=aT[:, t0:t0 + w], start=True, stop=True),
                        ["wa2h", "aT"], ["bank%d" % (ti % 2)])
                    S.c("act", lambda e, bk=bk, t0=t0, w=w, h=h: e.activation(
                        out=spb[:, t0:t0 + w], in_=bk[:, 0:w], func=AF.Exp, scale=-1.0, bias=nba[:, l, h:h + 1]),
                        ["bank%d" % (ti % 2), "nba%d" % l], ["spb"])
                S.c("act", lambda e: e.activation(out=spb, in_=spb, func=AF.Ln, scale=1.0, bias=onec[:]), ["spb", "onec"], ["spb"])
                for c in range(18):
                    S.c("dve", lambda e, c=c: e.tensor_tensor_scan(
                        out=cs[:, c * 128:(c + 1) * 128], data0=onesf, data1=spb[:, c * 128:(c + 1) * 128], initial=0.0,
                        op0=ALU.mult, op1=ALU.add), ["spb", "onesf"], ["cs"])
                S.c("dve", lambda e: e.tensor_copy(out=dtp, in_=cs[:, 127:2304:128]), ["cs"], ["dtp"])
                for c in range(18):
                    S.c("dve", lambda e, c=c: e.scalar_tensor_tensor(
                        out=cs[64:128, c * 128:(c + 1) * 128], in0=cs[64:128, c * 128:(c + 1) * 128], scalar=dtp[64:128, c:c + 1],
                        in1=spb[64:128, c * 128:(c + 1) * 128], op0=ALU.subtract, op1=ALU.subtract), ["cs", "dtp", "spb"], ["cs"])
                S.c("dve", lambda e: e.tensor_copy(out=dt2[0:64, :], in_=cs[0:64, 127:2304:128]), ["cs"], ["dt2"])
                S.c("dve", lambda e: e.tensor_copy(out=dt2[64:128, :], in_=cs[64:128, 0:2304:128]), ["cs", "dt2"], ["dt2"])
                S.c("act", lambda e: e.activation(out=dex, in_=dt2, func=AF.Exp, scale=sca[:, 0:1]), ["dt2", "sca"], ["dex"])
                S.c("dve", lambda e: e.tensor_reduce(out=dsum, in_=dt2[:, 2:18], axis=AX.X, op=ALU.add), ["dt2"], ["dsum"])
                S.c("act", lambda e: e.activation(out=gx[:, 128:129], in_=dsum, func=AF.Exp, scale=sca[:, 0:1]), ["dsum", "sca"], ["gxD"])
                for ti, (t0, w) in enumerate(TILES):
                    S.c("act", lambda e, t0=t0, w=w: e.activation(out=ebt[:, 0:w], in_=cs[:, t0:t0 + w], func=AF.Exp, scale=sca[:, 0:1]),
                        ["cs", "sca"], ["ebt"])
                    S.c("act", lambda e, t0=t0, w=w: e.activation(out=enbt[:, 0:w], in_=cs[:, t0:t0 + w], func=AF.Exp, scale=sca[:, 1:2]),
                        ["cs", "sca"], ["enbt"])
                    S.c("dve", lambda e, t0=t0, w=w: e.scalar_tensor_tensor(
                        out=qp[:, t0:t0 + w], in0=qT2[:, t0:t0 + w], scalar=0.125, in1=ebt[:, 0:w], op0=ALU.mult, op1=ALU.mult),
                        ["qT2_0", "qT2_1", "ebt"], ["qp"])
                    S.c("dve", lambda e, t0=t0, w=w: e.tensor_tensor(
                        out=kpp[:, t0:t0 + w], in0=kT2[:, t0:t0 + w], in1=enbt[:, 0:w], op=ALU.mult),
                        ["kT2_0", "kT2_1", "enbt"], ["kpp"])
                    S.c("dve", lambda e, t0=t0, w=w: e.tensor_copy(out=kppF[0:64, t0:t0 + w], in_=kpp[0:64, t0:t0 + w]), ["kpp"], ["kppF"])
                    S.c("dve", lambda e, t0=t0, w=w: e.tensor_copy(out=kppB[64:128, t0:t0 + w], in_=kpp[64:128, t0:t0 + w]), ["kpp"], ["kppB"])
                for c in range(18):
                    S.c("pe", lambda e, c=c: e.transpose(out=bfb[:, 0:128], in_=kpp[:, c * 128:(c + 1) * 128], identity=ident_bf[:]),
                        ["kpp", "ident_bf"], ["bank7"])
                    S.c("dve", lambda e, c=c: e.tensor_copy(out=kppT[:, c, :], in_=bfb[:, 0:128]), ["bank7"], ["kppT%d" % (c % 2)])
                    bk = banks[2 + c % 2]
                    S.c("pe", lambda e, c=c, bk=bk: e.matmul(bk[:, 0:128], lhsT=kppT[:, c, :], rhs=vh[:, c, :], start=True, stop=True),
                        ["kppT%d" % (c % 2), "vh"], ["bank%d" % (2 + c % 2)])
                    S.c("act", lambda e, c=c, bk=bk: e.activation(out=U[:, c * 128:(c + 1) * 128], in_=bk[:, 0:128], func=AF.Identity,
                                                                 scale=dex[:, c:c + 1]), ["bank%d" % (2 + c % 2), "dex"], ["U"])

                def chain(rows, out_ap, in_ap, c):
                    r0, r1 = rows
                    S.c("dve", lambda e: e.scalar_tensor_tensor(
                        out=out_ap[r0:r1, :], in0=in_ap[r0:r1, :], scalar=dex[r0:r1, c:c + 1], in1=U[r0:r1, c * 128:(c + 1) * 128],
                        op0=ALU.mult, op1=ALU.add), ["U", "dex", "chain%d" % r0], ["chain%d" % r0])
                FW, BW = (0, 64), (64, 128)
                chain(FW, Lctx, U[:, 0:128], 1)
                chain(BW, Lctx, U[:, 128:256], 0)
                chain(FW, gx[:, 0:128], U[:, 256:384], 3)
                for c in range(4, 18):
                    chain(FW, gx[:, 0:128], gx[:, 0:128], c)
                chain(BW, gx[:, 0:128], U[:, 17 * 128:18 * 128], 16)
                for c in range(15, 1, -1):
                    chain(BW, gx[:, 0:128], gx[:, 0:128], c)
                S.dma("pool", lambda e, h=h: e.dma_start(out=gx_d[h], in_=gx), ["chain0", "chain64", "gxD"], [("gx_d", h)])
                ag(gx_d[h], gxo_d[h], [("gx_d", h)], [("gxo_d", h)])
                c_lo = mtiles[0][0] // 128
                for c in range(c_lo, 18):
                    ci = c % 2
                    ba = banks[ci]
                    cols = slice(c * 128, (c + 1) * 128)
                    S.c("pe", lambda e, ba=ba, cols=cols: e.matmul(ba[:, 0:128], lhsT=kppF[:, cols], rhs=qp[:, cols], start=True, stop=True),
                        ["kppF", "qp"], ["bank%d" % ci])
                    S.c("pe", lambda e, ba=ba, cols=cols: e.matmul(ba[:, 128:256], lhsT=kppB[:, cols], rhs=qp[:, cols], start=True, stop=True),
                        ["kppB", "qp"], ["bank%d" % ci])
                    S.c("dve", lambda e, ba=ba, c=c: e.tensor_tensor(out=ATall[:, c, :], in0=ba[:, 0:256], in1=gmask[:], op=ALU.mult),
                        ["bank%d" % ci, "gmask"], ["atall"])
                S.dma("pool", lambda e, h=h: e.dma_start(out=gth, in_=gxo_d[h].rearrange("(r p) m -> p r m", p=128)),
                      [("gxo_d", h)], ["gth"])
                S.c("dve", lambda e: e.scalar_tensor_tensor(out=dpr, in0=gth[:, :, 128], scalar=-1.0, in1=flg[:, 0:4],
                                                            op0=ALU.add, op1=ALU.mult), ["gth", "flg"], ["dpr"])
                S.c("dve", lambda e: e.tensor_scalar(out=dpr, in0=dpr, scalar1=1.0, scalar2=None, op0=ALU.add), ["dpr"], ["dpr"])
                for i in range(4):
                    S.c("dve", lambda e, i=i: e.tensor_scalar(out=gth[:, i, 0:128], in0=gth[:, i, 0:128], scalar1=flg[:, i:i + 1],
                                                             scalar2=None, op0=ALU.mult), ["gth", "flg"], ["gth"])
                for (r0, r1), order in ((FW, (0, 1, 2, 3)), (BW, (3, 2, 1, 0))):
                    src = Lctx
                    for i in order:
                        S.c("dve", lambda e, r0=r0, r1=r1, i=i, src=src: e.scalar_tensor_tensor(
                            out=Tc[r0:r1, :], in0=src[r0:r1, :], scalar=dpr[r0:r1, i:i + 1], in1=gth[r0:r1, i, 0:128],
                            op0=ALU.mult, op1=ALU.add), ["gth", "dpr", "chain%d" % r0, "tc%d" % r0], ["tc%d" % r0])
                        src = Tc
                S.c("dve", lambda e: e.memset(S32[0:64, 0, :], 0.0), [], ["s32_0"])
                S.c("dve", lambda e: e.memset(S32[64:128, 1, :], 0.0), [], ["s32_64"])
                S.c("dve", lambda e: e.tensor_copy(out=S32[0:64, 1, :], in_=U[0:64, 0:128]), ["U"], ["s32_0"])
                S.c("dve", lambda e: e.tensor_copy(out=S32[64:128, 0, :], in_=U[64:128, 128:256]), ["U"], ["s32_64"])
                S.c("dve", lambda e: e.tensor_copy(out=S32[0:64, 2, :], in_=Tc[0:64, :]), ["tc0"], ["s32_0"])
                S.c("dve", lambda e: e.tensor_copy(out=S32[64:128, 17, :], in_=Tc[64:128, :]), ["tc64"], ["s32_64"])
                for c in range(2, 17):
                    S.c("dve", lambda e, c=c: e.scalar_tensor_tensor(
                        out=S32[0:64, c + 1, :], in0=S32[0:64, c, :], scalar=dex[0:64, c:c + 1], in1=U[0:64, c * 128:(c + 1) * 128],
                        op0=ALU.mult, op1=ALU.add), ["U", "dex", "s32_0"], ["s32_0"])
                for c in range(17, 2, -1):
                    S.c("dve", lambda e, c=c: e.scalar_tensor_tensor(
                        out=S32[64:128, c - 1, :], in0=S32[64:128, c, :], scalar=dex[64:128, c:c + 1], in1=U[64:128, c * 128:(c + 1) * 128],
                        op0=ALU.mult, op1=ALU.add), ["U", "dex", "s32_64"], ["s32_64"])
                S.c("dve", lambda e: e.tensor_copy(out=Sbf[:, 0:18, :], in_=S32[:, 0:18, :]), ["s32_0", "s32_64"], ["Sbf"])
                for ti, (t0, w) in enumerate(mtiles):
                    bo = banks[4 + ti % 2]
                    for cc in range(w // 128):
                        c = t0 // 128 + cc
                        cols = slice(c * 128, (c + 1) * 128)
                        at = ATall[:, c, :]
                        oc = bo[:, cc * 128:(cc + 1) * 128]
                        S.c("pe", lambda e, oc=oc, c=c, cols=cols: e.matmul(oc, lhsT=Sbf[:, c, :], rhs=qp[:, cols], start=True, stop=False),
                            ["Sbf", "qp"], ["bank%d" % (4 + ti % 2)], pe_acc=True)
                        S.c("pe", lambda e, oc=oc, c=c, at=at: e.matmul(oc, lhsT=vh[:, c, :], rhs=at[:, 0:128], start=False, stop=False),
                            ["vh", "atall"], ["bank%d" % (4 + ti % 2)], pe_acc=True)
                        S.c("pe", lambda e, oc=oc, c=c, at=at: e.matmul(oc, lhsT=vh[:, c, :], rhs=at[:, 128:256], start=False, stop=True),
                            ["vh", "atall"], ["bank%d" % (4 + ti % 2)], pe_acc=True)
                    S.c("act", lambda e, bo=bo, w=w: e.activation(out=sqb[:, 0:w], in_=bo[:, 0:w], func=AF.Square),
                        ["bank%d" % (4 + ti % 2)], ["gsq"])
                    S.c("pe", lambda e, w=w: e.matmul(banks[6][:, 0:w], lhsT=ones_bf[:], rhs=sqb[:, 0:w], start=True, stop=True),
                        ["gsq", "ones"], ["bank6"])
                    S.c("act", lambda e, w=w: e.activation(out=rsb[:, 0:w], in_=banks[6][:, 0:w], func=AF.Sqrt, scale=1.0 / 128, bias=epsc[:]),
                        ["bank6", "epsc"], ["grs"])
                    S.c("dve", lambda e, w=w: e.reciprocal(out=rsb[:, 0:w], in_=rsb[:, 0:w]), ["grs"], ["grs"])
                    S.c("dve", lambda e, bo=bo, w=w: e.tensor_tensor(out=osb[:, 0:w], in0=bo[:, 0:w], in1=rsb[:, 0:w], op=ALU.mult),
                        ["bank%d" % (4 + ti % 2), "grs"], ["gos"])
                    S.c("act", lambda e, t0=t0, w=w: e.activation(out=srb[:, 0:w], in_=rT[:, t0:t0 + w], func=AF.Silu), ["rT"], ["gsr"])
                    S.c("dve", lambda e, w=w, h=h: e.scalar_tensor_tensor(
                        out=ogb[:, 0:w], in0=osb[:, 0:w], scalar=ggo[:, l, h:h + 1], in1=srb[:, 0:w], op0=ALU.mult, op1=ALU.mult),
                        ["gos", "gsr", "ggo%d" % l], ["gog"])
                    S.dma("sp", lambda e, h=h, t0=t0, w=w: e.dma_start(out=br[0, 128 * h:128 * (h + 1), t0:t0 + w], in_=ogb[:, 0:w]),
                          ["gog"], [("br", 0, h, t0)])
                S.barrier()

        def final_out():
            for ti, (t0, w) in enumerate(TILES if debug else TILES[1:]):
                sq = bbuf[:, 24576:24576 + KC * 512].rearrange("p (k t) -> p k t", k=KC)[:, :, 0:w]
                S.c("act", lambda e, t0=t0, w=w, sq=sq: e.activation(out=sq, in_=xs[:, :, t0:t0 + w], func=AF.Square),
                    [("xs", t0, k_) for k_ in range(KC)], ["sq"])
                for kc in range(KC):
                    S.c("pe", lambda e, kc=kc, w=w, sq=sq: e.matmul(
                        banks[7][:, 0:w], lhsT=ones_bf[:], rhs=sq[:, kc, :], start=(kc == 0), stop=(kc == KC - 1)),
                        ["sq", "ones"], ["bank7"], pe_acc=True)
                rs = fbuf[:, 0:w]
                S.c("act", lambda e, w=w, rs=rs: e.activation(out=rs, in_=banks[7][:, 0:w], func=AF.Sqrt,
                                                             scale=1.0 / D, bias=epsc[:]), ["bank7", "epsc"], ["rs"])
                S.c("dve", lambda e, rs=rs: e.reciprocal(out=rs, in_=rs), ["rs"], ["rs"])
                ot = fbuf[:, 2048:2048 + KC * 512].rearrange("p (k t) -> p k t", k=KC)
                for kc in range(KC):
                    S.c("dve", lambda e, kc=kc, t0=t0, w=w, rs=rs, ot=ot: e.scalar_tensor_tensor(
                        out=ot[:, kc, 0:w], in0=xs[:, kc, t0:t0 + w], scalar=gfin[:, kc:kc + 1], in1=rs,
                        op0=ALU.mult, op1=ALU.mult), [("xs", t0, kc), "rs", "gfin"], ["otile"])
                if t0 < CTX:
                    S.dma("sp", lambda e, w=w, ot=ot: e.dma_start(
                        out=out_ctx.rearrange("(kc p) t -> p kc t", p=128), in_=ot[:, :, 0:w]), ["otile"], ["outc"])
                else:
                    S.dma("sp", lambda e, t0=t0, w=w, ot=ot: e.dma_start(
                        out=out.rearrange("(kc p) t -> p kc t", p=128)[:, :, t0 - CTX:t0 - CTX + w], in_=ot[:, :, 0:w]),
                        ["otile"], ["out"])
            S.barrier()

        def forward():
            for l in range(DEPTH):
                last = (l == DEPTH - 1)
                ffn(l, 0, TILES)
                if stop_after == ("ffn1", l):
                    return
                mtiles = TILES[1:] if last else TILES
                norm_to_hall(l, 1, TILES, "mix")
                win_proj(l, TILES)
                if "nogla" in dbg_opts:
                    zero_branch(0, mtiles)
                else:
                    gla_branch(l, mtiles)
                fnet_exchange_in()
                if "nofnet" in dbg_opts:
                    zero_branch(2, mtiles)
                sgu_branch(l, mtiles)
                conv_branch(l, mtiles)
                if "nofnet" not in dbg_opts:
                    fnet_branch(l, not last)
                if l == 0:
                    for k_, r0_ in enumerate((1568, 2080, 2592, 3104)):
                        dump(k_ * 256, pfm[r0_:r0_ + 128, 0:256], 256, [])
                        dump(1024 + k_ * 256, pfm[r0_:r0_ + 128, 256:512], 256, [])
                    S.barrier()
                merge(l, mtiles)
                if stop_after == ("mix", l):
                    return
                ffn(l, 1, TILES[1:] if last else TILES)
                if stop_after == ("ffn2", l):
                    return

        forward()
        final_out()

        sems = {e: [es.enter_context(nc.semaphore("sem_%s_%d" % (e, k_))) for k_ in range(S.epoch[e] + 1)]
                for e in Sched.ENGS}
        with nc.Block() as block:
            S.emit(block, sems, dma_sems, cc_sem)
        if debug:
            print("sem epochs", S.epoch, "max count", max(S.max_counts.values()))
    return nc


_NC_CACHE = {}


def make_in_maps(inputs):
    f = lambda a: np.ascontiguousarray(np.asarray(a, dtype=np.float32))
    x, c, ctx, c_ctx = f(inputs["x"]), f(inputs["c"]), f(inputs["ctx"]), f(inputs["c_ctx"])
    shared = {
        "ident": np.eye(128, dtype=np.float32),
        "g_norm": f(inputs["g_norm"]).reshape(DEPTH, 24, 128),
        "w_ff1": f(inputs["w_ff1"]), "w_ff3": f(inputs["w_ff3"]), "w_ff2": f(inputs["w_ff2"]),
        "g_final": f(inputs["g_final"]).reshape(8, 128),
        "w_in": f(inputs["w_in"]),
        "w_sguT": np.ascontiguousarray(f(inputs["w_sgu"]).transpose(0, 1, 3, 2)),
        "b_sgu": f(inputs["b_sgu"]).reshape(DEPTH, 512),
        "w_conv": f(inputs["w_conv"]).reshape(DEPTH, 12, 128),
        "w_branch": f(inputs["w_branch"]), "w_gate": f(inputs["w_gate"]),
        "b_gate": f(inputs["b_gate"]).reshape(DEPTH, 32, 128),
        "w_out": f(inputs["w_out"]),
    }
    bf = ml_dtypes.bfloat16
    cidx = np.arange(128, dtype=np.float64)
    ang = 2 * np.pi * np.outer(cidx, cidx) / 128
    shared["fc_tab"] = np.concatenate([np.cos(ang), -np.sin(ang)], 1).astype(bf)
    a64 = 2 * np.pi * np.outer(np.arange(64.0), np.arange(64.0)) / 64
    shared["fhi_tab"] = np.concatenate([np.cos(a64), -np.sin(a64), np.sin(a64), np.cos(a64)], 1).astype(bf)
    khi = np.arange(64.0)[:, None, None]
    nlo = np.arange(128.0)[None, :, None]
    klo = np.arange(128.0)[None, None, :]
    am = 2 * np.pi * nlo * (khi + 64 * klo) / 8192
    shared["m_tab"] = (np.concatenate([np.cos(am), np.sin(am)], 2) / 1024.0).astype(bf)
    a256 = 2 * np.pi * np.outer(np.arange(256.0), np.arange(256.0)) / 256
    sc = 1.0 / np.sqrt(256.0 * 128.0)
    shared["f256_tab"] = (np.concatenate([np.cos(a256), np.sin(a256)], 1) * sc).reshape(2, 128, 512).astype(bf)
    wa = f(inputs["w_gla_a2"]); ba = f(inputs["b_gla_a2"])
    wa2 = np.zeros((DEPTH, 4, 32, 128), np.float32)
    ba2 = np.zeros((DEPTH, 4, 128), np.float32)
    for h_ in range(4):
        wa2[:, h_, 0:16, 0:64] = wa[:, 0, :, 64 * h_:64 * h_ + 64]
        wa2[:, h_, 16:32, 64:128] = wa[:, 1, :, 64 * h_:64 * h_ + 64]
        ba2[:, h_, 0:64] = ba[:, 0, 64 * h_:64 * h_ + 64]
        ba2[:, h_, 64:128] = ba[:, 1, 64 * h_:64 * h_ + 64]
    shared["wa2"] = wa2
    shared["ba2"] = ba2
    shared["ggo"] = f(inputs["g_gla_norm"])
    jj, ii = np.meshgrid(np.arange(128), np.arange(128), indexing="ij")
    shared["gmask"] = np.concatenate([(jj <= ii), (jj >= ii)], 1).astype(np.float32).astype(bf)
    sc_ = np.zeros((128, 2), np.float32)
    sc_[0:64, 0] = -1.0 / 16; sc_[64:128, 0] = 1.0 / 16
    sc_[:, 1] = -sc_[:, 0]
    shared["sca"] = sc_
    w_ada_full = f(inputs["w_ada"]); b_ada_full = f(inputs["b_ada"])
    maps = []
    for r in range(8):
        b, j = r // 4, r % 4
        xf = np.concatenate([ctx[b], x[b, j * TL:(j + 1) * TL]], axis=0).T
        m = dict(shared)
        m["x_fm"] = np.ascontiguousarray(xf)
        sel = np.zeros((4, 128, 128), np.float32)
        sel[j] = np.eye(128, dtype=np.float32)
        m["selm"] = sel.astype(ml_dtypes.bfloat16)
        fl = np.zeros((128, 8), np.float32)
        for i_ in range(4):
            fl[0:64, i_] = 1.0 if i_ < j else 0.0
            fl[64:128, i_] = 1.0 if i_ > j else 0.0
        m["flags"] = fl
        m["cvec3"] = np.ascontiguousarray(np.concatenate([c[0].reshape(8, 128), c[1].reshape(8, 128), c_ctx.reshape(8, 128)], 0))
        m["wada_sh"] = np.ascontiguousarray(w_ada_full[:, :, 1152 * r:1152 * (r + 1)])
        m["bada_sh"] = np.ascontiguousarray(b_ada_full[:, 1152 * r:1152 * (r + 1)]).reshape(DEPTH, 9, 128)
        bs = np.zeros((128, 2), np.float32)
        bs[:, b] = 1.0
        m["bsel"] = bs
        maps.append(m)
    return maps


def kernel(**inputs):
    if "nc" not in _NC_CACHE:
        _NC_CACHE["nc"] = build_program()
    nc = _NC_CACHE["nc"]
    maps = make_in_maps(inputs)
    res = run_bass_kernel_spmd(nc, maps, core_ids=list(range(8)))
    outp = np.empty((2, SEQ, D), dtype=np.float32)
    for r in range(8):
        b, j = r // 4, r % 4
        outp[b, j * TL:(j + 1) * TL, :] = res.results[r]["out_fm"].T
    return outp
```
